# Optimizing a Trainium2 kernel written in Bass

```python
import jax, jax.numpy as jnp
from jax import lax
import numpy as np

D_MODEL = 1024
BATCH = 8
SEQ = 4096
DEPTH = 4

CTX_LEN = 256
GRID_W = 64
N_MIXERS = 2
N_A_LAYERS = (DEPTH + 1) // 2
N_B_LAYERS = DEPTH // 2
HEAD_DIM = 64
A_Q_HEADS = D_MODEL // HEAD_DIM
A_KV_HEADS = 4
A_QW = A_Q_HEADS * HEAD_DIM
A_KW = A_KV_HEADS * HEAD_DIM
WINDOW = 128
BLOCK = 128
B_HEADS = 16
B_NOPE = 64
B_ROPE = 32
B_V = 64
Q_LORA = 512
KV_LORA = 256
D_FF = 2816
CONV_W = 3
ROPE_BASE = 10000.0
EPS = 1e-6
NEG = -1e30

kernel_name = "hybrid_swa_sink_mla_convglu_prefix_ctx"


def rmsnorm(x, g):
    xf = x.astype(jnp.float32)
    y = xf * lax.rsqrt(jnp.mean(xf * xf, axis=-1, keepdims=True) + EPS)
    return (y * g.astype(jnp.float32)).astype(x.dtype)


def modulate(h, shift, scale):
    return h * (1.0 + scale) + shift


def axial_angles(rows, rot_dim):
    row = jnp.repeat(jnp.arange(rows), GRID_W).astype(jnp.float32)
    col = jnp.tile(jnp.arange(GRID_W), rows).astype(jnp.float32)
    n_freq = rot_dim // 4
    inv = ROPE_BASE ** (-jnp.arange(n_freq, dtype=jnp.float32) / n_freq)
    return jnp.concatenate([row[:, None] * inv, col[:, None] * inv], axis=-1)


def apply_rope(x, ang):
    ang = ang.reshape((ang.shape[0],) + (1,) * (x.ndim - 3) + (ang.shape[-1],))
    cos = jnp.cos(ang).astype(x.dtype)
    sin = jnp.sin(ang).astype(x.dtype)
    x1 = x[..., 0::2]
    x2 = x[..., 1::2]
    return jnp.stack([x1 * cos - x2 * sin, x1 * sin + x2 * cos], axis=-1).reshape(x.shape)


def dwconv_centred(u, w, b):
    up = jnp.pad(u, ((0, 0), (1, 1), (0, 0)))
    return up[:, :-2] * w[0] + up[:, 1:-1] * w[1] + up[:, 2:] * w[2] + b


def conv_glu(h, w_in, conv_w, conv_b, w_out):
    ab = h @ w_in
    a, v = ab[..., :D_FF], ab[..., D_FF:]
    a = dwconv_centred(a, conv_w, conv_b)
    return (jax.nn.silu(a) * v) @ w_out


def window_gqa(h_lat, h_ctx, wqkv, wo, sink, ang, with_ctx_out):
    B, S, _ = h_lat.shape
    L = h_ctx.shape[1]
    G = A_Q_HEADS // A_KV_HEADS
    scale = HEAD_DIM ** -0.5

    def proj(h):
        n = h.shape[1]
        qkv = h @ wqkv
        q = qkv[..., :A_QW].reshape(B, n, A_KV_HEADS, G, HEAD_DIM)
        k = qkv[..., A_QW:A_QW + A_KW].reshape(B, n, A_KV_HEADS, HEAD_DIM)
        v = qkv[..., A_QW + A_KW:].reshape(B, n, A_KV_HEADS, HEAD_DIM)
        return q, k, v

    q_lat, k_lat, v_lat = proj(h_lat)
    q_lat = apply_rope(q_lat, ang) * scale
    k_lat = apply_rope(k_lat, ang)
    q_ctx, k_ctx, v_ctx = proj(h_ctx)
    sink_f = sink.astype(jnp.float32).reshape(A_KV_HEADS, G)[None, :, :, None, None]

    nb = S // BLOCK
    kv_len = BLOCK + 2 * WINDOW
    kp = jnp.pad(k_lat, ((0, 0), (WINDOW, WINDOW), (0, 0), (0, 0)))
    vp = jnp.pad(v_lat, ((0, 0), (WINDOW, WINDOW), (0, 0), (0, 0)))

    def block(i):
        q0 = i * BLOCK
        qb = lax.dynamic_slice_in_dim(q_lat, q0, BLOCK, axis=1)
        kb = lax.dynamic_slice_in_dim(kp, q0, kv_len, axis=1)
        vb = lax.dynamic_slice_in_dim(vp, q0, kv_len, axis=1)
        s_win = jnp.einsum('bqkgd,bskd->bkgqs', qb, kb).astype(jnp.float32)
        qpos = q0 + jnp.arange(BLOCK)
        kpos = q0 - WINDOW + jnp.arange(kv_len)
        valid = (jnp.abs(qpos[:, None] - kpos[None, :]) <= WINDOW) & (kpos >= 0) & (kpos < S)
        s_win = jnp.where(valid, s_win, NEG)
        s_ctx = jnp.einsum('bqkgd,bskd->bkgqs', qb, k_ctx).astype(jnp.float32)
        s_snk = jnp.broadcast_to(sink_f, s_win.shape[:-1] + (1,))
        p = jax.nn.softmax(jnp.concatenate([s_win, s_ctx, s_snk], axis=-1), axis=-1)
        o = (jnp.einsum('bkgqs,bskd->bqkgd', p[..., :kv_len].astype(vb.dtype), vb)
             + jnp.einsum('bkgqs,bskd->bqkgd', p[..., kv_len:kv_len + L].astype(v_ctx.dtype), v_ctx))
        return o.reshape(B, BLOCK, A_QW)

    o_lat = lax.map(block, jnp.arange(nb))
    o_lat = o_lat.transpose(1, 0, 2, 3).reshape(B, S, A_QW)
    out_lat = o_lat @ wo
    if not with_ctx_out:
        return out_lat, None
    s = jnp.einsum('bqkgd,bskd->bkgqs', q_ctx * scale, k_ctx).astype(jnp.float32)
    s_snk = jnp.broadcast_to(sink_f, s.shape[:-1] + (1,))
    p = jax.nn.softmax(jnp.concatenate([s, s_snk], axis=-1), axis=-1)[..., :-1]
    o_ctx = jnp.einsum('bkgqs,bskd->bqkgd', p.astype(v_ctx.dtype), v_ctx).reshape(B, L, A_QW)
    return out_lat, o_ctx @ wo


def mla(h_lat, h_ctx, wdown, qnorm_g, wuq, kvnorm_g, wuk, wuv, wo, ang, with_ctx_out):
    B, S, _ = h_lat.shape
    L = h_ctx.shape[1]
    scale = (B_NOPE + B_ROPE) ** -0.5

    def proj(h, ang_h):
        n = h.shape[1]
        d = h @ wdown
        cq = rmsnorm(d[..., :Q_LORA], qnorm_g)
        ckv = rmsnorm(d[..., Q_LORA:Q_LORA + KV_LORA], kvnorm_g)
        k_rope = d[..., Q_LORA + KV_LORA:][:, :, None, :]
        q = (cq @ wuq).reshape(B, n, B_HEADS, B_NOPE + B_ROPE)
        q_nope, q_rope = q[..., :B_NOPE], q[..., B_NOPE:]
        if ang_h is not None:
            q_rope = apply_rope(q_rope, ang_h)
            k_rope = apply_rope(k_rope, ang_h)
        k_nope = (ckv @ wuk).reshape(B, n, B_HEADS, B_NOPE)
        v = (ckv @ wuv).reshape(B, n, B_HEADS, B_V)
        q = jnp.concatenate([q_nope, q_rope], axis=-1) * scale
        k = jnp.concatenate([k_nope, jnp.broadcast_to(k_rope, (B, n, B_HEADS, B_ROPE))], axis=-1)
        return q, k, v

    q_lat, k_lat, v_lat = proj(h_lat, ang)
    q_ctx, k_ctx, v_ctx = proj(h_ctx, None)
    k_all = jnp.concatenate([k_ctx, k_lat], axis=1)
    v_all = jnp.concatenate([v_ctx, v_lat], axis=1)

    def block(i):
        qb = lax.dynamic_slice_in_dim(q_lat, i * BLOCK, BLOCK, axis=1)
        s = jnp.einsum('bqhd,bkhd->bhqk', qb, k_all).astype(jnp.float32)
        p = jax.nn.softmax(s, axis=-1)
        return jnp.einsum('bhqk,bkhd->bqhd', p.astype(v_all.dtype), v_all).reshape(B, BLOCK, B_HEADS * B_V)

    o_lat = lax.map(block, jnp.arange(S // BLOCK))
    o_lat = o_lat.transpose(1, 0, 2, 3).reshape(B, S, B_HEADS * B_V)
    out_lat = o_lat @ wo
    if not with_ctx_out:
        return out_lat, None
    s = jnp.einsum('bqhd,bkhd->bhqk', q_ctx, k_ctx).astype(jnp.float32)
    p = jax.nn.softmax(s, axis=-1)
    o_ctx = jnp.einsum('bhqk,bkhd->bqhd', p.astype(v_ctx.dtype), v_ctx).reshape(B, L, B_HEADS * B_V)
    return out_lat, o_ctx @ wo


def setup_inputs(seed: int = 0) -> dict:
    key = jax.random.key(seed)
    ks = jax.random.split(key, 24)
    f32 = jnp.float32

    def nrm(k, shape, fan_in):
        return jax.random.normal(k, shape, f32) * (fan_in ** -0.5)

    def gain(k, shape):
        return 1.0 + 0.02 * jax.random.normal(k, shape, f32)

    def bias(k, shape):
        return 0.02 * jax.random.normal(k, shape, f32)

    return {
        "x": jax.random.normal(ks[0], (BATCH, SEQ, D_MODEL), f32),
        "c": jax.random.normal(ks[1], (BATCH, D_MODEL), f32),
        "ctx": jax.random.normal(ks[2], (BATCH, CTX_LEN, D_MODEL), f32),
        "c_ctx": jax.random.normal(ks[3], (D_MODEL,), f32),
        "mod_w": nrm(ks[4], (DEPTH, D_MODEL, 6 * D_MODEL), D_MODEL),
        "mod_b": bias(ks[5], (DEPTH, 6 * D_MODEL)),
        "norm1_g": gain(ks[6], (DEPTH, D_MODEL)),
        "norm2_g": gain(ks[7], (DEPTH, D_MODEL)),
        "a_wqkv": nrm(ks[8], (N_A_LAYERS, D_MODEL, A_QW + 2 * A_KW), D_MODEL),
        "a_wo": nrm(ks[9], (N_A_LAYERS, A_QW, D_MODEL), A_QW),
        "a_sink": jax.random.normal(ks[10], (N_A_LAYERS, A_Q_HEADS), f32),
        "b_wdown": nrm(ks[11], (N_B_LAYERS, D_MODEL, Q_LORA + KV_LORA + B_ROPE), D_MODEL),
        "b_qnorm_g": gain(ks[12], (N_B_LAYERS, Q_LORA)),
        "b_wuq": nrm(ks[13], (N_B_LAYERS, Q_LORA, B_HEADS * (B_NOPE + B_ROPE)), Q_LORA),
        "b_kvnorm_g": gain(ks[14], (N_B_LAYERS, KV_LORA)),
        "b_wuk": nrm(ks[15], (N_B_LAYERS, KV_LORA, B_HEADS * B_NOPE), KV_LORA),
        "b_wuv": nrm(ks[16], (N_B_LAYERS, KV_LORA, B_HEADS * B_V), KV_LORA),
        "b_wo": nrm(ks[17], (N_B_LAYERS, B_HEADS * B_V, D_MODEL), B_HEADS * B_V),
        "f_win": nrm(ks[18], (DEPTH, D_MODEL, 2 * D_FF), D_MODEL),
        "f_conv_w": nrm(ks[19], (DEPTH, CONV_W, D_FF), CONV_W),
        "f_conv_b": bias(ks[20], (DEPTH, D_FF)),
        "f_wout": nrm(ks[21], (DEPTH, D_FF, D_MODEL), D_FF),
        "final_g": gain(ks[22], (D_MODEL,)),
    }


def reference(x, c, ctx, c_ctx, mod_w, mod_b, norm1_g, norm2_g, a_wqkv, a_wo, a_sink,
              b_wdown, b_qnorm_g, b_wuq, b_kvnorm_g, b_wuk, b_wuv, b_wo,
              f_win, f_conv_w, f_conv_b, f_wout, final_g):
    rows = x.shape[1] // GRID_W
    ang_a = axial_angles(rows, HEAD_DIM)
    ang_b = axial_angles(rows, B_ROPE)
    y = ctx
    silu_c = jax.nn.silu(c)
    silu_cc = jax.nn.silu(c_ctx)
    for i in range(DEPTH):
        with_ctx = i < DEPTH - 1
        mod_l = (silu_c @ mod_w[i] + mod_b[i])[:, None, :]
        mod_c = silu_cc @ mod_w[i] + mod_b[i]
        sh1, sc1, g1, sh2, sc2, g2 = jnp.split(mod_l, 6, axis=-1)
        csh1, csc1, cg1, csh2, csc2, cg2 = jnp.split(mod_c, 6, axis=-1)
        h_lat = modulate(rmsnorm(x, norm1_g[i]), sh1, sc1)
        h_ctx = modulate(rmsnorm(y, norm1_g[i]), csh1, csc1)
        j = i // N_MIXERS
        if i % N_MIXERS == 0:
            o_lat, o_ctx = window_gqa(h_lat, h_ctx, a_wqkv[j], a_wo[j], a_sink[j], ang_a, with_ctx)
        else:
            o_lat, o_ctx = mla(h_lat, h_ctx, b_wdown[j], b_qnorm_g[j], b_wuq[j], b_kvnorm_g[j],
                               b_wuk[j], b_wuv[j], b_wo[j], ang_b, with_ctx)
        x = x + g1 * o_lat
        h2 = modulate(rmsnorm(x, norm2_g[i]), sh2, sc2)
        x = x + g2 * conv_glu(h2, f_win[i], f_conv_w[i], f_conv_b[i], f_wout[i])
        if with_ctx:
            y = y + cg1 * o_ctx
            hc2 = modulate(rmsnorm(y, norm2_g[i]), csh2, csc2)
            y = y + cg2 * conv_glu(hc2, f_win[i], f_conv_w[i], f_conv_b[i], f_wout[i])
    return rmsnorm(x, final_g)
```

```python
import numpy as np
from contextlib import ExitStack
import concourse.bass as bass
import concourse.mybir as mybir
from concourse.bass_utils import run_bass_kernel_spmd

F32 = mybir.dt.float32
BF16 = mybir.dt.bfloat16
AF = mybir.ActivationFunctionType
ALU = mybir.AluOpType

D = 1024
T = 4096
L = 256
U = T + L
NB = U // 128
DFF = 2816
NCF = DFF // 128
DEPTH = 4
EPS = 1e-6
GRID_W = 64
ENGS = ("pe", "dve", "act", "pool", "sp")
_DBG = {}
ENGATTR = {"pe": "tensor", "dve": "vector", "act": "scalar", "pool": "gpsimd", "sp": "sync"}


class Sched:
    def __init__(self, nc, block, stack):
        self.nc, self.block, self.stack = nc, block, stack
        self.sem, self.cnt = {}, {}
        self.seen = {e: {} for e in ENGS}
        self.res = {}
        for e in ENGS:
            self.sem[e] = stack.enter_context(nc.semaphore("s_" + e))
            self.cnt[e] = 0

    def chan(self, name):
        if name not in self.sem:
            self.sem[name] = self.stack.enter_context(self.nc.semaphore("c_" + name))
            self.cnt[name] = 0
        return name

    def _deps(self, eng, reads, writes):
        deps = {}

        def add(p, c, raw):
            if p == eng and eng == "pe":
                return
            if deps.get(p, 0) < c:
                deps[p] = c

        for r in reads:
            st = self.res.get(r)
            if st and st["w"]:
                add(st["w"][0], st["w"][1], True)
        for w in writes:
            st = self.res.get(w)
            if st:
                if st["w"]:
                    add(st["w"][0], st["w"][1], False)
                for p, c in st["r"].items():
                    add(p, c, False)
        return deps

    def _commit(self, prod, val, reads, writes):
        for r in reads:
            st = self.res.setdefault(r, {"w": None, "r": {}})
            if st["r"].get(prod, 0) < val:
                st["r"][prod] = val
        for w in writes:
            self.res[w] = {"w": (prod, val), "r": {}}

    def _waits(self, eng, deps):
        waits = []
        for p, c in deps.items():
            if self.seen[eng].get(p, 0) < c:
                waits.append((p, c))
                self.seen[eng][p] = c
        return waits

    @staticmethod
    def _excl(r, w):
        ps = [x for x in r if isinstance(x, tuple) and x and x[0] == "ps"]
        if not ps:
            return list(r), list(w)
        return [x for x in r if x not in ps], list(w) + [x for x in ps if x not in w]

    def rec(self):
        self._rec = []

    def end_rec(self):
        lst, self._rec = self._rec, None
        return lst

    def play(self, lists):
        lists = [l for l in lists if l]
        idx = [0] * len(lists)
        total = sum(len(l) for l in lists)
        for _ in range(total):
            best, bi = None, -1
            for i, l in enumerate(lists):
                if idx[i] < len(l):
                    frac = idx[i] / len(l)
                    if best is None or frac < best:
                        best, bi = frac, i
            kind, a = lists[bi][idx[bi]]
            idx[bi] += 1
            if kind == "op":
                self.op(*a)
            else:
                self.dma(*a)

    def op(self, eng, fn, r=(), w=()):
        if getattr(self, "_rec", None) is not None:
            self._rec.append(("op", (eng, fn, tuple(r), tuple(w))))
            return
        r, w = self._excl(r, w)
        deps = self._deps(eng, r, w)
        waits = self._waits(eng, deps)
        self.cnt[eng] += 1
        val = self.cnt[eng]
        sem = self.sem

        def body(e):
            for p, c in waits:
                e.wait_ge(sem[p], c)
            fn(e).then_inc(sem[eng], 1)

        getattr(self.block, ENGATTR[eng])(body)
        self._commit(eng, val, r, w)

    def pe(self, fn, r=(), w=()):
        self.op("pe", fn, r, w)

    def dve(self, fn, r=(), w=()):
        self.op("dve", fn, r, w)

    def act(self, fn, r=(), w=()):
        self.op("act", fn, r, w)

    def pool(self, fn, r=(), w=()):
        self.op("pool", fn, r, w)

    def dma(self, q, chan, fn, r=(), w=()):
        if getattr(self, "_rec", None) is not None:
            self._rec.append(("dma", (q, chan, fn, tuple(r), tuple(w))))
            return
        self.chan(chan)
        r, w = self._excl(r, w)
        deps = self._deps(q, r, w)
        if self.cnt[chan] > 0 and deps.get(chan, 0) < self.cnt[chan]:
            deps[chan] = self.cnt[chan]
        waits = self._waits(q, deps)
        self.cnt[chan] += 16
        val = self.cnt[chan]
        sem = self.sem

        def body(e):
            for p, c in waits:
                e.wait_ge(sem[p], c)
            fn(e).then_inc(sem[chan], 16)

        getattr(self.block, ENGATTR[q])(body)
        self._commit(chan, val, r, w)

    def raw(self, eng, fn):
        getattr(self.block, ENGATTR[eng])(lambda e: fn(e))

    def barrier(self):
        snap = dict(self.cnt)
        sem = self.sem
        for eng in ENGS:
            waits = []
            for p, c in snap.items():
                if c > 0 and self.seen[eng].get(p, 0) < c:
                    if p == eng and eng == "pe":
                        continue
                    waits.append((p, c))
                    self.seen[eng][p] = c
            if waits:
                def body(e, waits=waits):
                    for p, c in waits:
                        e.wait_ge(sem[p], c)
                getattr(self.block, ENGATTR[eng])(body)
        self.res = {}

    def finish(self):
        snap = dict(self.cnt)
        sem = self.sem

        def body(e):
            for p, c in snap.items():
                if c > 0:
                    e.wait_ge(sem[p], c)
        self.block.sync(body)


def build(n_phases=None, debug=False):
    nc = bass.Bass("TRN2", target_bir_lowering=False)

    def din(name, shape, dt=F32):
        return nc.dram_tensor(name, list(shape), dt, kind="ExternalInput").ap()

    x_in = din("x", [T, D])
    ctx_in = din("ctx", [L, D])
    cc_in = din("cc", [16, 128])
    mod_w = din("mod_w", [DEPTH, D, 6 * D])
    mod_b = din("mod_b", [DEPTH, 6 * D])
    norm1_g = din("norm1_g", [DEPTH, D])
    norm2_g = din("norm2_g", [DEPTH, D])
    a_wqkv = din("a_wqkv", [2, D, 1536])
    a_wo = din("a_wo", [2, D, D])
    a_sink = din("a_sink", [2, 16])
    b_wdown = din("b_wdown", [2, D, 800])
    b_qnorm_g = din("b_qnorm_g", [2, 512])
    b_wuq = din("b_wuq", [2, 512, 1536])
    b_kvnorm_g = din("b_kvnorm_g", [2, 256])
    b_wuk = din("b_wuk", [2, 256, 1024])
    b_wuv = din("b_wuv", [2, 256, 1024])
    b_wo = din("b_wo", [2, D, D])
    f_win = din("f_win", [DEPTH, D, 2 * DFF])
    f_conv_w = din("f_conv_w", [DEPTH, 3, DFF])
    f_conv_b = din("f_conv_b", [DEPTH, DFF])
    f_wout = din("f_wout", [DEPTH, DFF, D])
    final_g = din("final_g", [1, D])
    ident_in = din("ident", [128, 128])
    maskp_in = din("maskp", [128, 128])
    maskn_in = din("maskn", [128, 128])
    cosA_in = din("cosA", [T, 32])
    sinA_in = din("sinA", [T, 32])
    cosB_in = din("cosB", [T, 16])
    sinB_in = din("sinB", [T, 16])

    out = nc.dram_tensor("out", [T, D], F32, kind="ExternalOutput").ap()
    if debug:
        Xs = nc.dram_tensor("xs", [U, D], F32, kind="ExternalOutput").ap()
    else:
        Xs = nc.dram_tensor("xs", [U, D], F32).ap()
    qT_d = nc.dram_tensor("qT", [16, 96, U], BF16).ap()

    stack = ExitStack()
    with stack:
        uniq = [0]

        def sb(name, shape, dt=F32, st=stack):
            uniq[0] += 1
            return st.enter_context(nc.sbuf_tensor("%s_%d" % (name, uniq[0]), list(shape), dt))

        banks = [stack.enter_context(nc.psum_tensor("bank%d" % i, [128, 512], F32)) for i in range(8)]

        def PB(i):
            return ("ps", i)

        identF = sb("identF", [128, 128])
        identB = sb("identB", [128, 128], BF16)
        zerosB = sb("zerosB", [128, 512], BF16)
        maskP = sb("maskP", [128, 128], BF16)
        maskN = sb("maskN", [128, 128], BF16)
        n1g = sb("n1g", [128, 32])
        n2g = sb("n2g", [128, 32])
        modbF = sb("modbF", [128, DEPTH * 48])
        cw = sb("cw", [128, DEPTH * 3 * NCF])
        cb = sb("cb", [128, DEPTH * NCF])
        qng = sb("qng", [128, 8])
        kvg = sb("kvg", [128, 4])
        cF = sb("cF", [128, 16])
        finalgB = sb("finalgB", [128, D])
        sinkB = sb("sinkB", [128, 32])
        esink = sb("esink", [128, 16])
        neghalf = sb("neghalf", [128, 1])
        gateB = sb("gateB", [128, 4, D])
        modF = sb("modF", [128, 48, 2])
        Gs = sb("Gs", [128, 2, 8, 2])
        scT = sb("scT", [128, 8, 2], BF16)
        screp = sb("screp", [128, 8, 2, 128], BF16)
        stage = sb("stage", [128, 128])
        ssr = sb("ssr", [128, 8])
        msr = sb("msr", [128, 8])
        rstdr = sb("rstdr", [128, 8])
        junk = sb("junk", [128, D], BF16)
        xn = sb("xn", [128, 2, D])

        block = stack.enter_context(nc.Block())
        S = Sched(nc, block, stack)
        ctr = {"st": 0, "xn": 0}

        S.dma("sp", "k0", lambda e: e.dma_start(out=identF[:, :], in_=ident_in[:, :]), w=["identF"])
        S.dma("pool", "k1", lambda e: e.dma_start(out=identB[:, :], in_=ident_in[:, :]), w=["identB"])
        S.dma("pool", "k2", lambda e: e.dma_start(out=maskP[:, :], in_=maskp_in[:, :]), w=["maskP"])
        S.dma("pool", "k3", lambda e: e.dma_start(out=maskN[:, :], in_=maskn_in[:, :]), w=["maskN"])
        S.dve(lambda e: e.memset(zerosB[:, :], 0.0), w=["zerosB"])
        S.dve(lambda e: e.memset(neghalf[:, :], -0.5), w=["neghalf"])
        S.dma("sp", "k4", lambda e: e.dma_start(out=finalgB[:, :], in_=final_g[0:1, :].broadcast_to([128, D])), w=["finalgB"])
        S.dma("sp", "k5", lambda e: e.dma_start(out=sinkB[:, :], in_=a_sink.rearrange("a h -> (a h)").unsqueeze(0).broadcast_to([128, 32])), w=["sinkB"])

        def featmajor(dst, src2d, n):
            S.dma("sp", "k5", lambda e: e.dma_start(out=stage[0:n, :], in_=src2d), w=["stage"])
            S.pe(lambda e: e.transpose(out=banks[0][:, 0:n], in_=stage[0:n, :], identity=identF[0:n, 0:n]),
                 r=["stage", "identF"], w=[PB(0)])
            S.dve(lambda e: e.tensor_copy(out=dst, in_=banks[0][:, 0:n]), r=[PB(0)], w=["params"])

        featmajor(n1g[:, :], norm1_g.rearrange("l (k p) -> (l k) p", p=128), 32)
        featmajor(n2g[:, :], norm2_g.rearrange("l (k p) -> (l k) p", p=128), 32)
        mb2 = mod_b.rearrange("l (k p) -> (l k) p", p=128)
        featmajor(modbF[:, 0:128], mb2[0:128, :], 128)
        featmajor(modbF[:, 128:192], mb2[128:192, :], 64)
        cw2 = f_conv_w.rearrange("l j (k p) -> (l j k) p", p=128)
        featmajor(cw[:, 0:128], cw2[0:128, :], 128)
        featmajor(cw[:, 128:256], cw2[128:256, :], 128)
        featmajor(cw[:, 256:264], cw2[256:264, :], 8)
        featmajor(cb[:, :], f_conv_b.rearrange("l (k p) -> (l k) p", p=128), 88)
        featmajor(qng[:, :], b_qnorm_g.rearrange("l (k p) -> (l k) p", p=128), 8)
        featmajor(kvg[:, :], b_kvnorm_g.rearrange("l (k p) -> (l k) p", p=128), 4)
        featmajor(cF[:, :], cc_in[:, :], 16)
        S.raw("act", lambda e: e.preload_act_table(AF.Silu))
        S.act(lambda e: e.activation(out=scT[:, :, :].rearrange("p k j -> p j k"),
                                     in_=cF[:, :].rearrange("p (j k) -> p j k", j=2), func=AF.Silu),
              r=["params"], w=["scT"])
        for j in range(2):
            S.dve(lambda e, j=j: e.tensor_copy(out=screp[:, :, j, :], in_=scT[:, :, j:j + 1].broadcast_to([128, 8, 128])),
                  r=["scT"], w=["screp"])

        def rstd_of(ss_ap, n_feat):
            i = ctr["st"] % 8
            return i

        def new_stat():
            i = ctr["st"] % 8
            ctr["st"] += 1
            return i

        def row_rstd(src, src_res, n_feat):
            i = new_stat()
            rn = ("st", i)
            S.act(lambda e: e.activation(out=junk[:, 0:n_feat], in_=src, func=AF.Square, accum_out=ssr[:, i:i + 1]),
                  r=src_res, w=[rn])
            S.dve(lambda e: e.tensor_scalar(out=msr[:, i:i + 1], in0=ssr[:, i:i + 1], scalar1=1.0 / n_feat, scalar2=EPS,
                                            op0=ALU.mult, op1=ALU.add), r=[rn], w=[rn])
            S.pool(lambda e: e.tensor_tensor(out=rstdr[:, i:i + 1], in0=msr[:, i:i + 1], in1=neghalf[:, :], op=ALU.pow),
                   r=[rn, "neghalf"], w=[rn])
            return rstdr[:, i:i + 1], rn

        def normT(xb, xb_res, G, SH, dst_fn, dst_res, psb):
            rs, rn = row_rstd(xb, xb_res, D)
            s = ctr["xn"] % 2
            ctr["xn"] += 1
            S.act(lambda e: e.activation(out=xn[:, s, :], in_=xb, func=AF.Copy, scale=rs), r=list(xb_res) + [rn], w=[("xn", s)])
            for hlf in range(2):
                for kk in range(4):
                    k = hlf * 4 + kk
                    S.pe(lambda e, k=k, kk=kk: e.transpose(out=banks[psb][:, kk * 128:(kk + 1) * 128],
                                                           in_=xn[:, s, k * 128:(k + 1) * 128], identity=identF[:, :]),
                         r=[("xn", s), "identF"], w=[PB(psb)])
                for kk in range(4):
                    k = hlf * 4 + kk
                    S.dve(lambda e, k=k, kk=kk: e.tensor_scalar(out=dst_fn(k), in0=banks[psb][:, kk * 128:(kk + 1) * 128],
                                                               scalar1=G(k), scalar2=SH(k), op0=ALU.mult, op1=ALU.add),
                          r=[PB(psb), "mods"], w=[dst_res])

        def Xsrc(l, b):
            if l == 0:
                return ctx_in[b * 128:(b + 1) * 128, :] if b < 2 else x_in[(b - 2) * 128:(b - 1) * 128, :]
            return Xs[b * 128:(b + 1) * 128, :]

        def rope_tm(ps_ap, H, npair, cs, sn, out_ap, tmp, res_in, res_out, tmp_res):
            csb = cs.unsqueeze(1).broadcast_to([128, H, npair])
            snb = sn.unsqueeze(1).broadcast_to([128, H, npair])
            ev, od = ps_ap[:, :, :, 0], ps_ap[:, :, :, 1]
            t0 = tmp[:, 0, 0:H * npair].rearrange("p (h i) -> p h i", h=H)
            t1 = tmp[:, 1, 0:H * npair].rearrange("p (h i) -> p h i", h=H)
            S.dve(lambda e: e.tensor_tensor(out=t0, in0=ev, in1=csb, op=ALU.mult), r=res_in, w=[tmp_res + "0"])
            S.dve(lambda e: e.tensor_tensor(out=t1, in0=od, in1=snb, op=ALU.mult), r=res_in, w=[tmp_res + "1"])
            S.dve(lambda e: e.tensor_tensor(out=out_ap[:, :, :, 0], in0=t0, in1=t1, op=ALU.subtract),
                  r=[tmp_res + "0", tmp_res + "1"], w=[res_out])
            S.dve(lambda e: e.tensor_tensor(out=t0, in0=ev, in1=snb, op=ALU.mult), r=res_in, w=[tmp_res + "0"])
            S.dve(lambda e: e.tensor_tensor(out=t1, in0=od, in1=csb, op=ALU.mult), r=res_in, w=[tmp_res + "1"])
            S.dve(lambda e: e.tensor_tensor(out=out_ap[:, :, :, 1], in0=t0, in1=t1, op=ALU.add),
                  r=[tmp_res + "0", tmp_res + "1"], w=[res_out])

        def modulation(l):
            with ExitStack() as ms:
                mw = [sb("mw%d" % i, [128, 8, D], BF16, ms) for i in range(2)]
                modbB = sb("modbB", [128, 2, D], F32, ms)
                mwv = mod_w[l].rearrange("(k p) n -> p k n", p=128)
                psF = banks[7]
                for m in range(6):
                    buf = mw[m % 2]
                    S.dma("pool", "mw%d" % (m % 2),
                          lambda e, m=m, buf=buf: e.dma_start(out=buf[:, :, :], in_=mwv[:, :, m * D:(m + 1) * D]),
                          w=[("mw", m % 2)])
                    for js in range(8):
                        col = (m * 8 + js) * 2
                        for k in range(8):
                            S.pe(lambda e, js=js, k=k, col=col, buf=buf: e.matmul(
                                psF[:, col:col + 2], lhsT=buf[:, k, js * 128:(js + 1) * 128], rhs=scT[:, k, :],
                                start=(k == 0), stop=(k == 7)), r=[("mw", m % 2), "scT"], w=[PB(7)])
                    if m in (2, 5):
                        gi = 0 if m == 2 else 2
                        mi = 0 if m == 2 else 1
                        S.dma("sp", "k6", lambda e, m=m, mi=mi: e.dma_start(
                            out=modbB[:, mi, :], in_=mod_b[l:l + 1, m * D:(m + 1) * D].broadcast_to([128, D])), w=[("modbB", mi)])
                        for j in range(2):
                            for hlf in range(2):
                                pb = 5 + hlf
                                for k in range(8):
                                    S.pe(lambda e, j=j, hlf=hlf, k=k, pb=pb, buf=buf: e.matmul(
                                        banks[pb][:, :], lhsT=screp[:, k, j, :], rhs=buf[:, k, hlf * 512:(hlf + 1) * 512],
                                        start=(k == 0), stop=(k == 7)), r=[("mw", m % 2), "screp"], w=[PB(pb)])
                                S.dve(lambda e, j=j, hlf=hlf, pb=pb, gi=gi, mi=mi: e.tensor_tensor(
                                    out=gateB[:, gi + j, hlf * 512:(hlf + 1) * 512], in0=banks[pb][:, :],
                                    in1=modbB[:, mi, hlf * 512:(hlf + 1) * 512], op=ALU.add),
                                    r=[PB(pb), ("modbB", mi)], w=["gates"])
                S.dve(lambda e: e.tensor_tensor(out=modF[:, :, :], in0=psF[:, 0:96].rearrange("p (f j) -> p f j", j=2),
                                                in1=modbF[:, l * 48:(l + 1) * 48].unsqueeze(2).broadcast_to([128, 48, 2]),
                                                op=ALU.add), r=[PB(7), "params"], w=["mods"])
                for ni, (gt, sc0) in enumerate(((n1g, 8), (n2g, 32))):
                    S.dve(lambda e, ni=ni, sc0=sc0: e.tensor_scalar(out=Gs[:, ni, :, :], in0=modF[:, sc0:sc0 + 8, :], scalar1=1.0,
                                                                   scalar2=None, op0=ALU.add), r=["mods"], w=["mods"])
                    S.dve(lambda e, ni=ni, gt=gt: e.tensor_tensor(
                        out=Gs[:, ni, :, :], in0=Gs[:, ni, :, :],
                        in1=gt[:, l * 8:(l + 1) * 8].unsqueeze(2).broadcast_to([128, 8, 2]), op=ALU.mult),
                        r=["mods", "params"], w=["mods"])
                S.barrier()

        def Gf(ni, j):
            return lambda k: Gs[:, ni, k, j:j + 1]

        def SHf(ni, j):
            base = 0 if ni == 0 else 24
            return lambda k: modF[:, base + k, j:j + 1]

        def proj_T(O_ap, O_res, tl, sl=0):
            OTt = tl["OT"][sl]
            pbT = tl["pbT"]
            psT = banks[pbT][:, :].bitcast(BF16)
            for k in range(8):
                S.pe(lambda e, k=k: e.transpose(out=psT[:, k * 128:(k + 1) * 128], in_=O_ap[:, k * 128:(k + 1) * 128],
                                               identity=identB[:, :]), r=list(O_res) + ["identB"], w=[PB(pbT)])
            S.act(lambda e: e.activation(out=OTt[:, :], in_=psT[:, :], func=AF.Copy), r=[PB(pbT)], w=[("OT", sl)])

        def proj_MM(wo_sb, xb_ap, xb_res, gate_idx, b, tl, sl=0):
            OTt, xnew = tl["OT"][sl], tl["xnew"][sl]
            for hlf in range(2):
                pb = tl["pbo"][hlf]
                for k in range(8):
                    S.pe(lambda e, k=k, hlf=hlf, pb=pb: e.matmul(banks[pb][:, :], lhsT=OTt[:, k * 128:(k + 1) * 128],
                                                               rhs=wo_sb[:, k, hlf * 512:(hlf + 1) * 512],
                                                               start=(k == 0), stop=(k == 7)), r=[("OT", sl), "wo"], w=[PB(pb)])
                S.dve(lambda e, hlf=hlf, pb=pb: e.tensor_tensor(out=xnew[:, hlf * 512:(hlf + 1) * 512], in0=banks[pb][:, :],
                                                              in1=gateB[:, gate_idx, hlf * 512:(hlf + 1) * 512], op=ALU.mult),
                      r=[PB(pb), "gates"], w=[("xnew", sl, hlf)])
                S.pool(lambda e, hlf=hlf: e.tensor_tensor(out=xnew[:, hlf * 512:(hlf + 1) * 512], in0=xnew[:, hlf * 512:(hlf + 1) * 512],
                                                         in1=xb_ap[:, hlf * 512:(hlf + 1) * 512], op=ALU.add),
                       r=[("xnew", sl, hlf)] + list(xb_res), w=[("xnew", sl, hlf)])
            S.dma("sp", "xst%d" % sl, lambda e: e.dma_start(out=Xs[b * 128:(b + 1) * 128, :], in_=xnew[:, :]),
                  r=[("xnew", sl, 0), ("xnew", sl, 1)], w=[("X", b)])

        def proj_residual(O_ap, O_res, wo_sb, xb_ap, xb_res, gate_idx, b, tl):
            proj_T(O_ap, O_res, tl, 0)
            proj_MM(wo_sb, xb_ap, xb_res, gate_idx, b, tl, 0)

        def phase_A(l):
            ja = l // 2
            with_ctx = l < DEPTH - 1
            S.raw("act", lambda e: e.preload_act_table(AF.Exp))
            with ExitStack() as ps_:
                wqkv = sb("wqkv", [128, 8, 1536], BF16, ps_)
                wo_sb = sb("wo", [128, 8, D], BF16, ps_)
                KT = sb("KT", [128, 4, U], BF16, ps_)
                Vaug = sb("Vaug", [128, NB, 4, 65], BF16, ps_)
                xbr = sb("xbr", [128, 4, D], F32, ps_)
                cosA = sb("cosA", [128, 32, 32], F32, ps_)
                sinA = sb("sinA", [128, 32, 32], F32, ps_)
                S.dma("sp", "k6", lambda e: e.dma_start(out=cosA[:, :, :], in_=cosA_in.rearrange("(b p) i -> p b i", p=128)), w=["cosA"])
                S.dma("sp", "k7", lambda e: e.dma_start(out=sinA[:, :, :], in_=sinA_in.rearrange("(b p) i -> p b i", p=128)), w=["sinA"])
                hT = sb("hT", [128, 2, 8, 128], BF16, ps_)
                qr = sb("qr", [128, 16, 32, 2], BF16, ps_)
                krd = sb("krd", [128, 4, 2, 64], BF16, ps_)
                rtmp = sb("rtmp", [128, 2, 256], F32, ps_)
                QT = sb("QT", [128, 3, 16, 128], BF16, ps_)
                PT = sb("PT", [128, 2, 5, 4, 128], BF16, ps_)
                Osb = sb("Osb", [128, D], BF16, ps_)
                den = sb("den", [128, 2, 4], F32, ps_)
                OTt = sb("OTt", [128, D], BF16, ps_)
                xnew = sb("xnew", [128, D], F32, ps_)
                tl = {"OT": [OTt], "xnew": [xnew], "pbT": 6, "pbo": (7, 6)}

                wv = a_wqkv[ja].rearrange("(k p) n -> p k n", p=128)
                for i in range(3):
                    S.dma("pool", "wA%d" % i, lambda e, i=i: e.dma_start(out=wqkv[:, :, i * 512:(i + 1) * 512], in_=wv[:, :, i * 512:(i + 1) * 512]),
                          w=["wqkv"])
                wov = a_wo[ja].rearrange("(k p) n -> p k n", p=128)
                for i in range(2):
                    S.dma("pool", "wA%d" % i, lambda e, i=i: e.dma_start(out=wo_sb[:, :, i * 512:(i + 1) * 512], in_=wov[:, :, i * 512:(i + 1) * 512]),
                          w=["wo"])
                S.pool(lambda e: e.memset(Vaug[:, :, :, 64:65], 1.0), w=["Vones"])
                for qs0 in range(3):
                    S.pool(lambda e, qs0=qs0: e.memset(QT[:, qs0, :, :], 0.0), w=[("QT", qs0)])
                S.act(lambda e: e.activation(out=esink[:, :], in_=sinkB[:, ja * 16:(ja + 1) * 16], func=AF.Exp), r=["sinkB"], w=["esink"])

                def xloadA(b):
                    S.dma("sp", "xld%d" % (b % 2), lambda e: e.dma_start(out=xbr[:, b % 4, :], in_=Xsrc(l, b)),
                          r=[("X", b)], w=[("xb", b % 4)])

                xloadA(0)

                def stage1(b):
                    j = 1 if b < 2 else 0
                    xs_ = b % 4
                    if b + 1 < NB:
                        xloadA(b + 1)
                    hs = b % 2
                    normT(xbr[:, xs_, :], [("xb", xs_)], Gf(0, j), SHf(0, j), lambda k: hT[:, hs, k, :], ("hT", hs), 0)
                    need_q = (b >= 2) or with_ctx

                    def piece(pc, pb):
                        for k in range(8):
                            S.pe(lambda e, k=k: e.matmul(banks[pb][:, :], lhsT=hT[:, hs, k, :], rhs=wqkv[:, k, pc * 512:(pc + 1) * 512],
                                                         start=(k == 0), stop=(k == 7)), r=[("hT", hs), "wqkv"], w=[PB(pb)])

                    piece(2, 1)
                    if b >= 2:
                        blk = b - 2
                        rope_tm(banks[1][:, 0:256].rearrange("p (h i t) -> p h i t", h=4, t=2), 4, 32,
                                cosA[:, blk, :], sinA[:, blk, :],
                                krd[:, :, 0, :].rearrange("p h (i t) -> p h i t", t=2), rtmp,
                                [PB(1), "cosA", "sinA"], "krd0", "rtmp")
                    else:
                        S.act(lambda e: e.activation(out=krd[:, :, 0, :], in_=banks[1][:, 0:256].rearrange("p (h d) -> p h d", h=4),
                                                     func=AF.Copy), r=[PB(1)], w=["krd0"])
                    S.act(lambda e: e.activation(out=Vaug[:, b, :, 0:64], in_=banks[1][:, 256:512].rearrange("p (h d) -> p h d", h=4),
                                                 func=AF.Copy), r=[PB(1)], w=[("V", b)])
                    S.pool(lambda e: e.tensor_copy(out=krd[:, :, 1, :], in_=krd[:, :, 0, :]), r=["krd0"], w=["krd1"])
                    psk = banks[0][:, :].bitcast(BF16)
                    kflat = krd[:, :, :, :].rearrange("p h c d -> p (h c d)")
                    for g in range(4):
                        S.pe(lambda e, g=g: e.transpose(out=psk[:, g * 128:(g + 1) * 128], in_=kflat[:, g * 128:(g + 1) * 128],
                                                       identity=identB[:, :]), r=["krd0", "krd1", "identB"], w=[PB(0)])
                    S.dve(lambda e: e.tensor_copy(out=KT[:, :, b * 128:(b + 1) * 128],
                                                  in_=psk[:, 0:512].rearrange("p (g t) -> p g t", g=4)), r=[PB(0)], w=[("K", b)])
                    if not need_q:
                        return
                    for pc, pb in ((0, 2), (1, 1)):
                        piece(pc, pb)
                        if b >= 2:
                            rope_tm(banks[pb][:, :].rearrange("p (h i t) -> p h i t", h=8, t=2), 8, 32,
                                    cosA[:, b - 2, :], sinA[:, b - 2, :], qr[:, pc * 8:(pc + 1) * 8, :, :], rtmp,
                                    [PB(pb), "cosA", "sinA"], ("qr", pc), "rtmp")
                        else:
                            S.act(lambda e, pc=pc, pb=pb: e.activation(out=qr[:, pc * 8:(pc + 1) * 8, :, :].rearrange("p h i t -> p (h i t)"),
                                                                      in_=banks[pb][:, :], func=AF.Copy), r=[PB(pb)], w=[("qr", pc)])
                    psq = banks[3][:, :].bitcast(BF16)
                    qs = b % 3
                    qflat = qr[:, :, :, :].rearrange("p h i t -> p (h i t)")
                    for k in range(8):
                        S.pe(lambda e, k=k: e.transpose(out=psq[:, k * 128:(k + 1) * 128], in_=qflat[:, k * 128:(k + 1) * 128],
                                                       identity=identB[:, :]), r=[("qr", k // 4), "identB"], w=[PB(3)])
                    S.act(lambda e: e.activation(out=QT[0:64, qs, 0:16:2, :], in_=psq[0:64, :].rearrange("p (k t) -> p k t", k=8), func=AF.Copy),
                          r=[PB(3)], w=[("QT", qs, 0)])
                    S.dve(lambda e: e.tensor_copy(out=QT[64:128, qs, 1:16:2, :], in_=psq[64:128, :].rearrange("p (k t) -> p k t", k=8)),
                          r=[PB(3)], w=[("QT", qs, 1)])

                def stage2(b):
                    j = 1 if b < 2 else 0
                    qs = b % 3
                    if b < 2:
                        keys = [(0, None), (1, None)]
                    else:
                        keys = [(0, None), (1, None)]
                        if b - 1 >= 2:
                            keys.append((b - 1, "P"))
                        keys.append((b, None))
                        if b + 1 < NB:
                            keys.append((b + 1, "N"))
                    nk = len(keys)
                    def qk_exp(g):
                        ps_ = (b * 4 + g) % 2
                        pt_res = ("PT", ps_)
                        for ji, (kb, mk) in enumerate(keys):
                            pb = 4 + ji % 2
                            S.pe(lambda e, kb=kb, pb=pb: e.matmul(
                                banks[pb][:, :].rearrange("p (c t) -> p c t", c=4),
                                lhsT=KT[:, g, kb * 128:(kb + 1) * 128], rhs=QT[:, qs, 4 * g:4 * g + 4, :], start=True, stop=True),
                                r=[("K", kb), ("QT", qs), ("QT", qs, 0), ("QT", qs, 1)], w=[PB(pb)])
                            S.act(lambda e, pb=pb, ji=ji: e.activation(
                                out=PT[:, ps_, ji, :, :], in_=banks[pb][:, :].rearrange("p (c t) -> p c t", c=4), func=AF.Exp, scale=0.125),
                                r=[PB(pb)], w=[(pt_res, ji)])
                            if mk is not None:
                                mt = maskP if mk == "P" else maskN
                                S.pool(lambda e, ji=ji, mt=mt: e.tensor_tensor(
                                    out=PT[:, ps_, ji, :, :], in0=PT[:, ps_, ji, :, :],
                                    in1=mt[:, :].unsqueeze(1).broadcast_to([128, 4, 128]), op=ALU.mult),
                                    r=[(pt_res, ji), "maskP", "maskN"], w=[(pt_res, ji)])

                    def pv_norm(g):
                        ps_ = (b * 4 + g) % 2
                        pt_res = ("PT", ps_)
                        ob = 6
                        for s in range(4):
                            for ji, (kb, mk) in enumerate(keys):
                                S.pe(lambda e, s=s, ji=ji, kb=kb: e.matmul(
                                    banks[ob][:, s * 65:(s + 1) * 65], lhsT=PT[:, ps_, ji, s, :], rhs=Vaug[:, kb, g, :],
                                    start=(ji == 0), stop=(ji == nk - 1)),
                                    r=[(pt_res, ji), ("V", kb), "Vones"], w=[PB(ob)])
                        ov = banks[ob][:, 0:260].rearrange("p (s d) -> p s d", s=4)
                        S.dve(lambda e: e.tensor_tensor(out=den[:, ps_, :], in0=ov[:, :, 64], in1=esink[:, 4 * g:4 * g + 4], op=ALU.add),
                              r=[PB(ob), "esink"], w=[("den", ps_)])
                        S.dve(lambda e: e.reciprocal(out=den[:, ps_, :], in_=den[:, ps_, :]), r=[("den", ps_)], w=[("den", ps_)])
                        Ov = Osb[:, :].rearrange("p (h d) -> p h d", d=64)
                        S.dve(lambda e: e.tensor_tensor(
                            out=Ov[:, 4 * g:4 * g + 4, :], in0=ov[:, :, 0:64],
                            in1=den[:, ps_, :].unsqueeze(2).broadcast_to([128, 4, 64]), op=ALU.mult),
                            r=[PB(ob), ("den", ps_)], w=["Osb"])

                    qk_exp(0)
                    for g in range(4):
                        if g + 1 < 4:
                            qk_exp(g + 1)
                        pv_norm(g)
                    proj_residual(Osb, ["Osb"], wo_sb, xbr[:, b % 4, :], [("xb", b % 4)], 0 + j, b, tl)

                stage1(0)
                stage1(1)
                if with_ctx:
                    stage2(0)
                    stage2(1)
                stage1(2)
                stage1(3)
                for b in range(2, NB):
                    S.rec()
                    stage2(b)
                    r2 = S.end_rec()
                    r1 = None
                    if b + 2 < NB:
                        S.rec()
                        stage1(b + 2)
                        r1 = S.end_rec()
                    S.play([r2, r1])
                S.barrier()

        def phase_B(l):
            jb = l // 2
            with_ctx = l < DEPTH - 1
            b0 = 0 if with_ctx else 2
            scale = 96.0 ** -0.5
            S.raw("act", lambda e: e.preload_act_table(AF.Exp))
            with ExitStack() as pB:
                ckvT = sb("ckvT", [128, 2, U], BF16, pB)
                Kb = sb("Kb", [128, 2, U], BF16, pB)
                with ExitStack() as p1:
                    wdn = sb("wdn", [128, 8, 800], BF16, p1)
                    wuq = sb("wuq", [128, 4, 1536], BF16, p1)
                    xbr = sb("xbr", [128, 2, D], F32, p1)
                    hT = sb("hT", [128, 2, 8, 128], BF16, p1)
                    cqn = sb("cqn", [128, 512], F32, p1)
                    ckn = sb("ckn", [128, 256], F32, p1)
                    cqT = sb("cqT", [128, 2, 4, 128], BF16, p1)
                    rtmp1 = sb("rtmp1", [128, 2, 16], F32, p1)
                    krs = sb("krs", [128, 96], F32, p1)
                    rtmp = sb("rtmp", [128, 2, 256], F32, p1)
                    qf = sb("qf", [128, 16, 32], F32, p1)
                    qrb = sb("qrb", [128, 16, 96], BF16, p1)
                    QTa = sb("QTa", [128, 2, 16, 512], BF16, p1)
                    cosB = sb("cosB", [128, 32, 16], F32, p1)
                    sinB = sb("sinB", [128, 32, 16], F32, p1)
                    S.dma("sp", "k6", lambda e: e.dma_start(out=cosB[:, :, :], in_=cosB_in.rearrange("(b p) i -> p b i", p=128)), w=["cosB"])
                    S.dma("sp", "k7", lambda e: e.dma_start(out=sinB[:, :, :], in_=sinB_in.rearrange("(b p) i -> p b i", p=128)), w=["sinB"])
                    wdv = b_wdown[jb].rearrange("(k p) n -> p k n", p=128)
                    S.dma("pool", "wB0", lambda e: e.dma_start(out=wdn[:, :, 0:400], in_=wdv[:, :, 0:400]), w=["wdn"])
                    S.dma("pool", "wB1", lambda e: e.dma_start(out=wdn[:, :, 400:800], in_=wdv[:, :, 400:800]), w=["wdn"])
                    wqv = b_wuq[jb].rearrange("(k p) n -> p k n", p=128)
                    for i in range(3):
                        S.dma("pool", "wB%d" % (i % 2), lambda e, i=i: e.dma_start(out=wuq[:, :, i * 512:(i + 1) * 512], in_=wqv[:, :, i * 512:(i + 1) * 512]),
                              w=["wuq"])
                    S.dve(lambda e: e.memset(krs[:, :], 0.0), w=["krs"])

                    def qstore(grp, nblk, u0):
                        qa = grp % 2
                        for hh in range(2):
                            S.dma("sp", "qst%d" % hh, lambda e, hh=hh: e.dma_start(
                                out=qT_d[hh * 8:(hh + 1) * 8, :, u0:u0 + nblk * 128].rearrange("h r u -> r h u"),
                                in_=QTa[0:96, qa, hh * 8:(hh + 1) * 8, 0:nblk * 128]), r=[("QTa", qa)], w=[("qT", grp)])

                    def xload(b):
                        S.dma("sp", "xld%d" % (b % 2), lambda e, b=b: e.dma_start(out=xbr[:, b % 2, :], in_=Xsrc(l, b)),
                              r=[("X", b)], w=[("xb", b % 2)])

                    def front(b):
                        j = 1 if b < 2 else 0
                        xs_ = b % 2
                        if b + 1 < NB:
                            xload(b + 1)
                        hs = b % 2
                        normT(xbr[:, xs_, :], [("xb", xs_)], Gf(0, j), SHf(0, j), lambda k, hs=hs: hT[:, hs, k, :], ("hT", hs), 0)
                        for pc, (c0, c1) in enumerate(((0, 512), (512, 800))):
                            pb = 1 + pc
                            for k in range(8):
                                S.pe(lambda e, k=k, pb=pb, c0=c0, c1=c1, hs=hs: e.matmul(
                                    banks[pb][:, 0:c1 - c0], lhsT=hT[:, hs, k, :], rhs=wdn[:, k, c0:c1],
                                    start=(k == 0), stop=(k == 7)), r=[("hT", hs), "wdn"], w=[PB(pb)])
                        need_q = (b >= 2) or with_ctx
                        rs, rn = row_rstd(banks[2][:, 0:256], [PB(2)], 256)
                        S.act(lambda e, rs=rs: e.activation(out=ckn[:, :], in_=banks[2][:, 0:256], func=AF.Copy, scale=rs),
                              r=[PB(2), rn], w=["ckn"])
                        if b >= 2:
                            blk = b - 2
                            rope_tm(banks[2][:, 256:288].rearrange("p (h i t) -> p h i t", h=1, t=2), 1, 16,
                                    cosB[:, blk, :], sinB[:, blk, :],
                                    krs[:, 64:96].rearrange("p (h i t) -> p h i t", h=1, t=2), rtmp1,
                                    [PB(2), "cosB", "sinB"], "krs", "rtmp1")
                        else:
                            S.dve(lambda e: e.tensor_copy(out=krs[:, 64:96], in_=banks[2][:, 256:288]), r=[PB(2)], w=["krs"])
                        for k in range(2):
                            S.pe(lambda e, k=k: e.transpose(out=banks[3][:, k * 128:(k + 1) * 128], in_=ckn[:, k * 128:(k + 1) * 128],
                                                           identity=identF[:, :]), r=["ckn", "identF"], w=[PB(3)])
                        S.pe(lambda e: e.transpose(out=banks[3][0:96, 256:384], in_=krs[:, :], identity=identF[:, :]),
                             r=["krs", "identF"], w=[PB(3)])
                        for k in range(2):
                            S.dve(lambda e, k=k, b=b: e.tensor_scalar(out=ckvT[:, k, b * 128:(b + 1) * 128], in0=banks[3][:, k * 128:(k + 1) * 128],
                                                                     scalar1=kvg[:, jb * 2 + k:jb * 2 + k + 1], scalar2=None, op0=ALU.mult),
                                  r=[PB(3), "params"], w=[("ckvT", b)])
                        for kk in range(2):
                            S.act(lambda e, kk=kk, b=b: e.activation(out=Kb[64:96, kk, b * 128:(b + 1) * 128], in_=banks[3][64:96, 256:384],
                                                                    func=AF.Copy), r=[PB(3)], w=[("KR", b)])
                        if not need_q:
                            return
                        cs_ = b % 2
                        rs2, rn2 = row_rstd(banks[1][:, :], [PB(1)], 512)
                        S.act(lambda e, rs2=rs2: e.activation(out=cqn[:, :], in_=banks[1][:, :], func=AF.Copy, scale=rs2),
                              r=[PB(1), rn2], w=["cqn"])
                        for k in range(4):
                            S.pe(lambda e, k=k: e.transpose(out=banks[4][:, k * 128:(k + 1) * 128], in_=cqn[:, k * 128:(k + 1) * 128],
                                                           identity=identF[:, :]), r=["cqn", "identF"], w=[PB(4)])
                        for k in range(4):
                            S.dve(lambda e, k=k, cs_=cs_: e.tensor_scalar(out=cqT[:, cs_, k, :], in0=banks[4][:, k * 128:(k + 1) * 128],
                                                                scalar1=qng[:, jb * 4 + k:jb * 4 + k + 1], scalar2=None, op0=ALU.mult),
                                  r=[PB(4), "params"], w=[("cqT", cs_)])

                    def back(b):
                        cs_ = b % 2
                        for pc in range(4):
                            pb = 5 + pc % 2
                            for k in range(4):
                                S.pe(lambda e, k=k, pb=pb, pc=pc, cs_=cs_: e.matmul(banks[pb][:, 0:384], lhsT=cqT[:, cs_, k, :],
                                                                         rhs=wuq[:, k, pc * 384:(pc + 1) * 384],
                                                                         start=(k == 0), stop=(k == 3)), r=[("cqT", cs_), "wuq"], w=[PB(pb)])
                            pv = banks[pb][:, 0:384].rearrange("p (h d) -> p h d", h=4)
                            if b >= 2:
                                S.act(lambda e, pv=pv, pc=pc: e.activation(out=qrb[:, pc * 4:(pc + 1) * 4, 0:64], in_=pv[:, :, 0:64], func=AF.Copy),
                                      r=[PB(pb)], w=[("qrb", pc, 0)])
                                S.act(lambda e, pv=pv, pc=pc: e.activation(out=qf[:, pc * 4:(pc + 1) * 4, :], in_=pv[:, :, 64:96], func=AF.Copy),
                                      r=[PB(pb)], w=[("qf", pc)])
                            else:
                                S.act(lambda e, pv=pv, pc=pc: e.activation(out=qrb[:, pc * 4:(pc + 1) * 4, :], in_=pv, func=AF.Copy),
                                      r=[PB(pb)], w=[("qrb", pc, 0), ("qrb", pc, 1)])
                        if b >= 2:
                            rope_tm(qf[:, :, :].rearrange("p h (i t) -> p h i t", t=2), 16, 16,
                                    cosB[:, b - 2, :], sinB[:, b - 2, :],
                                    qrb[:, :, 64:96].rearrange("p h (i t) -> p h i t", t=2), rtmp,
                                    [("qf", 0), ("qf", 1), ("qf", 2), ("qf", 3), "cosB", "sinB"], ("qrb", "rope"), "rtmp")
                        if b < 2:
                            grp, bi = 0, b
                        else:
                            grp, bi = 1 + (b - 2) // 4, (b - 2) % 4
                        qa = grp % 2
                        for hh in range(2):
                            pb = 7
                            pst = banks[pb][:, :].bitcast(BF16)
                            for h8 in range(8):
                                h = hh * 8 + h8
                                S.pe(lambda e, h=h, h8=h8, pst=pst: e.transpose(out=pst[0:96, h8 * 128:(h8 + 1) * 128], in_=qrb[:, h, :],
                                                                               identity=identB[:, :]),
                                     r=[("qrb", h // 4, 0), ("qrb", h // 4, 1), ("qrb", "rope"), "identB"], w=[PB(pb)])
                            S.act(lambda e, hh=hh, pst=pst, qa=qa, bi=bi: e.activation(
                                out=QTa[0:96, qa, hh * 8:(hh + 1) * 8, bi * 128:(bi + 1) * 128],
                                in_=pst[0:96, :].rearrange("p (h t) -> p h t", h=8), func=AF.Copy), r=[PB(pb)], w=[("QTa", qa)])
                        if b == 1:
                            qstore(0, 2, 0)
                        elif b >= 2 and bi == 3:
                            qstore(grp, 4, 256 + (grp - 1) * 512)

                    xload(0)
                    recF, recB = {}, {}
                    for b in range(NB):
                        S.rec()
                        front(b)
                        recF[b] = S.end_rec()
                        if (b >= 2) or with_ctx:
                            S.rec()
                            back(b)
                            recB[b] = S.end_rec()
                    S.play([recF[0]])
                    for b in range(NB):
                        S.play([recB.get(b), recF.get(b + 1)])
                    S.barrier()
                with ExitStack() as p2:
                    Oall = sb("Oall", [128, NB, D], BF16, p2)
                    with ExitStack() as p2a:
                        wuk = sb("wuk", [128, 2, D], BF16, p2a)
                        wuv = sb("wuv", [128, 2, D], BF16, p2a)
                        Qb = sb("Qb", [128, 2, U], BF16, p2a)
                        Vb = sb("Vb", [128, 2, NB, 65], BF16, p2a)
                        PTr = sb("PTr", [128, 6, 512], BF16, p2a)
                        rec = sb("rec", [128, 2, 4], F32, p2a)
                        S.dma("pool", "wB0", lambda e: e.dma_start(out=wuk[:, :, :], in_=b_wuk[jb].rearrange("(k p) n -> p k n", p=128)), w=["wuk"])
                        S.dma("pool", "wB1", lambda e: e.dma_start(out=wuv[:, :, :], in_=b_wuv[jb].rearrange("(k p) n -> p k n", p=128)), w=["wuv"])
                        S.pool(lambda e: e.memset(Vb[:, :, :, 64:65], 1.0), w=["Vbones"])
                        ptc = [0]

                        def build_kv(h):
                            hb = h % 2
                            S.dma("sp", "qld%d" % hb, lambda e, h=h, hb=hb: e.dma_start(out=Qb[0:96, hb, :], in_=qT_d[h, :, :]),
                                  r=[("qT", g_) for g_ in range(9)], w=[("Qb", hb)])
                            for kt in range(9):
                                u0 = kt * 512
                                n = min(512, U - u0)
                                pb = 1
                                for k in range(2):
                                    S.pe(lambda e, k=k, u0=u0, n=n, pb=pb, h=h: e.matmul(
                                        banks[pb][0:64, 0:n], lhsT=wuk[:, k, h * 64:(h + 1) * 64], rhs=ckvT[:, k, u0:u0 + n],
                                        start=(k == 0), stop=(k == 1)), r=["wuk"] + [("ckvT", bb) for bb in range(u0 // 128, (u0 + n) // 128)],
                                        w=[PB(pb)])
                                S.dve(lambda e, u0=u0, n=n, pb=pb, hb=hb: e.tensor_copy(out=Kb[0:64, hb, u0:u0 + n], in_=banks[pb][0:64, 0:n]),
                                      r=[PB(pb)], w=[("Kn", hb)])
                            for vg in range(5):
                                kb0 = vg * 7
                                nkb = min(7, NB - kb0)
                                pb = 3
                                for kk in range(nkb):
                                    kb = kb0 + kk
                                    for k in range(2):
                                        S.pe(lambda e, k=k, kb=kb, kk=kk, pb=pb, h=h: e.matmul(
                                            banks[pb][:, kk * 64:(kk + 1) * 64], lhsT=ckvT[:, k, kb * 128:(kb + 1) * 128],
                                            rhs=wuv[:, k, h * 64:(h + 1) * 64], start=(k == 0), stop=(k == 1)),
                                            r=["wuv", ("ckvT", kb)], w=[PB(pb)])
                                S.dve(lambda e, kb0=kb0, nkb=nkb, pb=pb, hb=hb: e.tensor_copy(
                                    out=Vb[:, hb, kb0:kb0 + nkb, 0:64], in_=banks[pb][:, 0:nkb * 64].rearrange("p (k d) -> p k d", d=64)),
                                    r=[PB(pb)], w=[("Vb", hb)])

                        qtiles = ([(0, 2, [0, 1])] if with_ctx else []) + [(2 + 4 * t, 4, list(range(NB))) for t in range(8)]

                        def attention(h):
                            hb = h % 2
                            items = []
                            for qi, (qb0, nqb, kbs) in enumerate(qtiles):
                                for ki, kb in enumerate(kbs):
                                    items.append((qi, qb0, nqb, ki, kb, len(kbs)))
                            slots = {}

                            def stA(i):
                                qi, qb0, nqb, ki, kb, nk = items[i]
                                nq = nqb * 128
                                sbk = (7, 0, 2, 4)[ptc[0] % 4]
                                pslot = ptc[0] % 6
                                ptc[0] += 1
                                slots[i] = pslot
                                S.pe(lambda e: e.matmul(
                                    banks[sbk][:, 0:nq], lhsT=Kb[0:96, hb, kb * 128:(kb + 1) * 128],
                                    rhs=Qb[0:96, hb, qb0 * 128:qb0 * 128 + nq], start=True, stop=True),
                                    r=[("Kn", hb), ("KR", kb), ("Qb", hb)], w=[PB(sbk)])
                                S.act(lambda e: e.activation(out=PTr[:, pslot, 0:nq], in_=banks[sbk][:, 0:nq], func=AF.Exp, scale=scale),
                                      r=[PB(sbk)], w=[("PTr", pslot)])

                            def stB(i):
                                qi, qb0, nqb, ki, kb, nk = items[i]
                                ob = 5 + (h * 9 + qi) % 2
                                pslot = slots[i]
                                if ki == 0:
                                    S.pe(lambda e: e.matmul(banks[ob][:, 0:nqb * 65], lhsT=zerosB[:, 0:128], rhs=zerosB[:, 0:nqb * 65],
                                                            start=True, stop=True), r=["zerosB"], w=[PB(ob)])
                                for qq in range(nqb):
                                    S.pe(lambda e, qq=qq: e.matmul(
                                        banks[ob][:, qq * 65:(qq + 1) * 65], lhsT=PTr[:, pslot, qq * 128:(qq + 1) * 128],
                                        rhs=Vb[:, hb, kb, :], start=False, stop=True, skip_group_check=True),
                                        r=[("PTr", pslot), ("Vb", hb), "Vbones"], w=[PB(ob)])
                                if ki == nk - 1:
                                    rs_ = (h * 9 + qi) % 2
                                    ov = banks[ob][:, 0:nqb * 65].rearrange("p (q d) -> p q d", d=65)
                                    S.dve(lambda e: e.reciprocal(out=rec[:, rs_, 0:nqb], in_=ov[:, :, 64]), r=[PB(ob)], w=[("rec", rs_)])
                                    S.dve(lambda e: e.tensor_tensor(
                                        out=Oall[:, qb0:qb0 + nqb, h * 64:(h + 1) * 64], in0=ov[:, :, 0:64],
                                        in1=rec[:, rs_, 0:nqb].unsqueeze(2).broadcast_to([128, nqb, 64]), op=ALU.mult),
                                        r=[PB(ob), ("rec", rs_)], w=["Oall"])

                            n_it = len(items)
                            for i in range(min(3, n_it)):
                                stA(i)
                            for i in range(n_it):
                                if i + 3 < n_it:
                                    stA(i + 3)
                                stB(i)

                        build_kv(0)
                        for h in range(16):
                            S.rec()
                            attention(h)
                            ra = S.end_rec()
                            rb = None
                            if h + 1 < 16:
                                S.rec()
                                build_kv(h + 1)
                                rb = S.end_rec()
                            S.play([ra, rb])
                        S.barrier()
                    with ExitStack() as p3:
                        wo_sb = sb("wo", [128, 8, D], BF16, p3)
                        xbr = sb("xbr", [128, 2, D], F32, p3)
                        OTt = sb("OTt", [128, 2, D], BF16, p3)
                        xnew = sb("xnew", [128, 2, D], F32, p3)
                        tl = {"OT": [OTt[:, 0, :], OTt[:, 1, :]], "xnew": [xnew[:, 0, :], xnew[:, 1, :]], "pbT": 4, "pbo": (1, 2)}
                        wov = b_wo[jb].rearrange("(k p) n -> p k n", p=128)
                        for i in range(2):
                            S.dma("pool", "wB%d" % i, lambda e, i=i: e.dma_start(out=wo_sb[:, :, i * 512:(i + 1) * 512], in_=wov[:, :, i * 512:(i + 1) * 512]),
                                  w=["wo"])
                        def xload3(b):
                            S.dma("sp", "xld%d" % (b % 2), lambda e, b=b: e.dma_start(out=xbr[:, b % 2, :], in_=Xsrc(l, b)),
                                  r=[("X", b)], w=[("xb", b % 2)])

                        xload3(b0)
                        proj_T(Oall[:, b0, :], [], tl, b0 % 2)
                        for b in range(b0, NB):
                            j = 1 if b < 2 else 0
                            xs_ = b % 2
                            if b + 1 < NB:
                                xload3(b + 1)
                                proj_T(Oall[:, b + 1, :], [], tl, (b + 1) % 2)
                            proj_MM(wo_sb, xbr[:, xs_, :], [("xb", xs_)], 0 + j, b, tl, b % 2)
                        S.barrier()

        def phase_F(l):
            with_ctx = l < DEPTH - 1
            last = l == DEPTH - 1
            S.raw("act", lambda e: e.preload_act_table(AF.Silu))
            with ExitStack() as pf:
                wout = sb("wout", [128, NCF, D], BF16, pf)
                wr = sb("wr", [128, 4, 8, 256], BF16, pf)
                h2T = sb("h2T", [128, 2, 8, 512], BF16, pf)
                halo = sb("halo", [128, 3, 8, 2], BF16, pf)
                gT = sb("gT", [128, NCF, 512], BF16, pf)
                xbr = sb("xbr", [128, 2, D], F32, pf)
                xb3 = sb("xb3", [128, 2, D], F32, pf)
                acc = sb("acc", [128, 2, 512], F32, pf)
                sl = sb("sl", [128, 2, 512], F32, pf)
                xnew = sb("xnew", [128, 2, D], F32, pf)
                yo = sb("yo", [128, 2, D], F32, pf)
                wov = f_wout[l].rearrange("(c p) n -> p c n", p=128)
                for i in range(2):
                    S.dma("pool", "wF%d" % i, lambda e, i=i: e.dma_start(out=wout[:, i * 11:(i + 1) * 11, :], in_=wov[:, i * 11:(i + 1) * 11, :]),
                          w=["wout"])
                wiv = f_win[l].rearrange("(k p) n -> p k n", p=128)
                tiles = ([(0, 2)] if with_ctx else []) + [(2 + 4 * t, 4) for t in range(8)]
                wc = [0]
                xc = [0]

                def n1(ti, bi):
                    b0_, nb_ = tiles[ti]
                    hs = ti % 2
                    lat_first = (b0_ == 2)
                    lat_last = (b0_ + nb_ == NB)
                    if bi == 0:
                        if b0_ < 2 or lat_first:
                            S.pool(lambda e: e.memset(halo[:, ti % 3, :, 0:1], 0.0), w=[("halo", ti % 3, 0)])
                        if b0_ < 2 or lat_last:
                            S.pool(lambda e: e.memset(halo[:, ti % 3, :, 1:2], 0.0), w=[("halo", ti % 3, 1)])
                    b = b0_ + bi
                    j = 1 if b < 2 else 0
                    xs_ = b % 2
                    S.dma("sp", "xld%d" % (b % 2), lambda e: e.dma_start(out=xbr[:, xs_, :], in_=Xs[b * 128:(b + 1) * 128, :]),
                          r=[("X", b)], w=[("xb", xs_)])
                    normT(xbr[:, xs_, :], [("xb", xs_)], Gf(1, j), SHf(1, j),
                          lambda k: h2T[:, hs, k, bi * 128:(bi + 1) * 128], ("h2T", hs, bi), 0)
                    if b >= 2 and bi == 0 and not lat_first:
                        S.pool(lambda e: e.tensor_copy(out=halo[:, (ti - 1) % 3, :, 1:2], in_=h2T[:, hs, :, 0:1]),
                               r=[("h2T", hs, 0)], w=[("halo", (ti - 1) % 3, 1)])
                    if b >= 2 and bi == nb_ - 1 and not lat_last:
                        S.pool(lambda e: e.tensor_copy(out=halo[:, (ti + 1) % 3, :, 0:1],
                                                       in_=h2T[:, hs, :, nb_ * 128 - 1:nb_ * 128]),
                               r=[("h2T", hs, nb_ - 1)], w=[("halo", (ti + 1) % 3, 0)])

                def s2(ti):
                    b0_, nb_ = tiles[ti]
                    n = nb_ * 128
                    hs = ti % 2
                    hres = [("h2T", hs, bi) for bi in range(nb_)]
                    for c in range(NCF):
                        ws = wc[0] % 4
                        wc[0] += 1
                        S.dma("pool", "wi%d" % ws, lambda e, c=c, ws=ws: e.dma_start(out=wr[:, ws, :, 0:128], in_=wiv[:, :, c * 128:(c + 1) * 128]),
                              w=[("wr", ws, 0)])
                        S.dma("pool", "wj%d" % ws, lambda e, c=c, ws=ws: e.dma_start(out=wr[:, ws, :, 128:256],
                                                                                   in_=wiv[:, :, DFF + c * 128:DFF + (c + 1) * 128]),
                              w=[("wr", ws, 1)])
                        pa = 1 + 2 * (c % 2)
                        pv = 2 + 2 * (c % 2)
                        hcol = (c % 2) * 2
                        for k in range(8):
                            S.pe(lambda e, k=k, ws=ws, pa=pa: e.matmul(banks[pa][:, 0:n], lhsT=wr[:, ws, k, 0:128], rhs=h2T[:, hs, k, 0:n],
                                                                     start=(k == 0), stop=(k == 7)), r=[("wr", ws, 0)] + hres, w=[PB(pa)])
                        for k in range(8):
                            S.pe(lambda e, k=k, ws=ws, hcol=hcol: e.matmul(banks[5][:, hcol:hcol + 2], lhsT=wr[:, ws, k, 0:128],
                                                                         rhs=halo[:, ti % 3, k, :], start=(k == 0), stop=(k == 7)),
                                 r=[("wr", ws, 0), ("halo", ti % 3, 0), ("halo", ti % 3, 1)], w=[PB(5)])
                        for k in range(8):
                            S.pe(lambda e, k=k, ws=ws, pv=pv: e.matmul(banks[pv][:, 0:n], lhsT=wr[:, ws, k, 128:256], rhs=h2T[:, hs, k, 0:n],
                                                                     start=(k == 0), stop=(k == 7)), r=[("wr", ws, 1)] + hres, w=[PB(pv)])
                        a_ = c % 2
                        w0 = cw[:, (l * 3 + 0) * NCF + c:(l * 3 + 0) * NCF + c + 1]
                        w1 = cw[:, (l * 3 + 1) * NCF + c:(l * 3 + 1) * NCF + c + 1]
                        w2 = cw[:, (l * 3 + 2) * NCF + c:(l * 3 + 2) * NCF + c + 1]
                        bb = cb[:, l * NCF + c:l * NCF + c + 1]
                        ar = ("acc", a_)
                        S.act(lambda e, pa=pa, a_=a_, w1=w1, bb=bb: e.activation(out=acc[:, a_, 0:n], in_=banks[pa][:, 0:n], func=AF.Identity,
                                                                                scale=w1, bias=bb), r=[PB(pa), "params"], w=[ar])
                        S.dve(lambda e, pa=pa, a_=a_, w0=w0: e.scalar_tensor_tensor(out=acc[:, a_, 1:n], in0=banks[pa][:, 0:n - 1], scalar=w0,
                                                                                   in1=acc[:, a_, 1:n], op0=ALU.mult, op1=ALU.add),
                              r=[PB(pa), ar, "params"], w=[ar])
                        S.dve(lambda e, pa=pa, a_=a_, w2=w2: e.scalar_tensor_tensor(out=acc[:, a_, 0:n - 1], in0=banks[pa][:, 1:n], scalar=w2,
                                                                                   in1=acc[:, a_, 0:n - 1], op0=ALU.mult, op1=ALU.add),
                              r=[PB(pa), ar], w=[ar])
                        S.dve(lambda e, a_=a_, w0=w0, hcol=hcol: e.scalar_tensor_tensor(out=acc[:, a_, 0:1], in0=banks[5][:, hcol:hcol + 1], scalar=w0,
                                                                                       in1=acc[:, a_, 0:1], op0=ALU.mult, op1=ALU.add),
                              r=[PB(5), ar], w=[ar])
                        S.dve(lambda e, a_=a_, w2=w2, hcol=hcol: e.scalar_tensor_tensor(out=acc[:, a_, n - 1:n], in0=banks[5][:, hcol + 1:hcol + 2],
                                                                                       scalar=w2, in1=acc[:, a_, n - 1:n], op0=ALU.mult, op1=ALU.add),
                              r=[PB(5), ar], w=[ar])
                        S.act(lambda e, a_=a_: e.activation(out=sl[:, a_, 0:n], in_=acc[:, a_, 0:n], func=AF.Silu), r=[ar], w=[("sl", a_)])
                        S.dve(lambda e, a_=a_, pv=pv, c=c: e.tensor_tensor(out=gT[:, c, 0:n], in0=banks[pv][:, 0:n], in1=sl[:, a_, 0:n], op=ALU.mult),
                              r=[PB(pv), ("sl", a_)], w=[("gT", c)])

                def s3(ti):
                    b0_, nb_ = tiles[ti]
                    for bi in range(nb_):
                        b = b0_ + bi
                        j = 1 if b < 2 else 0
                        xs_ = b % 2
                        xo = xc[0] % 2
                        xc[0] += 1
                        S.dma("sp", "xl3%d" % xs_, lambda e, b=b, xs_=xs_: e.dma_start(out=xb3[:, xs_, :], in_=Xs[b * 128:(b + 1) * 128, :]),
                              r=[("X", b)], w=[("xb3", xs_)])
                        for hlf in range(2):
                            pb = 6 + hlf
                            for c in range(NCF):
                                S.pe(lambda e, c=c, hlf=hlf, pb=pb, bi=bi: e.matmul(banks[pb][:, :], lhsT=gT[:, c, bi * 128:(bi + 1) * 128],
                                                                                   rhs=wout[:, c, hlf * 512:(hlf + 1) * 512],
                                                                                   start=(c == 0), stop=(c == NCF - 1)),
                                     r=[("gT", c), "wout"], w=[PB(pb)])
                            S.dve(lambda e, hlf=hlf, pb=pb, xo=xo, j=j: e.tensor_tensor(out=xnew[:, xo, hlf * 512:(hlf + 1) * 512], in0=banks[pb][:, :],
                                                                                      in1=gateB[:, 2 + j, hlf * 512:(hlf + 1) * 512], op=ALU.mult),
                                  r=[PB(pb), "gates"], w=[("xnew", xo, hlf)])
                            S.pool(lambda e, hlf=hlf, xo=xo, xs_=xs_: e.tensor_tensor(out=xnew[:, xo, hlf * 512:(hlf + 1) * 512],
                                                                                    in0=xnew[:, xo, hlf * 512:(hlf + 1) * 512],
                                                                                    in1=xb3[:, xs_, hlf * 512:(hlf + 1) * 512], op=ALU.add),
                                   r=[("xnew", xo, hlf), ("xb3", xs_)], w=[("xnew", xo, hlf)])
                        xres = [("xnew", xo, 0), ("xnew", xo, 1)]
                        if not last:
                            S.dma("sp", "xst", lambda e, b=b, xo=xo: e.dma_start(out=Xs[b * 128:(b + 1) * 128, :], in_=xnew[:, xo, :]),
                                  r=xres, w=[("X", b)])
                        else:
                            i = new_stat()
                            rn = ("st", i)
                            S.dve(lambda e, xo=xo, i=i: e.scalar_tensor_tensor(out=yo[:, xo, :], in0=xnew[:, xo, :], scalar=1.0, in1=xnew[:, xo, :],
                                                                              op0=ALU.mult, op1=ALU.mult, accum_out=ssr[:, i:i + 1]),
                                  r=xres, w=[rn, ("yo", xo)])
                            S.dve(lambda e, i=i: e.tensor_scalar(out=msr[:, i:i + 1], in0=ssr[:, i:i + 1], scalar1=1.0 / D, scalar2=EPS,
                                                                op0=ALU.mult, op1=ALU.add), r=[rn], w=[rn])
                            S.pool(lambda e, i=i: e.tensor_tensor(out=rstdr[:, i:i + 1], in0=msr[:, i:i + 1], in1=neghalf[:, :], op=ALU.pow),
                                   r=[rn, "neghalf"], w=[rn])
                            S.dve(lambda e, xo=xo, i=i: e.scalar_tensor_tensor(out=yo[:, xo, :], in0=xnew[:, xo, :], scalar=rstdr[:, i:i + 1],
                                                                              in1=finalgB[:, :], op0=ALU.mult, op1=ALU.mult),
                                  r=xres + [rn, "finalgB", ("yo", xo)], w=[("yo", xo)])
                            S.dma("sp", "ost", lambda e, b=b, xo=xo: e.dma_start(out=out[(b - 2) * 128:(b - 1) * 128, :], in_=yo[:, xo, :]),
                                  r=[("yo", xo)], w=[("out", b)])

                nt = len(tiles)

                def s1(ti):
                    for bi in range(tiles[ti][1]):
                        n1(ti, bi)

                s1(0)
                for ti in range(nt):
                    if ti + 1 < nt:
                        s1(ti + 1)
                    s2(ti)
                    s3(ti)
                S.barrier()

        phases = []
        for l in range(DEPTH):
            phases.append(("M", l))
            phases.append(("A" if l % 2 == 0 else "B", l))
            phases.append(("F", l))
        nrun = 0
        for kind, l in phases:
            if kind == "M":
                if n_phases is not None and nrun >= n_phases:
                    break
                modulation(l)
                continue
            if n_phases is not None and nrun >= n_phases:
                break
            if kind == "A":
                phase_A(l)
            elif kind == "B":
                phase_B(l)
            else:
                phase_F(l)
            nrun += 1
        S.finish()
    return nc


def _consts():
    def ang(rot_dim):
        row = np.repeat(np.arange(T // GRID_W), GRID_W).astype(np.float32)
        col = np.tile(np.arange(GRID_W), T // GRID_W).astype(np.float32)
        nf = rot_dim // 4
        inv = (np.float32(10000.0) ** (-np.arange(nf, dtype=np.float32) / np.float32(nf))).astype(np.float32)
        return np.concatenate([row[:, None] * inv, col[:, None] * inv], axis=-1).astype(np.float32)
    aA, aB = ang(64), ang(32)
    k = np.arange(128)[:, None]
    q = np.arange(128)[None, :]
    return {
        "ident": np.eye(128, dtype=np.float32),
        "maskp": (k >= q).astype(np.float32),
        "maskn": (k <= q).astype(np.float32),
        "cosA": np.cos(aA).astype(np.float32), "sinA": np.sin(aA).astype(np.float32),
        "cosB": np.cos(aB).astype(np.float32), "sinB": np.sin(aB).astype(np.float32),
    }


_WNAMES = ["mod_w", "mod_b", "norm1_g", "norm2_g", "a_wqkv", "a_wo", "a_sink", "b_wdown", "b_qnorm_g", "b_wuq",
           "b_kvnorm_g", "b_wuk", "b_wuv", "b_wo", "f_win", "f_conv_w", "f_conv_b", "f_wout"]


def make_in_maps(inputs, cores):
    consts = _consts()
    shared = {n: np.ascontiguousarray(np.asarray(inputs[n], dtype=np.float32)) for n in _WNAMES}
    shared["final_g"] = np.ascontiguousarray(np.asarray(inputs["final_g"], dtype=np.float32).reshape(1, D))
    shared.update(consts)
    x = np.asarray(inputs["x"], dtype=np.float32)
    c = np.asarray(inputs["c"], dtype=np.float32)
    ctx = np.asarray(inputs["ctx"], dtype=np.float32)
    c_ctx = np.asarray(inputs["c_ctx"], dtype=np.float32)
    maps = []
    for b in cores:
        m = dict(shared)
        m["x"] = np.ascontiguousarray(x[b])
        m["ctx"] = np.ascontiguousarray(ctx[b])
        m["cc"] = np.ascontiguousarray(np.concatenate([c[b].reshape(8, 128), c_ctx.reshape(8, 128)], axis=0))
        maps.append(m)
    return maps


def kernel(**inputs):
    nc = build()
    maps = make_in_maps(inputs, list(range(8)))
    res = run_bass_kernel_spmd(nc, maps, core_ids=list(range(8)))
    return np.stack([np.asarray(r["out"], dtype=np.float32) for r in res.results], axis=0)
```

```python
import numpy as np
from contextlib import ExitStack
import concourse.bass as bass
import concourse.mybir as mybir
from concourse.bass_utils import run_bass_kernel_spmd

F32 = mybir.dt.float32
BF16 = mybir.dt.bfloat16
AF = mybir.ActivationFunctionType
ALU = mybir.AluOpType

D = 1024
T = 4096
L = 256
U = T + L
NB = U // 128
DFF = 2816
NCF = DFF // 128
DEPTH = 4
EPS = 1e-6
GRID_W = 64
ENGS = ("pe", "dve", "act", "pool", "sp")
_DBG = {}
ENGATTR = {"pe": "tensor", "dve": "vector", "act": "scalar", "pool": "gpsimd", "sp": "sync"}


class Sched:
    def __init__(self, nc, block, stack):
        self.nc, self.block, self.stack = nc, block, stack
        self.sem, self.cnt = {}, {}
        self.seen = {e: {} for e in ENGS}
        self.res = {}
        for e in ENGS:
            self.sem[e] = stack.enter_context(nc.semaphore("s_" + e))
            self.cnt[e] = 0

    def chan(self, name):
        if name not in self.sem:
            self.sem[name] = self.stack.enter_context(self.nc.semaphore("c_" + name))
            self.cnt[name] = 0
        return name

    def _deps(self, eng, reads, writes):
        deps = {}

        def add(p, c, raw):
            if p == eng and eng == "pe":
                return
            if deps.get(p, 0) < c:
                deps[p] = c

        for r in reads:
            st = self.res.get(r)
            if st and st["w"]:
                add(st["w"][0], st["w"][1], True)
        for w in writes:
            st = self.res.get(w)
            if st:
                if st["w"]:
                    add(st["w"][0], st["w"][1], False)
                for p, c in st["r"].items():
                    add(p, c, False)
        return deps

    def _commit(self, prod, val, reads, writes):
        for r in reads:
            st = self.res.setdefault(r, {"w": None, "r": {}})
            if st["r"].get(prod, 0) < val:
                st["r"][prod] = val
        for w in writes:
            self.res[w] = {"w": (prod, val), "r": {}}

    def _waits(self, eng, deps):
        waits = []
        for p, c in deps.items():
            if self.seen[eng].get(p, 0) < c:
                waits.append((p, c))
                self.seen[eng][p] = c
        return waits

    @staticmethod
    def _excl(r, w):
        ps = [x for x in r if isinstance(x, tuple) and x and x[0] == "ps"]
        if not ps:
            return list(r), list(w)
        return [x for x in r if x not in ps], list(w) + [x for x in ps if x not in w]

    def rec(self):
        self._rec = []

    def end_rec(self):
        lst, self._rec = self._rec, None
        return lst

    def play(self, lists):
        lists = [l for l in lists if l]
        idx = [0] * len(lists)
        total = sum(len(l) for l in lists)
        for _ in range(total):
            best, bi = None, -1
            for i, l in enumerate(lists):
                if idx[i] < len(l):
                    frac = idx[i] / len(l)
                    if best is None or frac < best:
                        best, bi = frac, i
            kind, a = lists[bi][idx[bi]]
            idx[bi] += 1
            if kind == "op":
                self.op(*a)
            else:
                self.dma(*a)

    def op(self, eng, fn, r=(), w=()):
        if getattr(self, "_rec", None) is not None:
            self._rec.append(("op", (eng, fn, tuple(r), tuple(w))))
            return
        r, w = self._excl(r, w)
        deps = self._deps(eng, r, w)
        waits = self._waits(eng, deps)
        self.cnt[eng] += 1
        val = self.cnt[eng]
        sem = self.sem

        def body(e):
            for p, c in waits:
                e.wait_ge(sem[p], c)
            fn(e).then_inc(sem[eng], 1)

        getattr(self.block, ENGATTR[eng])(body)
        self._commit(eng, val, r, w)

    def pe(self, fn, r=(), w=()):
        self.op("pe", fn, r, w)

    def dve(self, fn, r=(), w=()):
        self.op("dve", fn, r, w)

    def act(self, fn, r=(), w=()):
        self.op("act", fn, r, w)

    def pool(self, fn, r=(), w=()):
        self.op("pool", fn, r, w)

    def dma(self, q, chan, fn, r=(), w=()):
        if getattr(self, "_rec", None) is not None:
            self._rec.append(("dma", (q, chan, fn, tuple(r), tuple(w))))
            return
        self.chan(chan)
        r, w = self._excl(r, w)
        deps = self._deps(q, r, w)
        if self.cnt[chan] > 0 and deps.get(chan, 0) < self.cnt[chan]:
            deps[chan] = self.cnt[chan]
        waits = self._waits(q, deps)
        self.cnt[chan] += 16
        val = self.cnt[chan]
        sem = self.sem

        def body(e):
            for p, c in waits:
                e.wait_ge(sem[p], c)
            fn(e).then_inc(sem[chan], 16)

        getattr(self.block, ENGATTR[q])(body)
        self._commit(chan, val, r, w)

    def raw(self, eng, fn):
        getattr(self.block, ENGATTR[eng])(lambda e: fn(e))

    def barrier(self):
        snap = dict(self.cnt)
        sem = self.sem
        for eng in ENGS:
            waits = []
            for p, c in snap.items():
                if c > 0 and self.seen[eng].get(p, 0) < c:
                    if p == eng and eng == "pe":
                        continue
                    waits.append((p, c))
                    self.seen[eng][p] = c
            if waits:
                def body(e, waits=waits):
                    for p, c in waits:
                        e.wait_ge(sem[p], c)
                getattr(self.block, ENGATTR[eng])(body)
        self.res = {}

    def finish(self):
        snap = dict(self.cnt)
        sem = self.sem

        def body(e):
            for p, c in snap.items():
                if c > 0:
                    e.wait_ge(sem[p], c)
        self.block.sync(body)


def build(n_phases=None, debug=False):
    nc = bass.Bass("TRN2", target_bir_lowering=False)

    def din(name, shape, dt=F32):
        return nc.dram_tensor(name, list(shape), dt, kind="ExternalInput").ap()

    x_in = din("x", [T, D])
    ctx_in = din("ctx", [L, D])
    cc_in = din("cc", [16, 128])
    mod_w = din("mod_w", [DEPTH, D, 6 * D])
    mod_b = din("mod_b", [DEPTH, 6 * D])
    norm1_g = din("norm1_g", [DEPTH, D])
    norm2_g = din("norm2_g", [DEPTH, D])
    a_wqkv = din("a_wqkv", [2, D, 1536])
    a_wo = din("a_wo", [2, D, D])
    a_sink = din("a_sink", [2, 16])
    b_wdown = din("b_wdown", [2, D, 800])
    b_qnorm_g = din("b_qnorm_g", [2, 512])
    b_wuq = din("b_wuq", [2, 512, 1536])
    b_kvnorm_g = din("b_kvnorm_g", [2, 256])
    b_wuk = din("b_wuk", [2, 256, 1024])
    b_wuv = din("b_wuv", [2, 256, 1024])
    b_wo = din("b_wo", [2, D, D])
    f_win = din("f_win", [DEPTH, D, 2 * DFF])
    f_conv_w = din("f_conv_w", [DEPTH, 3, DFF])
    f_conv_b = din("f_conv_b", [DEPTH, DFF])
    f_wout = din("f_wout", [DEPTH, DFF, D])
    final_g = din("final_g", [1, D])
    ident_in = din("ident", [128, 128])
    maskp_in = din("maskp", [128, 128])
    maskn_in = din("maskn", [128, 128])
    cosA_in = din("cosA", [T, 32])
    sinA_in = din("sinA", [T, 32])
    cosB_in = din("cosB", [T, 16])
    sinB_in = din("sinB", [T, 16])

    out = nc.dram_tensor("out", [T, D], F32, kind="ExternalOutput").ap()
    if debug:
        Xs = nc.dram_tensor("xs", [U, D], F32, kind="ExternalOutput").ap()
    else:
        Xs = nc.dram_tensor("xs", [U, D], F32).ap()
    qT_d = nc.dram_tensor("qT", [16, 96, U], BF16).ap()

    stack = ExitStack()
    with stack:
        uniq = [0]

        def sb(name, shape, dt=F32, st=stack):
            uniq[0] += 1
            return st.enter_context(nc.sbuf_tensor("%s_%d" % (name, uniq[0]), list(shape), dt))

        banks = [stack.enter_context(nc.psum_tensor("bank%d" % i, [128, 512], F32)) for i in range(8)]

        def PB(i):
            return ("ps", i)

        identF = sb("identF", [128, 128])
        identB = sb("identB", [128, 128], BF16)
        zerosB = sb("zerosB", [128, 512], BF16)
        maskP = sb("maskP", [128, 128], BF16)
        maskN = sb("maskN", [128, 128], BF16)
        n1g = sb("n1g", [128, 32])
        n2g = sb("n2g", [128, 32])
        modbF = sb("modbF", [128, DEPTH * 48])
        cw = sb("cw", [128, DEPTH * 3 * NCF])
        cb = sb("cb", [128, DEPTH * NCF])
        qng = sb("qng", [128, 8])
        kvg = sb("kvg", [128, 4])
        cF = sb("cF", [128, 16])
        finalgB = sb("finalgB", [128, D])
        sinkB = sb("sinkB", [128, 32])
        esink = sb("esink", [128, 16])
        neghalf = sb("neghalf", [128, 1])
        gateB = sb("gateB", [128, 4, D])
        modF = sb("modF", [128, 48, 2])
        Gs = sb("Gs", [128, 2, 8, 2])
        scT = sb("scT", [128, 8, 2], BF16)
        screp = sb("screp", [128, 8, 2, 128], BF16)
        stage = sb("stage", [128, 128])
        ssr = sb("ssr", [128, 8])
        msr = sb("msr", [128, 8])
        rstdr = sb("rstdr", [128, 8])
        junk = sb("junk", [128, D], BF16)
        xn = sb("xn", [128, 2, D])

        block = stack.enter_context(nc.Block())
        S = Sched(nc, block, stack)
        ctr = {"st": 0, "xn": 0}

        S.dma("sp", "k0", lambda e: e.dma_start(out=identF[:, :], in_=ident_in[:, :]), w=["identF"])
        S.dma("pool", "k1", lambda e: e.dma_start(out=identB[:, :], in_=ident_in[:, :]), w=["identB"])
        S.dma("pool", "k2", lambda e: e.dma_start(out=maskP[:, :], in_=maskp_in[:, :]), w=["maskP"])
        S.dma("pool", "k3", lambda e: e.dma_start(out=maskN[:, :], in_=maskn_in[:, :]), w=["maskN"])
        S.dve(lambda e: e.memset(zerosB[:, :], 0.0), w=["zerosB"])
        S.dve(lambda e: e.memset(neghalf[:, :], -0.5), w=["neghalf"])
        S.dma("sp", "k4", lambda e: e.dma_start(out=finalgB[:, :], in_=final_g[0:1, :].broadcast_to([128, D])), w=["finalgB"])
        S.dma("sp", "k5", lambda e: e.dma_start(out=sinkB[:, :], in_=a_sink.rearrange("a h -> (a h)").unsqueeze(0).broadcast_to([128, 32])), w=["sinkB"])

        def featmajor(dst, src2d, n):
            S.dma("sp", "k5", lambda e: e.dma_start(out=stage[0:n, :], in_=src2d), w=["stage"])
            S.pe(lambda e: e.transpose(out=banks[0][:, 0:n], in_=stage[0:n, :], identity=identF[0:n, 0:n]),
                 r=["stage", "identF"], w=[PB(0)])
            S.dve(lambda e: e.tensor_copy(out=dst, in_=banks[0][:, 0:n]), r=[PB(0)], w=["params"])

        featmajor(n1g[:, :], norm1_g.rearrange("l (k p) -> (l k) p", p=128), 32)
        featmajor(n2g[:, :], norm2_g.rearrange("l (k p) -> (l k) p", p=128), 32)
        mb2 = mod_b.rearrange("l (k p) -> (l k) p", p=128)
        featmajor(modbF[:, 0:128], mb2[0:128, :], 128)
        featmajor(modbF[:, 128:192], mb2[128:192, :], 64)
        cw2 = f_conv_w.rearrange("l j (k p) -> (l j k) p", p=128)
        featmajor(cw[:, 0:128], cw2[0:128, :], 128)
        featmajor(cw[:, 128:256], cw2[128:256, :], 128)
        featmajor(cw[:, 256:264], cw2[256:264, :], 8)
        featmajor(cb[:, :], f_conv_b.rearrange("l (k p) -> (l k) p", p=128), 88)
        featmajor(qng[:, :], b_qnorm_g.rearrange("l (k p) -> (l k) p", p=128), 8)
        featmajor(kvg[:, :], b_kvnorm_g.rearrange("l (k p) -> (l k) p", p=128), 4)
        featmajor(cF[:, :], cc_in[:, :], 16)
        S.raw("act", lambda e: e.preload_act_table(AF.Silu))
        S.act(lambda e: e.activation(out=scT[:, :, :].rearrange("p k j -> p j k"),
                                     in_=cF[:, :].rearrange("p (j k) -> p j k", j=2), func=AF.Silu),
              r=["params"], w=["scT"])
        for j in range(2):
            S.dve(lambda e, j=j: e.tensor_copy(out=screp[:, :, j, :], in_=scT[:, :, j:j + 1].broadcast_to([128, 8, 128])),
                  r=["scT"], w=["screp"])

        def rstd_of(ss_ap, n_feat):
            i = ctr["st"] % 8
            return i

        def new_stat():
            i = ctr["st"] % 8
            ctr["st"] += 1
            return i

        def row_rstd(src, src_res, n_feat):
            i = new_stat()
            rn = ("st", i)
            S.act(lambda e: e.activation(out=junk[:, 0:n_feat], in_=src, func=AF.Square, accum_out=ssr[:, i:i + 1]),
                  r=src_res, w=[rn])
            S.dve(lambda e: e.tensor_scalar(out=msr[:, i:i + 1], in0=ssr[:, i:i + 1], scalar1=1.0 / n_feat, scalar2=EPS,
                                            op0=ALU.mult, op1=ALU.add), r=[rn], w=[rn])
            S.pool(lambda e: e.tensor_tensor(out=rstdr[:, i:i + 1], in0=msr[:, i:i + 1], in1=neghalf[:, :], op=ALU.pow),
                   r=[rn, "neghalf"], w=[rn])
            return rstdr[:, i:i + 1], rn

        def normT(xb, xb_res, G, SH, dst_fn, dst_res, psb, psb2=None, act_half=False):
            rs, rn = row_rstd(xb, xb_res, D)
            s = ctr["xn"] % 2
            ctr["xn"] += 1
            S.act(lambda e: e.activation(out=xn[:, s, :], in_=xb, func=AF.Copy, scale=rs), r=list(xb_res) + [rn], w=[("xn", s)])
            def tr(hlf, pbk):
                for kk in range(4):
                    k = hlf * 4 + kk
                    S.pe(lambda e, k=k, kk=kk: e.transpose(out=banks[pbk][:, kk * 128:(kk + 1) * 128],
                                                           in_=xn[:, s, k * 128:(k + 1) * 128], identity=identF[:, :]),
                         r=[("xn", s), "identF"], w=[PB(pbk)])

            def ev(hlf, pbk):
                for kk in range(4):
                    k = hlf * 4 + kk
                    if act_half and hlf == 1:
                        S.act(lambda e, k=k, kk=kk: e.activation(out=dst_fn(k), in_=banks[pbk][:, kk * 128:(kk + 1) * 128],
                                                                 func=AF.Identity, scale=G(k), bias=SH(k)),
                              r=[PB(pbk), "mods"], w=[dst_res])
                    else:
                        S.dve(lambda e, k=k, kk=kk: e.tensor_scalar(out=dst_fn(k), in0=banks[pbk][:, kk * 128:(kk + 1) * 128],
                                                                    scalar1=G(k), scalar2=SH(k), op0=ALU.mult, op1=ALU.add),
                              r=[PB(pbk), "mods"], w=[dst_res])

            if psb2 is None:
                tr(0, psb)
                ev(0, psb)
                tr(1, psb)
                ev(1, psb)
            else:
                tr(0, psb)
                tr(1, psb2)
                ev(0, psb)
                ev(1, psb2)

        def Xsrc(l, b):
            if l == 0:
                return ctx_in[b * 128:(b + 1) * 128, :] if b < 2 else x_in[(b - 2) * 128:(b - 1) * 128, :]
            return Xs[b * 128:(b + 1) * 128, :]

        def rope_tm(ps_ap, H, npair, cs, sn, out_ap, tmp, res_in, res_out, tmp_res):
            csb = cs.unsqueeze(1).broadcast_to([128, H, npair])
            snb = sn.unsqueeze(1).broadcast_to([128, H, npair])
            ev, od = ps_ap[:, :, :, 0], ps_ap[:, :, :, 1]
            t0 = tmp[:, 0, 0:H * npair].rearrange("p (h i) -> p h i", h=H)
            t1 = tmp[:, 1, 0:H * npair].rearrange("p (h i) -> p h i", h=H)
            S.dve(lambda e: e.tensor_tensor(out=t0, in0=ev, in1=csb, op=ALU.mult), r=res_in, w=[tmp_res + "0"])
            S.dve(lambda e: e.tensor_tensor(out=t1, in0=od, in1=snb, op=ALU.mult), r=res_in, w=[tmp_res + "1"])
            S.dve(lambda e: e.tensor_tensor(out=out_ap[:, :, :, 0], in0=t0, in1=t1, op=ALU.subtract),
                  r=[tmp_res + "0", tmp_res + "1"], w=[res_out])
            S.dve(lambda e: e.tensor_tensor(out=t0, in0=ev, in1=snb, op=ALU.mult), r=res_in, w=[tmp_res + "0"])
            S.dve(lambda e: e.tensor_tensor(out=t1, in0=od, in1=csb, op=ALU.mult), r=res_in, w=[tmp_res + "1"])
            S.dve(lambda e: e.tensor_tensor(out=out_ap[:, :, :, 1], in0=t0, in1=t1, op=ALU.add),
                  r=[tmp_res + "0", tmp_res + "1"], w=[res_out])

        def modulation(l):
            with ExitStack() as ms:
                mw = [sb("mw%d" % i, [128, 8, D], BF16, ms) for i in range(2)]
                modbB = sb("modbB", [128, 2, D], F32, ms)
                mwv = mod_w[l].rearrange("(k p) n -> p k n", p=128)
                psF = banks[7]
                for m in range(6):
                    buf = mw[m % 2]
                    S.dma("pool", "mw%d" % (m % 2),
                          lambda e, m=m, buf=buf: e.dma_start(out=buf[:, :, :], in_=mwv[:, :, m * D:(m + 1) * D]),
                          w=[("mw", m % 2)])
                    for js in range(8):
                        col = (m * 8 + js) * 2
                        for k in range(8):
                            S.pe(lambda e, js=js, k=k, col=col, buf=buf: e.matmul(
                                psF[:, col:col + 2], lhsT=buf[:, k, js * 128:(js + 1) * 128], rhs=scT[:, k, :],
                                start=(k == 0), stop=(k == 7)), r=[("mw", m % 2), "scT"], w=[PB(7)])
                    if m in (2, 5):
                        gi = 0 if m == 2 else 2
                        mi = 0 if m == 2 else 1
                        S.dma("sp", "k6", lambda e, m=m, mi=mi: e.dma_start(
                            out=modbB[:, mi, :], in_=mod_b[l:l + 1, m * D:(m + 1) * D].broadcast_to([128, D])), w=[("modbB", mi)])
                        for j in range(2):
                            for hlf in range(2):
                                pb = 5 + hlf
                                for k in range(8):
                                    S.pe(lambda e, j=j, hlf=hlf, k=k, pb=pb, buf=buf: e.matmul(
                                        banks[pb][:, :], lhsT=screp[:, k, j, :], rhs=buf[:, k, hlf * 512:(hlf + 1) * 512],
                                        start=(k == 0), stop=(k == 7)), r=[("mw", m % 2), "screp"], w=[PB(pb)])
                                S.dve(lambda e, j=j, hlf=hlf, pb=pb, gi=gi, mi=mi: e.tensor_tensor(
                                    out=gateB[:, gi + j, hlf * 512:(hlf + 1) * 512], in0=banks[pb][:, :],
                                    in1=modbB[:, mi, hlf * 512:(hlf + 1) * 512], op=ALU.add),
                                    r=[PB(pb), ("modbB", mi)], w=["gates"])
                S.dve(lambda e: e.tensor_tensor(out=modF[:, :, :], in0=psF[:, 0:96].rearrange("p (f j) -> p f j", j=2),
                                                in1=modbF[:, l * 48:(l + 1) * 48].unsqueeze(2).broadcast_to([128, 48, 2]),
                                                op=ALU.add), r=[PB(7), "params"], w=["mods"])
                for ni, (gt, sc0) in enumerate(((n1g, 8), (n2g, 32))):
                    S.dve(lambda e, ni=ni, sc0=sc0: e.tensor_scalar(out=Gs[:, ni, :, :], in0=modF[:, sc0:sc0 + 8, :], scalar1=1.0,
                                                                   scalar2=None, op0=ALU.add), r=["mods"], w=["mods"])
                    S.dve(lambda e, ni=ni, gt=gt: e.tensor_tensor(
                        out=Gs[:, ni, :, :], in0=Gs[:, ni, :, :],
                        in1=gt[:, l * 8:(l + 1) * 8].unsqueeze(2).broadcast_to([128, 8, 2]), op=ALU.mult),
                        r=["mods", "params"], w=["mods"])
                S.barrier()

        def Gf(ni, j):
            return lambda k: Gs[:, ni, k, j:j + 1]

        def SHf(ni, j):
            base = 0 if ni == 0 else 24
            return lambda k: modF[:, base + k, j:j + 1]

        def proj_T(O_ap, O_res, tl, sl=0):
            OTt = tl["OT"][sl]
            pbT = tl["pbT"]
            psT = banks[pbT][:, :].bitcast(BF16)
            for k in range(8):
                S.pe(lambda e, k=k: e.transpose(out=psT[:, k * 128:(k + 1) * 128], in_=O_ap[:, k * 128:(k + 1) * 128],
                                               identity=identB[:, :]), r=list(O_res) + ["identB"], w=[PB(pbT)])
            S.act(lambda e: e.activation(out=OTt[:, :], in_=psT[:, :], func=AF.Copy), r=[PB(pbT)], w=[("OT", sl)])

        def proj_MM(wo_sb, xb_ap, xb_res, gate_idx, b, tl, sl=0):
            OTt, xnew = tl["OT"][sl], tl["xnew"][sl]
            for hlf in range(2):
                pb = tl["pbo"][hlf]
                for k in range(8):
                    S.pe(lambda e, k=k, hlf=hlf, pb=pb: e.matmul(banks[pb][:, :], lhsT=OTt[:, k * 128:(k + 1) * 128],
                                                               rhs=wo_sb[:, k, hlf * 512:(hlf + 1) * 512],
                                                               start=(k == 0), stop=(k == 7)), r=[("OT", sl), "wo"], w=[PB(pb)])
                S.dve(lambda e, hlf=hlf, pb=pb: e.tensor_tensor(out=xnew[:, hlf * 512:(hlf + 1) * 512], in0=banks[pb][:, :],
                                                              in1=gateB[:, gate_idx, hlf * 512:(hlf + 1) * 512], op=ALU.mult),
                      r=[PB(pb), "gates"], w=[("xnew", sl, hlf)])
                S.pool(lambda e, hlf=hlf: e.tensor_tensor(out=xnew[:, hlf * 512:(hlf + 1) * 512], in0=xnew[:, hlf * 512:(hlf + 1) * 512],
                                                         in1=xb_ap[:, hlf * 512:(hlf + 1) * 512], op=ALU.add),
                       r=[("xnew", sl, hlf)] + list(xb_res), w=[("xnew", sl, hlf)])
            S.dma("sp", "xst%d" % sl, lambda e: e.dma_start(out=Xs[b * 128:(b + 1) * 128, :], in_=xnew[:, :]),
                  r=[("xnew", sl, 0), ("xnew", sl, 1)], w=[("X", b)])

        def proj_residual(O_ap, O_res, wo_sb, xb_ap, xb_res, gate_idx, b, tl):
            proj_T(O_ap, O_res, tl, 0)
            proj_MM(wo_sb, xb_ap, xb_res, gate_idx, b, tl, 0)

        def phase_A(l):
            ja = l // 2
            with_ctx = l < DEPTH - 1
            S.raw("act", lambda e: e.preload_act_table(AF.Exp))
            with ExitStack() as ps_:
                wqkv = sb("wqkv", [128, 8, 1536], BF16, ps_)
                wo_sb = sb("wo", [128, 8, D], BF16, ps_)
                KT = sb("KT", [128, 4, U], BF16, ps_)
                Vaug = sb("Vaug", [128, NB, 4, 65], BF16, ps_)
                xbr = sb("xbr", [128, 4, D], F32, ps_)
                cosA = sb("cosA", [128, 32, 32], F32, ps_)
                sinA = sb("sinA", [128, 32, 32], F32, ps_)
                S.dma("sp", "k6", lambda e: e.dma_start(out=cosA[:, :, :], in_=cosA_in.rearrange("(b p) i -> p b i", p=128)), w=["cosA"])
                S.dma("sp", "k7", lambda e: e.dma_start(out=sinA[:, :, :], in_=sinA_in.rearrange("(b p) i -> p b i", p=128)), w=["sinA"])
                hT = sb("hT", [128, 2, 8, 128], BF16, ps_)
                qr = sb("qr", [128, 16, 32, 2], BF16, ps_)
                krd = sb("krd", [128, 4, 2, 64], BF16, ps_)
                rtmp = sb("rtmp", [128, 2, 256], F32, ps_)
                QT = sb("QT", [128, 3, 16, 128], BF16, ps_)
                PT = sb("PT", [128, 2, 5, 4, 128], BF16, ps_)
                Osb = sb("Osb", [128, D], BF16, ps_)
                den = sb("den", [128, 2, 4], F32, ps_)
                OTt = sb("OTt", [128, D], BF16, ps_)
                xnew = sb("xnew", [128, D], F32, ps_)
                tl = {"OT": [OTt], "xnew": [xnew], "pbT": 6, "pbo": (7, 6)}

                wv = a_wqkv[ja].rearrange("(k p) n -> p k n", p=128)
                for i in range(3):
                    S.dma("pool", "wA%d" % i, lambda e, i=i: e.dma_start(out=wqkv[:, :, i * 512:(i + 1) * 512], in_=wv[:, :, i * 512:(i + 1) * 512]),
                          w=["wqkv"])
                wov = a_wo[ja].rearrange("(k p) n -> p k n", p=128)
                for i in range(2):
                    S.dma("pool", "wA%d" % i, lambda e, i=i: e.dma_start(out=wo_sb[:, :, i * 512:(i + 1) * 512], in_=wov[:, :, i * 512:(i + 1) * 512]),
                          w=["wo"])
                S.pool(lambda e: e.memset(Vaug[:, :, :, 64:65], 1.0), w=["Vones"])
                for qs0 in range(3):
                    S.pool(lambda e, qs0=qs0: e.memset(QT[:, qs0, :, :], 0.0), w=[("QT", qs0)])
                S.act(lambda e: e.activation(out=esink[:, :], in_=sinkB[:, ja * 16:(ja + 1) * 16], func=AF.Exp), r=["sinkB"], w=["esink"])

                def xloadA(b):
                    S.dma("sp", "xld%d" % (b % 2), lambda e: e.dma_start(out=xbr[:, b % 4, :], in_=Xsrc(l, b)),
                          r=[("X", b)], w=[("xb", b % 4)])

                xloadA(0)

                def stage1(b):
                    j = 1 if b < 2 else 0
                    xs_ = b % 4
                    if b + 1 < NB:
                        xloadA(b + 1)
                    hs = b % 2
                    normT(xbr[:, xs_, :], [("xb", xs_)], Gf(0, j), SHf(0, j), lambda k: hT[:, hs, k, :], ("hT", hs), 0)
                    need_q = (b >= 2) or with_ctx

                    def piece(pc, pb):
                        for k in range(8):
                            S.pe(lambda e, k=k: e.matmul(banks[pb][:, :], lhsT=hT[:, hs, k, :], rhs=wqkv[:, k, pc * 512:(pc + 1) * 512],
                                                         start=(k == 0), stop=(k == 7)), r=[("hT", hs), "wqkv"], w=[PB(pb)])

                    piece(2, 1)
                    if b >= 2:
                        blk = b - 2
                        rope_tm(banks[1][:, 0:256].rearrange("p (h i t) -> p h i t", h=4, t=2), 4, 32,
                                cosA[:, blk, :], sinA[:, blk, :],
                                krd[:, :, 0, :].rearrange("p h (i t) -> p h i t", t=2), rtmp,
                                [PB(1), "cosA", "sinA"], "krd0", "rtmp")
                    else:
                        S.act(lambda e: e.activation(out=krd[:, :, 0, :], in_=banks[1][:, 0:256].rearrange("p (h d) -> p h d", h=4),
                                                     func=AF.Copy), r=[PB(1)], w=["krd0"])
                    S.act(lambda e: e.activation(out=Vaug[:, b, :, 0:64], in_=banks[1][:, 256:512].rearrange("p (h d) -> p h d", h=4),
                                                 func=AF.Copy), r=[PB(1)], w=[("V", b)])
                    S.pool(lambda e: e.tensor_copy(out=krd[:, :, 1, :], in_=krd[:, :, 0, :]), r=["krd0"], w=["krd1"])
                    psk = banks[0][:, :].bitcast(BF16)
                    kflat = krd[:, :, :, :].rearrange("p h c d -> p (h c d)")
                    for g in range(4):
                        S.pe(lambda e, g=g: e.transpose(out=psk[:, g * 128:(g + 1) * 128], in_=kflat[:, g * 128:(g + 1) * 128],
                                                       identity=identB[:, :]), r=["krd0", "krd1", "identB"], w=[PB(0)])
                    S.dve(lambda e: e.tensor_copy(out=KT[:, :, b * 128:(b + 1) * 128],
                                                  in_=psk[:, 0:512].rearrange("p (g t) -> p g t", g=4)), r=[PB(0)], w=[("K", b)])
                    if not need_q:
                        return
                    for pc, pb in ((0, 2), (1, 1)):
                        piece(pc, pb)
                        if b >= 2:
                            rope_tm(banks[pb][:, :].rearrange("p (h i t) -> p h i t", h=8, t=2), 8, 32,
                                    cosA[:, b - 2, :], sinA[:, b - 2, :], qr[:, pc * 8:(pc + 1) * 8, :, :], rtmp,
                                    [PB(pb), "cosA", "sinA"], ("qr", pc), "rtmp")
                        else:
                            S.act(lambda e, pc=pc, pb=pb: e.activation(out=qr[:, pc * 8:(pc + 1) * 8, :, :].rearrange("p h i t -> p (h i t)"),
                                                                      in_=banks[pb][:, :], func=AF.Copy), r=[PB(pb)], w=[("qr", pc)])
                    psq = banks[3][:, :].bitcast(BF16)
                    qs = b % 3
                    qflat = qr[:, :, :, :].rearrange("p h i t -> p (h i t)")
                    for k in range(8):
                        S.pe(lambda e, k=k: e.transpose(out=psq[:, k * 128:(k + 1) * 128], in_=qflat[:, k * 128:(k + 1) * 128],
                                                       identity=identB[:, :]), r=[("qr", k // 4), "identB"], w=[PB(3)])
                    S.act(lambda e: e.activation(out=QT[0:64, qs, 0:16:2, :], in_=psq[0:64, :].rearrange("p (k t) -> p k t", k=8), func=AF.Copy),
                          r=[PB(3)], w=[("QT", qs, 0)])
                    S.dve(lambda e: e.tensor_copy(out=QT[64:128, qs, 1:16:2, :], in_=psq[64:128, :].rearrange("p (k t) -> p k t", k=8)),
                          r=[PB(3)], w=[("QT", qs, 1)])

                def stage2(b):
                    j = 1 if b < 2 else 0
                    qs = b % 3
                    if b < 2:
                        keys = [(0, None), (1, None)]
                    else:
                        keys = [(0, None), (1, None)]
                        if b - 1 >= 2:
                            keys.append((b - 1, "P"))
                        keys.append((b, None))
                        if b + 1 < NB:
                            keys.append((b + 1, "N"))
                    nk = len(keys)
                    def qk_exp(g):
                        ps_ = (b * 4 + g) % 2
                        pt_res = ("PT", ps_)
                        for ji, (kb, mk) in enumerate(keys):
                            pb = 4 + ji % 2
                            S.pe(lambda e, kb=kb, pb=pb: e.matmul(
                                banks[pb][:, :].rearrange("p (c t) -> p c t", c=4),
                                lhsT=KT[:, g, kb * 128:(kb + 1) * 128], rhs=QT[:, qs, 4 * g:4 * g + 4, :], start=True, stop=True),
                                r=[("K", kb), ("QT", qs), ("QT", qs, 0), ("QT", qs, 1)], w=[PB(pb)])
                            S.act(lambda e, pb=pb, ji=ji: e.activation(
                                out=PT[:, ps_, ji, :, :], in_=banks[pb][:, :].rearrange("p (c t) -> p c t", c=4), func=AF.Exp, scale=0.125),
                                r=[PB(pb)], w=[(pt_res, ji)])
                            if mk is not None:
                                mt = maskP if mk == "P" else maskN
                                S.pool(lambda e, ji=ji, mt=mt: e.tensor_tensor(
                                    out=PT[:, ps_, ji, :, :], in0=PT[:, ps_, ji, :, :],
                                    in1=mt[:, :].unsqueeze(1).broadcast_to([128, 4, 128]), op=ALU.mult),
                                    r=[(pt_res, ji), "maskP", "maskN"], w=[(pt_res, ji)])

                    def pv_norm(g):
                        ps_ = (b * 4 + g) % 2
                        pt_res = ("PT", ps_)
                        ob = 6
                        for s in range(4):
                            for ji, (kb, mk) in enumerate(keys):
                                S.pe(lambda e, s=s, ji=ji, kb=kb: e.matmul(
                                    banks[ob][:, s * 65:(s + 1) * 65], lhsT=PT[:, ps_, ji, s, :], rhs=Vaug[:, kb, g, :],
                                    start=(ji == 0), stop=(ji == nk - 1)),
                                    r=[(pt_res, ji), ("V", kb), "Vones"], w=[PB(ob)])
                        ov = banks[ob][:, 0:260].rearrange("p (s d) -> p s d", s=4)
                        S.dve(lambda e: e.tensor_tensor(out=den[:, ps_, :], in0=ov[:, :, 64], in1=esink[:, 4 * g:4 * g + 4], op=ALU.add),
                              r=[PB(ob), "esink"], w=[("den", ps_)])
                        S.dve(lambda e: e.reciprocal(out=den[:, ps_, :], in_=den[:, ps_, :]), r=[("den", ps_)], w=[("den", ps_)])
                        Ov = Osb[:, :].rearrange("p (h d) -> p h d", d=64)
                        S.dve(lambda e: e.tensor_tensor(
                            out=Ov[:, 4 * g:4 * g + 4, :], in0=ov[:, :, 0:64],
                            in1=den[:, ps_, :].unsqueeze(2).broadcast_to([128, 4, 64]), op=ALU.mult),
                            r=[PB(ob), ("den", ps_)], w=["Osb"])

                    qk_exp(0)
                    for g in range(4):
                        if g + 1 < 4:
                            qk_exp(g + 1)
                        pv_norm(g)
                    proj_residual(Osb, ["Osb"], wo_sb, xbr[:, b % 4, :], [("xb", b % 4)], 0 + j, b, tl)

                stage1(0)
                stage1(1)
                if with_ctx:
                    stage2(0)
                    stage2(1)
                stage1(2)
                stage1(3)
                for b in range(2, NB):
                    S.rec()
                    stage2(b)
                    r2 = S.end_rec()
                    r1 = None
                    if b + 2 < NB:
                        S.rec()
                        stage1(b + 2)
                        r1 = S.end_rec()
                    S.play([r2, r1])
                S.barrier()

        def phase_B(l):
            jb = l // 2
            with_ctx = l < DEPTH - 1
            b0 = 0 if with_ctx else 2
            scale = 96.0 ** -0.5
            S.raw("act", lambda e: e.preload_act_table(AF.Exp))
            with ExitStack() as pB:
                ckvT = sb("ckvT", [128, 2, U], BF16, pB)
                Kb = sb("Kb", [128, 2, U], BF16, pB)
                with ExitStack() as p1:
                    wdn = sb("wdn", [128, 8, 800], BF16, p1)
                    wuq = sb("wuq", [128, 4, 1536], BF16, p1)
                    xbr = sb("xbr", [128, 2, D], F32, p1)
                    hT = sb("hT", [128, 2, 8, 128], BF16, p1)
                    cqn = sb("cqn", [128, 512], F32, p1)
                    ckn = sb("ckn", [128, 256], F32, p1)
                    cqT = sb("cqT", [128, 2, 4, 128], BF16, p1)
                    rtmp1 = sb("rtmp1", [128, 2, 16], F32, p1)
                    krs = sb("krs", [128, 96], F32, p1)
                    rtmp = sb("rtmp", [128, 2, 256], F32, p1)
                    qf = sb("qf", [128, 16, 32], F32, p1)
                    qrb = sb("qrb", [128, 16, 96], BF16, p1)
                    QTa = sb("QTa", [128, 2, 16, 512], BF16, p1)
                    cosB = sb("cosB", [128, 32, 16], F32, p1)
                    sinB = sb("sinB", [128, 32, 16], F32, p1)
                    S.dma("sp", "k6", lambda e: e.dma_start(out=cosB[:, :, :], in_=cosB_in.rearrange("(b p) i -> p b i", p=128)), w=["cosB"])
                    S.dma("sp", "k7", lambda e: e.dma_start(out=sinB[:, :, :], in_=sinB_in.rearrange("(b p) i -> p b i", p=128)), w=["sinB"])
                    wdv = b_wdown[jb].rearrange("(k p) n -> p k n", p=128)
                    S.dma("pool", "wB0", lambda e: e.dma_start(out=wdn[:, :, 0:400], in_=wdv[:, :, 0:400]), w=["wdn"])
                    S.dma("pool", "wB1", lambda e: e.dma_start(out=wdn[:, :, 400:800], in_=wdv[:, :, 400:800]), w=["wdn"])
                    wqv = b_wuq[jb].rearrange("(k p) n -> p k n", p=128)
                    for i in range(3):
                        S.dma("pool", "wB%d" % (i % 2), lambda e, i=i: e.dma_start(out=wuq[:, :, i * 512:(i + 1) * 512], in_=wqv[:, :, i * 512:(i + 1) * 512]),
                              w=["wuq"])
                    S.dve(lambda e: e.memset(krs[:, :], 0.0), w=["krs"])

                    def qstore(grp, nblk, u0):
                        qa = grp % 2
                        for hh in range(2):
                            S.dma("sp", "qst%d" % hh, lambda e, hh=hh: e.dma_start(
                                out=qT_d[hh * 8:(hh + 1) * 8, :, u0:u0 + nblk * 128].rearrange("h r u -> r h u"),
                                in_=QTa[0:96, qa, hh * 8:(hh + 1) * 8, 0:nblk * 128]), r=[("QTa", qa)], w=[("qT", grp)])

                    def xload(b):
                        S.dma("sp", "xld%d" % (b % 2), lambda e, b=b: e.dma_start(out=xbr[:, b % 2, :], in_=Xsrc(l, b)),
                              r=[("X", b)], w=[("xb", b % 2)])

                    def front(b):
                        j = 1 if b < 2 else 0
                        xs_ = b % 2
                        if b + 1 < NB:
                            xload(b + 1)
                        hs = b % 2
                        normT(xbr[:, xs_, :], [("xb", xs_)], Gf(0, j), SHf(0, j), lambda k, hs=hs: hT[:, hs, k, :], ("hT", hs), 0)
                        for pc, (c0, c1) in enumerate(((0, 512), (512, 800))):
                            pb = 1 + pc
                            for k in range(8):
                                S.pe(lambda e, k=k, pb=pb, c0=c0, c1=c1, hs=hs: e.matmul(
                                    banks[pb][:, 0:c1 - c0], lhsT=hT[:, hs, k, :], rhs=wdn[:, k, c0:c1],
                                    start=(k == 0), stop=(k == 7)), r=[("hT", hs), "wdn"], w=[PB(pb)])
                        need_q = (b >= 2) or with_ctx
                        rs, rn = row_rstd(banks[2][:, 0:256], [PB(2)], 256)
                        S.act(lambda e, rs=rs: e.activation(out=ckn[:, :], in_=banks[2][:, 0:256], func=AF.Copy, scale=rs),
                              r=[PB(2), rn], w=["ckn"])
                        if b >= 2:
                            blk = b - 2
                            rope_tm(banks[2][:, 256:288].rearrange("p (h i t) -> p h i t", h=1, t=2), 1, 16,
                                    cosB[:, blk, :], sinB[:, blk, :],
                                    krs[:, 64:96].rearrange("p (h i t) -> p h i t", h=1, t=2), rtmp1,
                                    [PB(2), "cosB", "sinB"], "krs", "rtmp1")
                        else:
                            S.dve(lambda e: e.tensor_copy(out=krs[:, 64:96], in_=banks[2][:, 256:288]), r=[PB(2)], w=["krs"])
                        for k in range(2):
                            S.pe(lambda e, k=k: e.transpose(out=banks[3][:, k * 128:(k + 1) * 128], in_=ckn[:, k * 128:(k + 1) * 128],
                                                           identity=identF[:, :]), r=["ckn", "identF"], w=[PB(3)])
                        S.pe(lambda e: e.transpose(out=banks[3][0:96, 256:384], in_=krs[:, :], identity=identF[:, :]),
                             r=["krs", "identF"], w=[PB(3)])
                        for k in range(2):
                            S.dve(lambda e, k=k, b=b: e.tensor_scalar(out=ckvT[:, k, b * 128:(b + 1) * 128], in0=banks[3][:, k * 128:(k + 1) * 128],
                                                                     scalar1=kvg[:, jb * 2 + k:jb * 2 + k + 1], scalar2=None, op0=ALU.mult),
                                  r=[PB(3), "params"], w=[("ckvT", b)])
                        for kk in range(2):
                            S.act(lambda e, kk=kk, b=b: e.activation(out=Kb[64:96, kk, b * 128:(b + 1) * 128], in_=banks[3][64:96, 256:384],
                                                                    func=AF.Copy), r=[PB(3)], w=[("KR", b)])
                        if not need_q:
                            return
                        cs_ = b % 2
                        rs2, rn2 = row_rstd(banks[1][:, :], [PB(1)], 512)
                        S.act(lambda e, rs2=rs2: e.activation(out=cqn[:, :], in_=banks[1][:, :], func=AF.Copy, scale=rs2),
                              r=[PB(1), rn2], w=["cqn"])
                        for k in range(4):
                            S.pe(lambda e, k=k: e.transpose(out=banks[4][:, k * 128:(k + 1) * 128], in_=cqn[:, k * 128:(k + 1) * 128],
                                                           identity=identF[:, :]), r=["cqn", "identF"], w=[PB(4)])
                        for k in range(4):
                            S.dve(lambda e, k=k, cs_=cs_: e.tensor_scalar(out=cqT[:, cs_, k, :], in0=banks[4][:, k * 128:(k + 1) * 128],
                                                                scalar1=qng[:, jb * 4 + k:jb * 4 + k + 1], scalar2=None, op0=ALU.mult),
                                  r=[PB(4), "params"], w=[("cqT", cs_)])

                    def back(b):
                        cs_ = b % 2
                        for pc in range(4):
                            pb = 5 + pc % 2
                            for k in range(4):
                                S.pe(lambda e, k=k, pb=pb, pc=pc, cs_=cs_: e.matmul(banks[pb][:, 0:384], lhsT=cqT[:, cs_, k, :],
                                                                         rhs=wuq[:, k, pc * 384:(pc + 1) * 384],
                                                                         start=(k == 0), stop=(k == 3)), r=[("cqT", cs_), "wuq"], w=[PB(pb)])
                            pv = banks[pb][:, 0:384].rearrange("p (h d) -> p h d", h=4)
                            if b >= 2:
                                S.act(lambda e, pv=pv, pc=pc: e.activation(out=qrb[:, pc * 4:(pc + 1) * 4, 0:64], in_=pv[:, :, 0:64], func=AF.Copy),
                                      r=[PB(pb)], w=[("qrb", pc, 0)])
                                S.act(lambda e, pv=pv, pc=pc: e.activation(out=qf[:, pc * 4:(pc + 1) * 4, :], in_=pv[:, :, 64:96], func=AF.Copy),
                                      r=[PB(pb)], w=[("qf", pc)])
                            else:
                                S.act(lambda e, pv=pv, pc=pc: e.activation(out=qrb[:, pc * 4:(pc + 1) * 4, :], in_=pv, func=AF.Copy),
                                      r=[PB(pb)], w=[("qrb", pc, 0), ("qrb", pc, 1)])
                        if b >= 2:
                            rope_tm(qf[:, :, :].rearrange("p h (i t) -> p h i t", t=2), 16, 16,
                                    cosB[:, b - 2, :], sinB[:, b - 2, :],
                                    qrb[:, :, 64:96].rearrange("p h (i t) -> p h i t", t=2), rtmp,
                                    [("qf", 0), ("qf", 1), ("qf", 2), ("qf", 3), "cosB", "sinB"], ("qrb", "rope"), "rtmp")
                        if b < 2:
                            grp, bi = 0, b
                        else:
                            grp, bi = 1 + (b - 2) // 4, (b - 2) % 4
                        qa = grp % 2
                        for hh in range(2):
                            pb = 7
                            pst = banks[pb][:, :].bitcast(BF16)
                            for h8 in range(8):
                                h = hh * 8 + h8
                                S.pe(lambda e, h=h, h8=h8, pst=pst: e.transpose(out=pst[0:96, h8 * 128:(h8 + 1) * 128], in_=qrb[:, h, :],
                                                                               identity=identB[:, :]),
                                     r=[("qrb", h // 4, 0), ("qrb", h // 4, 1), ("qrb", "rope"), "identB"], w=[PB(pb)])
                            S.act(lambda e, hh=hh, pst=pst, qa=qa, bi=bi: e.activation(
                                out=QTa[0:96, qa, hh * 8:(hh + 1) * 8, bi * 128:(bi + 1) * 128],
                                in_=pst[0:96, :].rearrange("p (h t) -> p h t", h=8), func=AF.Copy), r=[PB(pb)], w=[("QTa", qa)])
                        if b == 1:
                            qstore(0, 2, 0)
                        elif b >= 2 and bi == 3:
                            qstore(grp, 4, 256 + (grp - 1) * 512)

                    xload(0)
                    recF, recB = {}, {}
                    for b in range(NB):
                        S.rec()
                        front(b)
                        recF[b] = S.end_rec()
                        if (b >= 2) or with_ctx:
                            S.rec()
                            back(b)
                            recB[b] = S.end_rec()
                    S.play([recF[0]])
                    for b in range(NB):
                        S.play([recB.get(b), recF.get(b + 1)])
                    S.barrier()
                with ExitStack() as p2:
                    Oall = sb("Oall", [128, NB, D], BF16, p2)
                    with ExitStack() as p2a:
                        wuk = sb("wuk", [128, 2, D], BF16, p2a)
                        wuv = sb("wuv", [128, 2, D], BF16, p2a)
                        Qb = sb("Qb", [128, 2, U], BF16, p2a)
                        Vb = sb("Vb", [128, 2, NB, 65], BF16, p2a)
                        PTr = sb("PTr", [128, 6, 512], BF16, p2a)
                        rec = sb("rec", [128, 2, 4], F32, p2a)
                        S.dma("pool", "wB0", lambda e: e.dma_start(out=wuk[:, :, :], in_=b_wuk[jb].rearrange("(k p) n -> p k n", p=128)), w=["wuk"])
                        S.dma("pool", "wB1", lambda e: e.dma_start(out=wuv[:, :, :], in_=b_wuv[jb].rearrange("(k p) n -> p k n", p=128)), w=["wuv"])
                        S.pool(lambda e: e.memset(Vb[:, :, :, 64:65], 1.0), w=["Vbones"])
                        ptc = [0]

                        def build_kv(h):
                            hb = h % 2
                            S.dma("sp", "qld%d" % hb, lambda e, h=h, hb=hb: e.dma_start(out=Qb[0:96, hb, :], in_=qT_d[h, :, :]),
                                  r=[("qT", g_) for g_ in range(9)], w=[("Qb", hb)])
                            for kt in range(9):
                                u0 = kt * 512
                                n = min(512, U - u0)
                                pb = 1
                                for k in range(2):
                                    S.pe(lambda e, k=k, u0=u0, n=n, pb=pb, h=h: e.matmul(
                                        banks[pb][0:64, 0:n], lhsT=wuk[:, k, h * 64:(h + 1) * 64], rhs=ckvT[:, k, u0:u0 + n],
                                        start=(k == 0), stop=(k == 1)), r=["wuk"] + [("ckvT", bb) for bb in range(u0 // 128, (u0 + n) // 128)],
                                        w=[PB(pb)])
                                S.dve(lambda e, u0=u0, n=n, pb=pb, hb=hb: e.tensor_copy(out=Kb[0:64, hb, u0:u0 + n], in_=banks[pb][0:64, 0:n]),
                                      r=[PB(pb)], w=[("Kn", hb)])
                            for vg in range(5):
                                kb0 = vg * 7
                                nkb = min(7, NB - kb0)
                                pb = 3
                                for kk in range(nkb):
                                    kb = kb0 + kk
                                    for k in range(2):
                                        S.pe(lambda e, k=k, kb=kb, kk=kk, pb=pb, h=h: e.matmul(
                                            banks[pb][:, kk * 64:(kk + 1) * 64], lhsT=ckvT[:, k, kb * 128:(kb + 1) * 128],
                                            rhs=wuv[:, k, h * 64:(h + 1) * 64], start=(k == 0), stop=(k == 1)),
                                            r=["wuv", ("ckvT", kb)], w=[PB(pb)])
                                S.dve(lambda e, kb0=kb0, nkb=nkb, pb=pb, hb=hb: e.tensor_copy(
                                    out=Vb[:, hb, kb0:kb0 + nkb, 0:64], in_=banks[pb][:, 0:nkb * 64].rearrange("p (k d) -> p k d", d=64)),
                                    r=[PB(pb)], w=[("Vb", hb)])

                        qtiles = ([(0, 2, [0, 1])] if with_ctx else []) + [(2 + 4 * t, 4, list(range(NB))) for t in range(8)]

                        def attention(h):
                            hb = h % 2
                            items = []
                            for qi, (qb0, nqb, kbs) in enumerate(qtiles):
                                for ki, kb in enumerate(kbs):
                                    items.append((qi, qb0, nqb, ki, kb, len(kbs)))
                            slots = {}

                            def stA(i):
                                qi, qb0, nqb, ki, kb, nk = items[i]
                                nq = nqb * 128
                                sbk = (7, 0, 2, 4)[ptc[0] % 4]
                                pslot = ptc[0] % 6
                                ptc[0] += 1
                                slots[i] = pslot
                                S.pe(lambda e: e.matmul(
                                    banks[sbk][:, 0:nq], lhsT=Kb[0:96, hb, kb * 128:(kb + 1) * 128],
                                    rhs=Qb[0:96, hb, qb0 * 128:qb0 * 128 + nq], start=True, stop=True),
                                    r=[("Kn", hb), ("KR", kb), ("Qb", hb)], w=[PB(sbk)])
                                S.act(lambda e: e.activation(out=PTr[:, pslot, 0:nq], in_=banks[sbk][:, 0:nq], func=AF.Exp, scale=scale),
                                      r=[PB(sbk)], w=[("PTr", pslot)])

                            def stB(i):
                                qi, qb0, nqb, ki, kb, nk = items[i]
                                ob = 5 + (h * 9 + qi) % 2
                                pslot = slots[i]
                                if ki == 0:
                                    S.pe(lambda e: e.matmul(banks[ob][:, 0:nqb * 65], lhsT=zerosB[:, 0:128], rhs=zerosB[:, 0:nqb * 65],
                                                            start=True, stop=True), r=["zerosB"], w=[PB(ob)])
                                for qq in range(nqb):
                                    S.pe(lambda e, qq=qq: e.matmul(
                                        banks[ob][:, qq * 65:(qq + 1) * 65], lhsT=PTr[:, pslot, qq * 128:(qq + 1) * 128],
                                        rhs=Vb[:, hb, kb, :], start=False, stop=True, skip_group_check=True),
                                        r=[("PTr", pslot), ("Vb", hb), "Vbones"], w=[PB(ob)])
                                if ki == nk - 1:
                                    rs_ = (h * 9 + qi) % 2
                                    ov = banks[ob][:, 0:nqb * 65].rearrange("p (q d) -> p q d", d=65)
                                    S.dve(lambda e: e.reciprocal(out=rec[:, rs_, 0:nqb], in_=ov[:, :, 64]), r=[PB(ob)], w=[("rec", rs_)])
                                    S.dve(lambda e: e.tensor_tensor(
                                        out=Oall[:, qb0:qb0 + nqb, h * 64:(h + 1) * 64], in0=ov[:, :, 0:64],
                                        in1=rec[:, rs_, 0:nqb].unsqueeze(2).broadcast_to([128, nqb, 64]), op=ALU.mult),
                                        r=[PB(ob), ("rec", rs_)], w=["Oall"])

                            n_it = len(items)
                            for i in range(min(3, n_it)):
                                stA(i)
                            for i in range(n_it):
                                if i + 3 < n_it:
                                    stA(i + 3)
                                stB(i)

                        build_kv(0)
                        for h in range(16):
                            S.rec()
                            attention(h)
                            ra = S.end_rec()
                            rb = None
                            if h + 1 < 16:
                                S.rec()
                                build_kv(h + 1)
                                rb = S.end_rec()
                            S.play([ra, rb])
                        S.barrier()
                    with ExitStack() as p3:
                        wo_sb = sb("wo", [128, 8, D], BF16, p3)
                        xbr = sb("xbr", [128, 2, D], F32, p3)
                        OTt = sb("OTt", [128, 2, D], BF16, p3)
                        xnew = sb("xnew", [128, 2, D], F32, p3)
                        tl = {"OT": [OTt[:, 0, :], OTt[:, 1, :]], "xnew": [xnew[:, 0, :], xnew[:, 1, :]], "pbT": 4, "pbo": (1, 2)}
                        wov = b_wo[jb].rearrange("(k p) n -> p k n", p=128)
                        for i in range(2):
                            S.dma("pool", "wB%d" % i, lambda e, i=i: e.dma_start(out=wo_sb[:, :, i * 512:(i + 1) * 512], in_=wov[:, :, i * 512:(i + 1) * 512]),
                                  w=["wo"])
                        def xload3(b):
                            S.dma("sp", "xld%d" % (b % 2), lambda e, b=b: e.dma_start(out=xbr[:, b % 2, :], in_=Xsrc(l, b)),
                                  r=[("X", b)], w=[("xb", b % 2)])

                        xload3(b0)
                        proj_T(Oall[:, b0, :], [], tl, b0 % 2)
                        for b in range(b0, NB):
                            j = 1 if b < 2 else 0
                            xs_ = b % 2
                            if b + 1 < NB:
                                xload3(b + 1)
                                proj_T(Oall[:, b + 1, :], [], tl, (b + 1) % 2)
                            proj_MM(wo_sb, xbr[:, xs_, :], [("xb", xs_)], 0 + j, b, tl, b % 2)
                        S.barrier()

        def phase_F(l):
            with_ctx = l < DEPTH - 1
            last = l == DEPTH - 1
            S.raw("act", lambda e: e.preload_act_table(AF.Silu))
            with ExitStack() as pf:
                wout = sb("wout", [128, NCF, D], BF16, pf)
                wr = sb("wr", [128, 4, 8, 256], BF16, pf)
                h2T = sb("h2T", [128, 2, 8, 512], BF16, pf)
                halo = sb("halo", [128, 3, 8, 2], BF16, pf)
                gT = sb("gT", [128, NCF, 512], BF16, pf)
                xbr4 = sb("xbr4", [128, 4, D], F32, pf)
                xn4 = sb("xn4", [128, 4, D], F32, pf)
                xb3 = sb("xb3", [128, 2, D], F32, pf)
                acc = sb("acc", [128, 2, 512], F32, pf)
                sl = sb("sl", [128, 2, 512], F32, pf)
                xnew = sb("xnew", [128, 2, D], F32, pf)
                yo = sb("yo", [128, 2, D], F32, pf)
                wov = f_wout[l].rearrange("(c p) n -> p c n", p=128)
                for i in range(2):
                    S.dma("pool", "wF%d" % i, lambda e, i=i: e.dma_start(out=wout[:, i * 11:(i + 1) * 11, :], in_=wov[:, i * 11:(i + 1) * 11, :]),
                          w=["wout"])
                wiv = f_win[l].rearrange("(k p) n -> p k n", p=128)
                tiles = ([(0, 2)] if with_ctx else []) + [(2 + 4 * t, 4) for t in range(8)]
                wc = [0]
                xc = [0]

                def s1(ti):
                    b0_, nb_ = tiles[ti]
                    hs = ti % 2
                    lat_first = (b0_ == 2)
                    lat_last = (b0_ + nb_ == NB)
                    if b0_ < 2 or lat_first:
                        S.pool(lambda e: e.memset(halo[:, ti % 3, :, 0:1], 0.0), w=[("halo", ti % 3, 0)])
                    if b0_ < 2 or lat_last:
                        S.pool(lambda e: e.memset(halo[:, ti % 3, :, 1:2], 0.0), w=[("halo", ti % 3, 1)])
                    st = []
                    for bi in range(nb_):
                        b = b0_ + bi
                        S.dma("sp", "xld%d" % bi, lambda e, b=b, bi=bi: e.dma_start(out=xbr4[:, bi, :], in_=Xs[b * 128:(b + 1) * 128, :]),
                              r=[("X", b)], w=[("xb", bi)])
                    for bi in range(nb_):
                        st.append(row_rstd(xbr4[:, bi, :], [("xb", bi)], D))
                    for bi in range(nb_):
                        rs, rn = st[bi]
                        S.act(lambda e, bi=bi, rs=rs: e.activation(out=xn4[:, bi, :], in_=xbr4[:, bi, :], func=AF.Copy, scale=rs),
                              r=[("xb", bi), rn], w=[("xn4", bi)])
                    for bi in range(nb_):
                        b = b0_ + bi
                        j = 1 if b < 2 else 0
                        G, SH = Gf(1, j), SHf(1, j)
                        for hlf in range(2):
                            pbk = (0, 6)[hlf]
                            for kk in range(4):
                                k = hlf * 4 + kk
                                S.pe(lambda e, k=k, kk=kk, pbk=pbk, bi=bi: e.transpose(out=banks[pbk][:, kk * 128:(kk + 1) * 128],
                                                                                   in_=xn4[:, bi, k * 128:(k + 1) * 128], identity=identF[:, :]),
                                     r=[("xn4", bi), "identF"], w=[PB(pbk)])
                        for hlf in range(2):
                            pbk = (0, 6)[hlf]
                            for kk in range(4):
                                k = hlf * 4 + kk
                                dst = h2T[:, hs, k, bi * 128:(bi + 1) * 128]
                                if hlf == 1:
                                    S.act(lambda e, k=k, kk=kk, pbk=pbk, dst=dst, G=G, SH=SH: e.activation(
                                        out=dst, in_=banks[pbk][:, kk * 128:(kk + 1) * 128], func=AF.Identity, scale=G(k), bias=SH(k)),
                                        r=[PB(pbk), "mods"], w=[("h2T", hs, bi)])
                                else:
                                    S.dve(lambda e, k=k, kk=kk, pbk=pbk, dst=dst, G=G, SH=SH: e.tensor_scalar(
                                        out=dst, in0=banks[pbk][:, kk * 128:(kk + 1) * 128], scalar1=G(k), scalar2=SH(k),
                                        op0=ALU.mult, op1=ALU.add), r=[PB(pbk), "mods"], w=[("h2T", hs, bi)])
                        if b >= 2 and bi == 0 and not lat_first:
                            S.pool(lambda e: e.tensor_copy(out=halo[:, (ti - 1) % 3, :, 1:2], in_=h2T[:, hs, :, 0:1]),
                                   r=[("h2T", hs, 0)], w=[("halo", (ti - 1) % 3, 1)])
                        if b >= 2 and bi == nb_ - 1 and not lat_last:
                            S.pool(lambda e: e.tensor_copy(out=halo[:, (ti + 1) % 3, :, 0:1],
                                                           in_=h2T[:, hs, :, nb_ * 128 - 1:nb_ * 128]),
                                   r=[("h2T", hs, nb_ - 1)], w=[("halo", (ti + 1) % 3, 0)])

                def s2(ti):
                    b0_, nb_ = tiles[ti]
                    n = nb_ * 128
                    hs = ti % 2
                    hres = [("h2T", hs, bi) for bi in range(nb_)]
                    for c in range(NCF):
                        ws = wc[0] % 4
                        wc[0] += 1
                        S.dma("pool", "wi%d" % ws, lambda e, c=c, ws=ws: e.dma_start(out=wr[:, ws, :, 0:128], in_=wiv[:, :, c * 128:(c + 1) * 128]),
                              w=[("wr", ws, 0)])
                        S.dma("pool", "wj%d" % ws, lambda e, c=c, ws=ws: e.dma_start(out=wr[:, ws, :, 128:256],
                                                                                   in_=wiv[:, :, DFF + c * 128:DFF + (c + 1) * 128]),
                              w=[("wr", ws, 1)])
                        pa = 1 + 2 * (c % 2)
                        pv = 2 + 2 * (c % 2)
                        hcol = (c % 2) * 2
                        for k in range(8):
                            S.pe(lambda e, k=k, ws=ws, pa=pa: e.matmul(banks[pa][:, 0:n], lhsT=wr[:, ws, k, 0:128], rhs=h2T[:, hs, k, 0:n],
                                                                     start=(k == 0), stop=(k == 7)), r=[("wr", ws, 0)] + hres, w=[PB(pa)])
                        for k in range(8):
                            S.pe(lambda e, k=k, ws=ws, hcol=hcol: e.matmul(banks[5][:, hcol:hcol + 2], lhsT=wr[:, ws, k, 0:128],
                                                                         rhs=halo[:, ti % 3, k, :], start=(k == 0), stop=(k == 7)),
                                 r=[("wr", ws, 0), ("halo", ti % 3, 0), ("halo", ti % 3, 1)], w=[PB(5)])
                        for k in range(8):
                            S.pe(lambda e, k=k, ws=ws, pv=pv: e.matmul(banks[pv][:, 0:n], lhsT=wr[:, ws, k, 128:256], rhs=h2T[:, hs, k, 0:n],
                                                                     start=(k == 0), stop=(k == 7)), r=[("wr", ws, 1)] + hres, w=[PB(pv)])
                        a_ = c % 2
                        w0 = cw[:, (l * 3 + 0) * NCF + c:(l * 3 + 0) * NCF + c + 1]
                        w1 = cw[:, (l * 3 + 1) * NCF + c:(l * 3 + 1) * NCF + c + 1]
                        w2 = cw[:, (l * 3 + 2) * NCF + c:(l * 3 + 2) * NCF + c + 1]
                        bb = cb[:, l * NCF + c:l * NCF + c + 1]
                        ar = ("acc", a_)
                        S.act(lambda e, pa=pa, a_=a_, w1=w1, bb=bb: e.activation(out=acc[:, a_, 0:n], in_=banks[pa][:, 0:n], func=AF.Identity,
                                                                                scale=w1, bias=bb), r=[PB(pa), "params"], w=[ar])
                        S.dve(lambda e, pa=pa, a_=a_, w0=w0: e.scalar_tensor_tensor(out=acc[:, a_, 1:n], in0=banks[pa][:, 0:n - 1], scalar=w0,
                                                                                   in1=acc[:, a_, 1:n], op0=ALU.mult, op1=ALU.add),
                              r=[PB(pa), ar, "params"], w=[ar])
                        S.dve(lambda e, pa=pa, a_=a_, w2=w2: e.scalar_tensor_tensor(out=acc[:, a_, 0:n - 1], in0=banks[pa][:, 1:n], scalar=w2,
                                                                                   in1=acc[:, a_, 0:n - 1], op0=ALU.mult, op1=ALU.add),
                              r=[PB(pa), ar], w=[ar])
                        S.dve(lambda e, a_=a_, w0=w0, hcol=hcol: e.scalar_tensor_tensor(out=acc[:, a_, 0:1], in0=banks[5][:, hcol:hcol + 1], scalar=w0,
                                                                                       in1=acc[:, a_, 0:1], op0=ALU.mult, op1=ALU.add),
                              r=[PB(5), ar], w=[ar])
                        S.dve(lambda e, a_=a_, w2=w2, hcol=hcol: e.scalar_tensor_tensor(out=acc[:, a_, n - 1:n], in0=banks[5][:, hcol + 1:hcol + 2],
                                                                                       scalar=w2, in1=acc[:, a_, n - 1:n], op0=ALU.mult, op1=ALU.add),
                              r=[PB(5), ar], w=[ar])
                        S.act(lambda e, a_=a_: e.activation(out=sl[:, a_, 0:n], in_=acc[:, a_, 0:n], func=AF.Silu), r=[ar], w=[("sl", a_)])
                        S.dve(lambda e, a_=a_, pv=pv, c=c: e.tensor_tensor(out=gT[:, c, 0:n], in0=banks[pv][:, 0:n], in1=sl[:, a_, 0:n], op=ALU.mult),
                              r=[PB(pv), ("sl", a_)], w=[("gT", c)])

                def s3(ti):
                    b0_, nb_ = tiles[ti]
                    for bi in range(nb_):
                        b = b0_ + bi
                        j = 1 if b < 2 else 0
                        xs_ = b % 2
                        xo = xc[0] % 2
                        xc[0] += 1
                        S.dma("sp", "xl3%d" % xs_, lambda e, b=b, xs_=xs_: e.dma_start(out=xb3[:, xs_, :], in_=Xs[b * 128:(b + 1) * 128, :]),
                              r=[("X", b)], w=[("xb3", xs_)])
                        for hlf in range(2):
                            pb = 6 + hlf
                            for c in range(NCF):
                                S.pe(lambda e, c=c, hlf=hlf, pb=pb, bi=bi: e.matmul(banks[pb][:, :], lhsT=gT[:, c, bi * 128:(bi + 1) * 128],
                                                                                   rhs=wout[:, c, hlf * 512:(hlf + 1) * 512],
                                                                                   start=(c == 0), stop=(c == NCF - 1)),
                                     r=[("gT", c), "wout"], w=[PB(pb)])
                            S.dve(lambda e, hlf=hlf, pb=pb, xo=xo, j=j: e.tensor_tensor(out=xnew[:, xo, hlf * 512:(hlf + 1) * 512], in0=banks[pb][:, :],
                                                                                      in1=gateB[:, 2 + j, hlf * 512:(hlf + 1) * 512], op=ALU.mult),
                                  r=[PB(pb), "gates"], w=[("xnew", xo, hlf)])
                            S.pool(lambda e, hlf=hlf, xo=xo, xs_=xs_: e.tensor_tensor(out=xnew[:, xo, hlf * 512:(hlf + 1) * 512],
                                                                                    in0=xnew[:, xo, hlf * 512:(hlf + 1) * 512],
                                                                                    in1=xb3[:, xs_, hlf * 512:(hlf + 1) * 512], op=ALU.add),
                                   r=[("xnew", xo, hlf), ("xb3", xs_)], w=[("xnew", xo, hlf)])
                        xres = [("xnew", xo, 0), ("xnew", xo, 1)]
                        if not last:
                            S.dma("sp", "xst", lambda e, b=b, xo=xo: e.dma_start(out=Xs[b * 128:(b + 1) * 128, :], in_=xnew[:, xo, :]),
                                  r=xres, w=[("X", b)])
                        else:
                            i = new_stat()
                            rn = ("st", i)
                            S.dve(lambda e, xo=xo, i=i: e.scalar_tensor_tensor(out=yo[:, xo, :], in0=xnew[:, xo, :], scalar=1.0, in1=xnew[:, xo, :],
                                                                              op0=ALU.mult, op1=ALU.mult, accum_out=ssr[:, i:i + 1]),
                                  r=xres, w=[rn, ("yo", xo)])
                            S.dve(lambda e, i=i: e.tensor_scalar(out=msr[:, i:i + 1], in0=ssr[:, i:i + 1], scalar1=1.0 / D, scalar2=EPS,
                                                                op0=ALU.mult, op1=ALU.add), r=[rn], w=[rn])
                            S.pool(lambda e, i=i: e.tensor_tensor(out=rstdr[:, i:i + 1], in0=msr[:, i:i + 1], in1=neghalf[:, :], op=ALU.pow),
                                   r=[rn, "neghalf"], w=[rn])
                            S.dve(lambda e, xo=xo, i=i: e.scalar_tensor_tensor(out=yo[:, xo, :], in0=xnew[:, xo, :], scalar=rstdr[:, i:i + 1],
                                                                              in1=finalgB[:, :], op0=ALU.mult, op1=ALU.mult),
                                  r=xres + [rn, "finalgB", ("yo", xo)], w=[("yo", xo)])
                            S.dma("sp", "ost", lambda e, b=b, xo=xo: e.dma_start(out=out[(b - 2) * 128:(b - 1) * 128, :], in_=yo[:, xo, :]),
                                  r=[("yo", xo)], w=[("out", b)])

                nt = len(tiles)

                s1(0)
                for ti in range(nt):
                    if ti + 1 < nt:
                        s1(ti + 1)
                    s2(ti)
                    s3(ti)
                S.barrier()

        phases = []
        for l in range(DEPTH):
            phases.append(("M", l))
            phases.append(("A" if l % 2 == 0 else "B", l))
            phases.append(("F", l))
        nrun = 0
        for kind, l in phases:
            if kind == "M":
                if n_phases is not None and nrun >= n_phases:
                    break
                modulation(l)
                continue
            if n_phases is not None and nrun >= n_phases:
                break
            if kind == "A":
                phase_A(l)
            elif kind == "B":
                phase_B(l)
            else:
                phase_F(l)
            nrun += 1
        S.finish()
    return nc


def _consts():
    def ang(rot_dim):
        row = np.repeat(np.arange(T // GRID_W), GRID_W).astype(np.float32)
        col = np.tile(np.arange(GRID_W), T // GRID_W).astype(np.float32)
        nf = rot_dim // 4
        inv = (np.float32(10000.0) ** (-np.arange(nf, dtype=np.float32) / np.float32(nf))).astype(np.float32)
        return np.concatenate([row[:, None] * inv, col[:, None] * inv], axis=-1).astype(np.float32)
    aA, aB = ang(64), ang(32)
    k = np.arange(128)[:, None]
    q = np.arange(128)[None, :]
    return {
        "ident": np.eye(128, dtype=np.float32),
        "maskp": (k >= q).astype(np.float32),
        "maskn": (k <= q).astype(np.float32),
        "cosA": np.cos(aA).astype(np.float32), "sinA": np.sin(aA).astype(np.float32),
        "cosB": np.cos(aB).astype(np.float32), "sinB": np.sin(aB).astype(np.float32),
    }


_WNAMES = ["mod_w", "mod_b", "norm1_g", "norm2_g", "a_wqkv", "a_wo", "a_sink", "b_wdown", "b_qnorm_g", "b_wuq",
           "b_kvnorm_g", "b_wuk", "b_wuv", "b_wo", "f_win", "f_conv_w", "f_conv_b", "f_wout"]


def make_in_maps(inputs, cores):
    consts = _consts()
    shared = {n: np.ascontiguousarray(np.asarray(inputs[n], dtype=np.float32)) for n in _WNAMES}
    shared["final_g"] = np.ascontiguousarray(np.asarray(inputs["final_g"], dtype=np.float32).reshape(1, D))
    shared.update(consts)
    x = np.asarray(inputs["x"], dtype=np.float32)
    c = np.asarray(inputs["c"], dtype=np.float32)
    ctx = np.asarray(inputs["ctx"], dtype=np.float32)
    c_ctx = np.asarray(inputs["c_ctx"], dtype=np.float32)
    maps = []
    for b in cores:
        m = dict(shared)
        m["x"] = np.ascontiguousarray(x[b])
        m["ctx"] = np.ascontiguousarray(ctx[b])
        m["cc"] = np.ascontiguousarray(np.concatenate([c[b].reshape(8, 128), c_ctx.reshape(8, 128)], axis=0))
        maps.append(m)
    return maps


def kernel(**inputs):
    nc = build()
    maps = make_in_maps(inputs, list(range(8)))
    res = run_bass_kernel_spmd(nc, maps, core_ids=list(range(8)))
    return np.stack([np.asarray(r["out"], dtype=np.float32) for r in res.results], axis=0)
```

```python
import numpy as np
from contextlib import ExitStack
import concourse.bass as bass
import concourse.mybir as mybir
from concourse.bass_utils import run_bass_kernel_spmd

F32 = mybir.dt.float32
BF16 = mybir.dt.bfloat16
AF = mybir.ActivationFunctionType
ALU = mybir.AluOpType

D = 1024
T = 4096
L = 256
U = T + L
NB = U // 128
DFF = 2816
NCF = DFF // 128
DEPTH = 4
EPS = 1e-6
GRID_W = 64
ENGS = ("pe", "dve", "act", "pool", "sp")
_DBG = {}
ENGATTR = {"pe": "tensor", "dve": "vector", "act": "scalar", "pool": "gpsimd", "sp": "sync"}


class Sched:
    def __init__(self, nc, block, stack):
        self.nc, self.block, self.stack = nc, block, stack
        self.sem, self.cnt = {}, {}
        self.seen = {e: {} for e in ENGS}
        self.res = {}
        for e in ENGS:
            self.sem[e] = stack.enter_context(nc.semaphore("s_" + e))
            self.cnt[e] = 0

    def chan(self, name):
        if name not in self.sem:
            self.sem[name] = self.stack.enter_context(self.nc.semaphore("c_" + name))
            self.cnt[name] = 0
        return name

    def _deps(self, eng, reads, writes):
        deps = {}

        def add(p, c, raw):
            if p == eng and eng == "pe":
                return
            if deps.get(p, 0) < c:
                deps[p] = c

        for r in reads:
            st = self.res.get(r)
            if st and st["w"]:
                add(st["w"][0], st["w"][1], True)
        for w in writes:
            st = self.res.get(w)
            if st:
                if st["w"]:
                    add(st["w"][0], st["w"][1], False)
                for p, c in st["r"].items():
                    add(p, c, False)
        return deps

    def _commit(self, prod, val, reads, writes):
        for r in reads:
            st = self.res.setdefault(r, {"w": None, "r": {}})
            if st["r"].get(prod, 0) < val:
                st["r"][prod] = val
        for w in writes:
            self.res[w] = {"w": (prod, val), "r": {}}

    def _waits(self, eng, deps):
        waits = []
        for p, c in deps.items():
            if self.seen[eng].get(p, 0) < c:
                waits.append((p, c))
                self.seen[eng][p] = c
        return waits

    @staticmethod
    def _excl(r, w):
        ps = [x for x in r if isinstance(x, tuple) and x and x[0] == "ps"]
        if not ps:
            return list(r), list(w)
        return [x for x in r if x not in ps], list(w) + [x for x in ps if x not in w]

    def rec(self):
        self._rec = []

    def end_rec(self):
        lst, self._rec = self._rec, None
        return lst

    def play(self, lists):
        lists = [l for l in lists if l]
        idx = [0] * len(lists)
        total = sum(len(l) for l in lists)
        for _ in range(total):
            best, bi = None, -1
            for i, l in enumerate(lists):
                if idx[i] < len(l):
                    frac = idx[i] / len(l)
                    if best is None or frac < best:
                        best, bi = frac, i
            kind, a = lists[bi][idx[bi]]
            idx[bi] += 1
            if kind == "op":
                self.op(*a)
            else:
                self.dma(*a)

    def op(self, eng, fn, r=(), w=()):
        if getattr(self, "_rec", None) is not None:
            self._rec.append(("op", (eng, fn, tuple(r), tuple(w))))
            return
        r, w = self._excl(r, w)
        deps = self._deps(eng, r, w)
        waits = self._waits(eng, deps)
        self.cnt[eng] += 1
        val = self.cnt[eng]
        sem = self.sem

        def body(e):
            for p, c in waits:
                e.wait_ge(sem[p], c)
            fn(e).then_inc(sem[eng], 1)

        getattr(self.block, ENGATTR[eng])(body)
        self._commit(eng, val, r, w)

    def pe(self, fn, r=(), w=()):
        self.op("pe", fn, r, w)

    def dve(self, fn, r=(), w=()):
        self.op("dve", fn, r, w)

    def act(self, fn, r=(), w=()):
        self.op("act", fn, r, w)

    def pool(self, fn, r=(), w=()):
        self.op("pool", fn, r, w)

    def dma(self, q, chan, fn, r=(), w=()):
        if getattr(self, "_rec", None) is not None:
            self._rec.append(("dma", (q, chan, fn, tuple(r), tuple(w))))
            return
        self.chan(chan)
        r, w = self._excl(r, w)
        deps = self._deps(q, r, w)
        if self.cnt[chan] > 0 and deps.get(chan, 0) < self.cnt[chan]:
            deps[chan] = self.cnt[chan]
        waits = self._waits(q, deps)
        self.cnt[chan] += 16
        val = self.cnt[chan]
        sem = self.sem

        def body(e):
            for p, c in waits:
                e.wait_ge(sem[p], c)
            fn(e).then_inc(sem[chan], 16)

        getattr(self.block, ENGATTR[q])(body)
        self._commit(chan, val, r, w)

    def raw(self, eng, fn):
        getattr(self.block, ENGATTR[eng])(lambda e: fn(e))

    def barrier(self):
        snap = dict(self.cnt)
        sem = self.sem
        for eng in ENGS:
            waits = []
            for p, c in snap.items():
                if c > 0 and self.seen[eng].get(p, 0) < c:
                    if p == eng and eng == "pe":
                        continue
                    waits.append((p, c))
                    self.seen[eng][p] = c
            if waits:
                def body(e, waits=waits):
                    for p, c in waits:
                        e.wait_ge(sem[p], c)
                getattr(self.block, ENGATTR[eng])(body)
        self.res = {}

    def finish(self):
        snap = dict(self.cnt)
        sem = self.sem

        def body(e):
            for p, c in snap.items():
                if c > 0:
                    e.wait_ge(sem[p], c)
        self.block.sync(body)


def build(n_phases=None, debug=False):
    nc = bass.Bass("TRN2", target_bir_lowering=False)

    def din(name, shape, dt=F32):
        return nc.dram_tensor(name, list(shape), dt, kind="ExternalInput").ap()

    x_in = din("x", [T, D])
    ctx_in = din("ctx", [L, D])
    cc_in = din("cc", [16, 128])
    mod_w = din("mod_w", [DEPTH, D, 6 * D])
    mod_b = din("mod_b", [DEPTH, 6 * D])
    norm1_g = din("norm1_g", [DEPTH, D])
    norm2_g = din("norm2_g", [DEPTH, D])
    a_wqkv = din("a_wqkv", [2, D, 1536])
    a_wo = din("a_wo", [2, D, D])
    a_sink = din("a_sink", [2, 16])
    b_wdown = din("b_wdown", [2, D, 800])
    b_qnorm_g = din("b_qnorm_g", [2, 512])
    b_wuq = din("b_wuq", [2, 512, 1536])
    b_kvnorm_g = din("b_kvnorm_g", [2, 256])
    b_wuk = din("b_wuk", [2, 256, 1024])
    b_wuv = din("b_wuv", [2, 256, 1024])
    b_wo = din("b_wo", [2, D, D])
    f_win = din("f_win", [DEPTH, D, 2 * DFF])
    f_conv_w = din("f_conv_w", [DEPTH, 3, DFF])
    f_conv_b = din("f_conv_b", [DEPTH, DFF])
    f_wout = din("f_wout", [DEPTH, DFF, D])
    final_g = din("final_g", [1, D])
    ident_in = din("ident", [128, 128])
    maskp_in = din("maskp", [128, 128])
    maskn_in = din("maskn", [128, 128])
    cosA_in = din("cosA", [T, 32])
    sinA_in = din("sinA", [T, 32])
    cosB_in = din("cosB", [T, 16])
    sinB_in = din("sinB", [T, 16])

    out = nc.dram_tensor("out", [T, D], F32, kind="ExternalOutput").ap()
    if debug:
        Xs = nc.dram_tensor("xs", [U, D], F32, kind="ExternalOutput").ap()
    else:
        Xs = nc.dram_tensor("xs", [U, D], F32).ap()
    qT_d = nc.dram_tensor("qT", [16, 96, U], BF16).ap()

    stack = ExitStack()
    with stack:
        uniq = [0]

        def sb(name, shape, dt=F32, st=stack):
            uniq[0] += 1
            return st.enter_context(nc.sbuf_tensor("%s_%d" % (name, uniq[0]), list(shape), dt))

        banks = [stack.enter_context(nc.psum_tensor("bank%d" % i, [128, 512], F32)) for i in range(8)]

        def PB(i):
            return ("ps", i)

        identF = sb("identF", [128, 128])
        identB = sb("identB", [128, 128], BF16)
        zerosB = sb("zerosB", [128, 512], BF16)
        maskP = sb("maskP", [128, 128], BF16)
        maskN = sb("maskN", [128, 128], BF16)
        n1g = sb("n1g", [128, 32])
        n2g = sb("n2g", [128, 32])
        modbF = sb("modbF", [128, DEPTH * 48])
        cw = sb("cw", [128, DEPTH * 3 * NCF])
        cb = sb("cb", [128, DEPTH * NCF])
        qng = sb("qng", [128, 8])
        kvg = sb("kvg", [128, 4])
        cF = sb("cF", [128, 16])
        finalgB = sb("finalgB", [128, D])
        sinkB = sb("sinkB", [128, 32])
        esink = sb("esink", [128, 16])
        neghalf = sb("neghalf", [128, 1])
        gateB = sb("gateB", [128, 4, D])
        modF = sb("modF", [128, 48, 2])
        Gs = sb("Gs", [128, 2, 8, 2])
        scT = sb("scT", [128, 8, 2], BF16)
        screp = sb("screp", [128, 8, 2, 128], BF16)
        stage = sb("stage", [128, 128])
        ssr = sb("ssr", [128, 8])
        msr = sb("msr", [128, 8])
        rstdr = sb("rstdr", [128, 8])
        junk = sb("junk", [128, D], BF16)
        xn = sb("xn", [128, 2, D])

        block = stack.enter_context(nc.Block())
        S = Sched(nc, block, stack)
        ctr = {"st": 0, "xn": 0}

        S.dma("sp", "k0", lambda e: e.dma_start(out=identF[:, :], in_=ident_in[:, :]), w=["identF"])
        S.dma("pool", "k1", lambda e: e.dma_start(out=identB[:, :], in_=ident_in[:, :]), w=["identB"])
        S.dma("pool", "k2", lambda e: e.dma_start(out=maskP[:, :], in_=maskp_in[:, :]), w=["maskP"])
        S.dma("pool", "k3", lambda e: e.dma_start(out=maskN[:, :], in_=maskn_in[:, :]), w=["maskN"])
        S.dve(lambda e: e.memset(zerosB[:, :], 0.0), w=["zerosB"])
        S.dve(lambda e: e.memset(neghalf[:, :], -0.5), w=["neghalf"])
        S.dma("sp", "k4", lambda e: e.dma_start(out=finalgB[:, :], in_=final_g[0:1, :].broadcast_to([128, D])), w=["finalgB"])
        S.dma("sp", "k5", lambda e: e.dma_start(out=sinkB[:, :], in_=a_sink.rearrange("a h -> (a h)").unsqueeze(0).broadcast_to([128, 32])), w=["sinkB"])

        def featmajor(dst, src2d, n):
            S.dma("sp", "k5", lambda e: e.dma_start(out=stage[0:n, :], in_=src2d), w=["stage"])
            S.pe(lambda e: e.transpose(out=banks[0][:, 0:n], in_=stage[0:n, :], identity=identF[0:n, 0:n]),
                 r=["stage", "identF"], w=[PB(0)])
            S.dve(lambda e: e.tensor_copy(out=dst, in_=banks[0][:, 0:n]), r=[PB(0)], w=["params"])

        featmajor(n1g[:, :], norm1_g.rearrange("l (k p) -> (l k) p", p=128), 32)
        featmajor(n2g[:, :], norm2_g.rearrange("l (k p) -> (l k) p", p=128), 32)
        mb2 = mod_b.rearrange("l (k p) -> (l k) p", p=128)
        featmajor(modbF[:, 0:128], mb2[0:128, :], 128)
        featmajor(modbF[:, 128:192], mb2[128:192, :], 64)
        cw2 = f_conv_w.rearrange("l j (k p) -> (l j k) p", p=128)
        featmajor(cw[:, 0:128], cw2[0:128, :], 128)
        featmajor(cw[:, 128:256], cw2[128:256, :], 128)
        featmajor(cw[:, 256:264], cw2[256:264, :], 8)
        featmajor(cb[:, :], f_conv_b.rearrange("l (k p) -> (l k) p", p=128), 88)
        featmajor(qng[:, :], b_qnorm_g.rearrange("l (k p) -> (l k) p", p=128), 8)
        featmajor(kvg[:, :], b_kvnorm_g.rearrange("l (k p) -> (l k) p", p=128), 4)
        featmajor(cF[:, :], cc_in[:, :], 16)
        S.raw("act", lambda e: e.preload_act_table(AF.Silu))
        S.act(lambda e: e.activation(out=scT[:, :, :].rearrange("p k j -> p j k"),
                                     in_=cF[:, :].rearrange("p (j k) -> p j k", j=2), func=AF.Silu),
              r=["params"], w=["scT"])
        for j in range(2):
            S.dve(lambda e, j=j: e.tensor_copy(out=screp[:, :, j, :], in_=scT[:, :, j:j + 1].broadcast_to([128, 8, 128])),
                  r=["scT"], w=["screp"])

        def rstd_of(ss_ap, n_feat):
            i = ctr["st"] % 8
            return i

        def new_stat():
            i = ctr["st"] % 8
            ctr["st"] += 1
            return i

        def row_rstd(src, src_res, n_feat):
            i = new_stat()
            rn = ("st", i)
            S.act(lambda e: e.activation(out=junk[:, 0:n_feat], in_=src, func=AF.Square, accum_out=ssr[:, i:i + 1]),
                  r=src_res, w=[rn, "junk"])
            S.dve(lambda e: e.tensor_scalar(out=msr[:, i:i + 1], in0=ssr[:, i:i + 1], scalar1=1.0 / n_feat, scalar2=EPS,
                                            op0=ALU.mult, op1=ALU.add), r=[rn], w=[rn])
            S.pool(lambda e: e.tensor_tensor(out=rstdr[:, i:i + 1], in0=msr[:, i:i + 1], in1=neghalf[:, :], op=ALU.pow),
                   r=[rn, "neghalf"], w=[rn])
            return rstdr[:, i:i + 1], rn

        def normT(xb, xb_res, G, SH, dst_fn, dst_res, psb, psb2=None, act_half=False):
            rs, rn = row_rstd(xb, xb_res, D)
            s = ctr["xn"] % 2
            ctr["xn"] += 1
            S.act(lambda e: e.activation(out=xn[:, s, :], in_=xb, func=AF.Copy, scale=rs), r=list(xb_res) + [rn], w=[("xn", s)])
            def tr(hlf, pbk):
                for kk in range(4):
                    k = hlf * 4 + kk
                    S.pe(lambda e, k=k, kk=kk: e.transpose(out=banks[pbk][:, kk * 128:(kk + 1) * 128],
                                                           in_=xn[:, s, k * 128:(k + 1) * 128], identity=identF[:, :]),
                         r=[("xn", s), "identF"], w=[PB(pbk)])

            def ev(hlf, pbk):
                for kk in range(4):
                    k = hlf * 4 + kk
                    if act_half and hlf == 1:
                        S.act(lambda e, k=k, kk=kk: e.activation(out=dst_fn(k), in_=banks[pbk][:, kk * 128:(kk + 1) * 128],
                                                                 func=AF.Identity, scale=G(k), bias=SH(k)),
                              r=[PB(pbk), "mods"], w=[dst_res])
                    else:
                        S.dve(lambda e, k=k, kk=kk: e.tensor_scalar(out=dst_fn(k), in0=banks[pbk][:, kk * 128:(kk + 1) * 128],
                                                                    scalar1=G(k), scalar2=SH(k), op0=ALU.mult, op1=ALU.add),
                              r=[PB(pbk), "mods"], w=[dst_res])

            if psb2 is None:
                tr(0, psb)
                ev(0, psb)
                tr(1, psb)
                ev(1, psb)
            else:
                tr(0, psb)
                tr(1, psb2)
                ev(0, psb)
                ev(1, psb2)

        def Xsrc(l, b):
            if l == 0:
                return ctx_in[b * 128:(b + 1) * 128, :] if b < 2 else x_in[(b - 2) * 128:(b - 1) * 128, :]
            return Xs[b * 128:(b + 1) * 128, :]

        def rope_tm(ps_ap, H, npair, cs, sn, out_ap, tmp, res_in, res_out, tmp_res):
            csb = cs.unsqueeze(1).broadcast_to([128, H, npair])
            snb = sn.unsqueeze(1).broadcast_to([128, H, npair])
            ev, od = ps_ap[:, :, :, 0], ps_ap[:, :, :, 1]
            t0 = tmp[:, 0, 0:H * npair].rearrange("p (h i) -> p h i", h=H)
            t1 = tmp[:, 1, 0:H * npair].rearrange("p (h i) -> p h i", h=H)
            S.dve(lambda e: e.tensor_tensor(out=t0, in0=ev, in1=csb, op=ALU.mult), r=res_in, w=[tmp_res + "0"])
            S.dve(lambda e: e.tensor_tensor(out=t1, in0=od, in1=snb, op=ALU.mult), r=res_in, w=[tmp_res + "1"])
            S.dve(lambda e: e.tensor_tensor(out=out_ap[:, :, :, 0], in0=t0, in1=t1, op=ALU.subtract),
                  r=[tmp_res + "0", tmp_res + "1"], w=[res_out])
            S.dve(lambda e: e.tensor_tensor(out=t0, in0=ev, in1=snb, op=ALU.mult), r=res_in, w=[tmp_res + "0"])
            S.dve(lambda e: e.tensor_tensor(out=t1, in0=od, in1=csb, op=ALU.mult), r=res_in, w=[tmp_res + "1"])
            S.dve(lambda e: e.tensor_tensor(out=out_ap[:, :, :, 1], in0=t0, in1=t1, op=ALU.add),
                  r=[tmp_res + "0", tmp_res + "1"], w=[res_out])

        def modulation(l):
            with ExitStack() as ms:
                mw = [sb("mw%d" % i, [128, 8, D], BF16, ms) for i in range(2)]
                modbB = sb("modbB", [128, 2, D], F32, ms)
                mwv = mod_w[l].rearrange("(k p) n -> p k n", p=128)
                psF = banks[7]
                for m in range(6):
                    buf = mw[m % 2]
                    S.dma("pool", "mw%d" % (m % 2),
                          lambda e, m=m, buf=buf: e.dma_start(out=buf[:, :, :], in_=mwv[:, :, m * D:(m + 1) * D]),
                          w=[("mw", m % 2)])
                    for js in range(8):
                        col = (m * 8 + js) * 2
                        for k in range(8):
                            S.pe(lambda e, js=js, k=k, col=col, buf=buf: e.matmul(
                                psF[:, col:col + 2], lhsT=buf[:, k, js * 128:(js + 1) * 128], rhs=scT[:, k, :],
                                start=(k == 0), stop=(k == 7)), r=[("mw", m % 2), "scT"], w=[PB(7)])
                    if m in (2, 5):
                        gi = 0 if m == 2 else 2
                        mi = 0 if m == 2 else 1
                        S.dma("sp", "k6", lambda e, m=m, mi=mi: e.dma_start(
                            out=modbB[:, mi, :], in_=mod_b[l:l + 1, m * D:(m + 1) * D].broadcast_to([128, D])), w=[("modbB", mi)])
                        for j in range(2):
                            for hlf in range(2):
                                pb = 5 + hlf
                                for k in range(8):
                                    S.pe(lambda e, j=j, hlf=hlf, k=k, pb=pb, buf=buf: e.matmul(
                                        banks[pb][:, :], lhsT=screp[:, k, j, :], rhs=buf[:, k, hlf * 512:(hlf + 1) * 512],
                                        start=(k == 0), stop=(k == 7)), r=[("mw", m % 2), "screp"], w=[PB(pb)])
                                S.dve(lambda e, j=j, hlf=hlf, pb=pb, gi=gi, mi=mi: e.tensor_tensor(
                                    out=gateB[:, gi + j, hlf * 512:(hlf + 1) * 512], in0=banks[pb][:, :],
                                    in1=modbB[:, mi, hlf * 512:(hlf + 1) * 512], op=ALU.add),
                                    r=[PB(pb), ("modbB", mi)], w=["gates"])
                S.dve(lambda e: e.tensor_tensor(out=modF[:, :, :], in0=psF[:, 0:96].rearrange("p (f j) -> p f j", j=2),
                                                in1=modbF[:, l * 48:(l + 1) * 48].unsqueeze(2).broadcast_to([128, 48, 2]),
                                                op=ALU.add), r=[PB(7), "params"], w=["mods"])
                for ni, (gt, sc0) in enumerate(((n1g, 8), (n2g, 32))):
                    S.dve(lambda e, ni=ni, sc0=sc0: e.tensor_scalar(out=Gs[:, ni, :, :], in0=modF[:, sc0:sc0 + 8, :], scalar1=1.0,
                                                                   scalar2=None, op0=ALU.add), r=["mods"], w=["mods"])
                    S.dve(lambda e, ni=ni, gt=gt: e.tensor_tensor(
                        out=Gs[:, ni, :, :], in0=Gs[:, ni, :, :],
                        in1=gt[:, l * 8:(l + 1) * 8].unsqueeze(2).broadcast_to([128, 8, 2]), op=ALU.mult),
                        r=["mods", "params"], w=["mods"])
                S.barrier()

        def Gf(ni, j):
            return lambda k: Gs[:, ni, k, j:j + 1]

        def SHf(ni, j):
            base = 0 if ni == 0 else 24
            return lambda k: modF[:, base + k, j:j + 1]

        def proj_T(O_ap, O_res, tl, sl=0):
            OTt = tl["OT"][sl]
            pbT = tl["pbT"]
            psT = banks[pbT][:, :].bitcast(BF16)
            for k in range(8):
                S.pe(lambda e, k=k: e.transpose(out=psT[:, k * 128:(k + 1) * 128], in_=O_ap[:, k * 128:(k + 1) * 128],
                                               identity=identB[:, :]), r=list(O_res) + ["identB"], w=[PB(pbT)])
            S.act(lambda e: e.activation(out=OTt[:, :], in_=psT[:, :], func=AF.Copy), r=[PB(pbT)], w=[("OT", sl)])

        def proj_MM(wo_sb, xb_ap, xb_res, gate_idx, b, tl, sl=0):
            OTt, xnew = tl["OT"][sl], tl["xnew"][sl]
            for hlf in range(2):
                pb = tl["pbo"][hlf]
                for k in range(8):
                    S.pe(lambda e, k=k, hlf=hlf, pb=pb: e.matmul(banks[pb][:, :], lhsT=OTt[:, k * 128:(k + 1) * 128],
                                                               rhs=wo_sb[:, k, hlf * 512:(hlf + 1) * 512],
                                                               start=(k == 0), stop=(k == 7)), r=[("OT", sl), "wo"], w=[PB(pb)])
                S.dve(lambda e, hlf=hlf, pb=pb: e.tensor_tensor(out=xnew[:, hlf * 512:(hlf + 1) * 512], in0=banks[pb][:, :],
                                                              in1=gateB[:, gate_idx, hlf * 512:(hlf + 1) * 512], op=ALU.mult),
                      r=[PB(pb), "gates"], w=[("xnew", sl, hlf)])
                S.pool(lambda e, hlf=hlf: e.tensor_tensor(out=xnew[:, hlf * 512:(hlf + 1) * 512], in0=xnew[:, hlf * 512:(hlf + 1) * 512],
                                                         in1=xb_ap[:, hlf * 512:(hlf + 1) * 512], op=ALU.add),
                       r=[("xnew", sl, hlf)] + list(xb_res), w=[("xnew", sl, hlf)])
            S.dma("sp", "xst%d" % sl, lambda e: e.dma_start(out=Xs[b * 128:(b + 1) * 128, :], in_=xnew[:, :]),
                  r=[("xnew", sl, 0), ("xnew", sl, 1)], w=[("X", b)])

        def proj_residual(O_ap, O_res, wo_sb, xb_ap, xb_res, gate_idx, b, tl):
            proj_T(O_ap, O_res, tl, 0)
            proj_MM(wo_sb, xb_ap, xb_res, gate_idx, b, tl, 0)

        def phase_A(l):
            ja = l // 2
            with_ctx = l < DEPTH - 1
            S.raw("act", lambda e: e.preload_act_table(AF.Exp))
            with ExitStack() as ps_:
                wqkv = sb("wqkv", [128, 8, 1536], BF16, ps_)
                wo_sb = sb("wo", [128, 8, D], BF16, ps_)
                KT = sb("KT", [128, 4, U], BF16, ps_)
                Vaug = sb("Vaug", [128, NB, 4, 65], BF16, ps_)
                xbr = sb("xbr", [128, 4, D], F32, ps_)
                cosA = sb("cosA", [128, 32, 32], F32, ps_)
                sinA = sb("sinA", [128, 32, 32], F32, ps_)
                S.dma("sp", "k6", lambda e: e.dma_start(out=cosA[:, :, :], in_=cosA_in.rearrange("(b p) i -> p b i", p=128)), w=["cosA"])
                S.dma("sp", "k7", lambda e: e.dma_start(out=sinA[:, :, :], in_=sinA_in.rearrange("(b p) i -> p b i", p=128)), w=["sinA"])
                hT = sb("hT", [128, 2, 8, 128], BF16, ps_)
                qr = sb("qr", [128, 16, 32, 2], BF16, ps_)
                krd = sb("krd", [128, 4, 2, 64], BF16, ps_)
                rtmp = sb("rtmp", [128, 2, 256], F32, ps_)
                QT = sb("QT", [128, 3, 16, 128], BF16, ps_)
                PT = sb("PT", [128, 2, 5, 4, 128], BF16, ps_)
                Osb = sb("Osb", [128, D], BF16, ps_)
                den = sb("den", [128, 2, 4], F32, ps_)
                OTt = sb("OTt", [128, D], BF16, ps_)
                xnew = sb("xnew", [128, D], F32, ps_)
                tl = {"OT": [OTt], "xnew": [xnew], "pbT": 6, "pbo": (7, 6)}

                wv = a_wqkv[ja].rearrange("(k p) n -> p k n", p=128)
                for i in range(3):
                    S.dma("pool", "wA%d" % i, lambda e, i=i: e.dma_start(out=wqkv[:, :, i * 512:(i + 1) * 512], in_=wv[:, :, i * 512:(i + 1) * 512]),
                          w=["wqkv"])
                wov = a_wo[ja].rearrange("(k p) n -> p k n", p=128)
                for i in range(2):
                    S.dma("pool", "wA%d" % i, lambda e, i=i: e.dma_start(out=wo_sb[:, :, i * 512:(i + 1) * 512], in_=wov[:, :, i * 512:(i + 1) * 512]),
                          w=["wo"])
                S.pool(lambda e: e.memset(Vaug[:, :, :, 64:65], 1.0), w=["Vones"])
                for qs0 in range(3):
                    S.pool(lambda e, qs0=qs0: e.memset(QT[:, qs0, :, :], 0.0), w=[("QT", qs0)])
                S.act(lambda e: e.activation(out=esink[:, :], in_=sinkB[:, ja * 16:(ja + 1) * 16], func=AF.Exp), r=["sinkB"], w=["esink"])

                def xloadA(b):
                    S.dma("sp", "xld%d" % (b % 2), lambda e: e.dma_start(out=xbr[:, b % 4, :], in_=Xsrc(l, b)),
                          r=[("X", b)], w=[("xb", b % 4)])

                xloadA(0)

                def stage1(b):
                    j = 1 if b < 2 else 0
                    xs_ = b % 4
                    if b + 1 < NB:
                        xloadA(b + 1)
                    hs = b % 2
                    normT(xbr[:, xs_, :], [("xb", xs_)], Gf(0, j), SHf(0, j), lambda k: hT[:, hs, k, :], ("hT", hs), 0)
                    need_q = (b >= 2) or with_ctx

                    def piece(pc, pb):
                        for k in range(8):
                            S.pe(lambda e, k=k: e.matmul(banks[pb][:, :], lhsT=hT[:, hs, k, :], rhs=wqkv[:, k, pc * 512:(pc + 1) * 512],
                                                         start=(k == 0), stop=(k == 7)), r=[("hT", hs), "wqkv"], w=[PB(pb)])

                    piece(2, 1)
                    if b >= 2:
                        blk = b - 2
                        rope_tm(banks[1][:, 0:256].rearrange("p (h i t) -> p h i t", h=4, t=2), 4, 32,
                                cosA[:, blk, :], sinA[:, blk, :],
                                krd[:, :, 0, :].rearrange("p h (i t) -> p h i t", t=2), rtmp,
                                [PB(1), "cosA", "sinA"], "krd0", "rtmp")
                    else:
                        S.act(lambda e: e.activation(out=krd[:, :, 0, :], in_=banks[1][:, 0:256].rearrange("p (h d) -> p h d", h=4),
                                                     func=AF.Copy), r=[PB(1)], w=["krd0"])
                    S.act(lambda e: e.activation(out=Vaug[:, b, :, 0:64], in_=banks[1][:, 256:512].rearrange("p (h d) -> p h d", h=4),
                                                 func=AF.Copy), r=[PB(1)], w=[("V", b)])
                    S.pool(lambda e: e.tensor_copy(out=krd[:, :, 1, :], in_=krd[:, :, 0, :]), r=["krd0"], w=["krd1"])
                    psk = banks[0][:, :].bitcast(BF16)
                    kflat = krd[:, :, :, :].rearrange("p h c d -> p (h c d)")
                    for g in range(4):
                        S.pe(lambda e, g=g: e.transpose(out=psk[:, g * 128:(g + 1) * 128], in_=kflat[:, g * 128:(g + 1) * 128],
                                                       identity=identB[:, :]), r=["krd0", "krd1", "identB"], w=[PB(0)])
                    S.dve(lambda e: e.tensor_copy(out=KT[:, :, b * 128:(b + 1) * 128],
                                                  in_=psk[:, 0:512].rearrange("p (g t) -> p g t", g=4)), r=[PB(0)], w=[("K", b)])
                    if not need_q:
                        return
                    for pc, pb in ((0, 2), (1, 1)):
                        piece(pc, pb)
                        if b >= 2:
                            rope_tm(banks[pb][:, :].rearrange("p (h i t) -> p h i t", h=8, t=2), 8, 32,
                                    cosA[:, b - 2, :], sinA[:, b - 2, :], qr[:, pc * 8:(pc + 1) * 8, :, :], rtmp,
                                    [PB(pb), "cosA", "sinA"], ("qr", pc), "rtmp")
                        else:
                            S.act(lambda e, pc=pc, pb=pb: e.activation(out=qr[:, pc * 8:(pc + 1) * 8, :, :].rearrange("p h i t -> p (h i t)"),
                                                                      in_=banks[pb][:, :], func=AF.Copy), r=[PB(pb)], w=[("qr", pc)])
                    psq = banks[3][:, :].bitcast(BF16)
                    qs = b % 3
                    qflat = qr[:, :, :, :].rearrange("p h i t -> p (h i t)")
                    for k in range(8):
                        S.pe(lambda e, k=k: e.transpose(out=psq[:, k * 128:(k + 1) * 128], in_=qflat[:, k * 128:(k + 1) * 128],
                                                       identity=identB[:, :]), r=[("qr", k // 4), "identB"], w=[PB(3)])
                    S.act(lambda e: e.activation(out=QT[0:64, qs, 0:16:2, :], in_=psq[0:64, :].rearrange("p (k t) -> p k t", k=8), func=AF.Copy),
                          r=[PB(3)], w=[("QT", qs, 0)])
                    S.dve(lambda e: e.tensor_copy(out=QT[64:128, qs, 1:16:2, :], in_=psq[64:128, :].rearrange("p (k t) -> p k t", k=8)),
                          r=[PB(3)], w=[("QT", qs, 1)])

                def stage2(b):
                    j = 1 if b < 2 else 0
                    qs = b % 3
                    if b < 2:
                        keys = [(0, None), (1, None)]
                    else:
                        keys = [(0, None), (1, None)]
                        if b - 1 >= 2:
                            keys.append((b - 1, "P"))
                        keys.append((b, None))
                        if b + 1 < NB:
                            keys.append((b + 1, "N"))
                    nk = len(keys)
                    def qk_exp(g):
                        ps_ = (b * 4 + g) % 2
                        pt_res = ("PT", ps_)
                        for ji, (kb, mk) in enumerate(keys):
                            pb = 4 + ji % 2
                            S.pe(lambda e, kb=kb, pb=pb: e.matmul(
                                banks[pb][:, :].rearrange("p (c t) -> p c t", c=4),
                                lhsT=KT[:, g, kb * 128:(kb + 1) * 128], rhs=QT[:, qs, 4 * g:4 * g + 4, :], start=True, stop=True),
                                r=[("K", kb), ("QT", qs), ("QT", qs, 0), ("QT", qs, 1)], w=[PB(pb)])
                            S.act(lambda e, pb=pb, ji=ji: e.activation(
                                out=PT[:, ps_, ji, :, :], in_=banks[pb][:, :].rearrange("p (c t) -> p c t", c=4), func=AF.Exp, scale=0.125),
                                r=[PB(pb)], w=[(pt_res, ji)])
                            if mk is not None:
                                mt = maskP if mk == "P" else maskN
                                S.pool(lambda e, ji=ji, mt=mt: e.tensor_tensor(
                                    out=PT[:, ps_, ji, :, :], in0=PT[:, ps_, ji, :, :],
                                    in1=mt[:, :].unsqueeze(1).broadcast_to([128, 4, 128]), op=ALU.mult),
                                    r=[(pt_res, ji), "maskP", "maskN"], w=[(pt_res, ji)])

                    def pv_norm(g):
                        ps_ = (b * 4 + g) % 2
                        pt_res = ("PT", ps_)
                        ob = 6
                        for s in range(4):
                            for ji, (kb, mk) in enumerate(keys):
                                S.pe(lambda e, s=s, ji=ji, kb=kb: e.matmul(
                                    banks[ob][:, s * 65:(s + 1) * 65], lhsT=PT[:, ps_, ji, s, :], rhs=Vaug[:, kb, g, :],
                                    start=(ji == 0), stop=(ji == nk - 1)),
                                    r=[(pt_res, ji), ("V", kb), "Vones"], w=[PB(ob)])
                        ov = banks[ob][:, 0:260].rearrange("p (s d) -> p s d", s=4)
                        S.dve(lambda e: e.tensor_tensor(out=den[:, ps_, :], in0=ov[:, :, 64], in1=esink[:, 4 * g:4 * g + 4], op=ALU.add),
                              r=[PB(ob), "esink"], w=[("den", ps_)])
                        S.dve(lambda e: e.reciprocal(out=den[:, ps_, :], in_=den[:, ps_, :]), r=[("den", ps_)], w=[("den", ps_)])
                        Ov = Osb[:, :].rearrange("p (h d) -> p h d", d=64)
                        S.dve(lambda e: e.tensor_tensor(
                            out=Ov[:, 4 * g:4 * g + 4, :], in0=ov[:, :, 0:64],
                            in1=den[:, ps_, :].unsqueeze(2).broadcast_to([128, 4, 64]), op=ALU.mult),
                            r=[PB(ob), ("den", ps_)], w=["Osb"])

                    qk_exp(0)
                    for g in range(4):
                        if g + 1 < 4:
                            qk_exp(g + 1)
                        pv_norm(g)
                    proj_residual(Osb, ["Osb"], wo_sb, xbr[:, b % 4, :], [("xb", b % 4)], 0 + j, b, tl)

                stage1(0)
                stage1(1)
                if with_ctx:
                    stage2(0)
                    stage2(1)
                stage1(2)
                stage1(3)
                for b in range(2, NB):
                    S.rec()
                    stage2(b)
                    r2 = S.end_rec()
                    r1 = None
                    if b + 2 < NB:
                        S.rec()
                        stage1(b + 2)
                        r1 = S.end_rec()
                    S.play([r2, r1])
                S.barrier()

        def phase_B(l):
            jb = l // 2
            with_ctx = l < DEPTH - 1
            b0 = 0 if with_ctx else 2
            scale = 96.0 ** -0.5
            S.raw("act", lambda e: e.preload_act_table(AF.Exp))
            with ExitStack() as pB:
                ckvT = sb("ckvT", [128, 2, U], BF16, pB)
                Kb = sb("Kb", [128, 2, U], BF16, pB)
                wuk = sb("wuk", [128, 2, D], BF16, pB)
                wuv = sb("wuv", [128, 2, D], BF16, pB)
                wo_sb = sb("wo", [128, 8, D], BF16, pB)
                with ExitStack() as p1:
                    wdn = sb("wdn", [128, 8, 800], BF16, p1)
                    wuq = sb("wuq", [128, 4, 1536], BF16, p1)
                    xbr = sb("xbr", [128, 2, D], F32, p1)
                    hT = sb("hT", [128, 2, 8, 128], BF16, p1)
                    cqn = sb("cqn", [128, 512], F32, p1)
                    ckn = sb("ckn", [128, 256], F32, p1)
                    cqT = sb("cqT", [128, 2, 4, 128], BF16, p1)
                    rtmp1 = sb("rtmp1", [128, 2, 16], F32, p1)
                    krs = sb("krs", [128, 96], F32, p1)
                    rtmp = sb("rtmp", [128, 2, 256], F32, p1)
                    qf = sb("qf", [128, 16, 32], F32, p1)
                    qrb = sb("qrb", [128, 16, 96], BF16, p1)
                    QTa = sb("QTa", [128, 2, 16, 512], BF16, p1)
                    cosB = sb("cosB", [128, 32, 16], F32, p1)
                    sinB = sb("sinB", [128, 32, 16], F32, p1)
                    S.dma("sp", "k6", lambda e: e.dma_start(out=cosB[:, :, :], in_=cosB_in.rearrange("(b p) i -> p b i", p=128)), w=["cosB"])
                    S.dma("sp", "k7", lambda e: e.dma_start(out=sinB[:, :, :], in_=sinB_in.rearrange("(b p) i -> p b i", p=128)), w=["sinB"])
                    wdv = b_wdown[jb].rearrange("(k p) n -> p k n", p=128)
                    S.dma("pool", "wB0", lambda e: e.dma_start(out=wdn[:, :, 0:400], in_=wdv[:, :, 0:400]), w=["wdn"])
                    S.dma("pool", "wB1", lambda e: e.dma_start(out=wdn[:, :, 400:800], in_=wdv[:, :, 400:800]), w=["wdn"])
                    wqv = b_wuq[jb].rearrange("(k p) n -> p k n", p=128)
                    for i in range(3):
                        S.dma("pool", "wB%d" % (i % 2), lambda e, i=i: e.dma_start(out=wuq[:, :, i * 512:(i + 1) * 512], in_=wqv[:, :, i * 512:(i + 1) * 512]),
                              w=["wuq"])
                    S.dve(lambda e: e.memset(krs[:, :], 0.0), w=["krs"])
                    S.dma("pool", "wB2", lambda e: e.dma_start(out=wuk[:, :, :], in_=b_wuk[jb].rearrange("(k p) n -> p k n", p=128)), w=["wuk"])
                    S.dma("pool", "wB3", lambda e: e.dma_start(out=wuv[:, :, :], in_=b_wuv[jb].rearrange("(k p) n -> p k n", p=128)), w=["wuv"])
                    wov = b_wo[jb].rearrange("(k p) n -> p k n", p=128)
                    for i in range(2):
                        S.dma("pool", "wB%d" % (4 + i), lambda e, i=i: e.dma_start(out=wo_sb[:, :, i * 512:(i + 1) * 512], in_=wov[:, :, i * 512:(i + 1) * 512]),
                              w=[("wo", i)])

                    def qstore(grp, nblk, u0):
                        qa = grp % 2
                        for hh in range(2):
                            S.dma("sp", "qst%d" % hh, lambda e, hh=hh: e.dma_start(
                                out=qT_d[hh * 8:(hh + 1) * 8, :, u0:u0 + nblk * 128].rearrange("h r u -> r h u"),
                                in_=QTa[0:96, qa, hh * 8:(hh + 1) * 8, 0:nblk * 128]), r=[("QTa", qa)], w=[("qT", grp)])

                    def xload(b):
                        S.dma("sp", "xld%d" % (b % 2), lambda e, b=b: e.dma_start(out=xbr[:, b % 2, :], in_=Xsrc(l, b)),
                              r=[("X", b)], w=[("xb", b % 2)])

                    def front(b):
                        j = 1 if b < 2 else 0
                        xs_ = b % 2
                        if b + 1 < NB:
                            xload(b + 1)
                        hs = b % 2
                        normT(xbr[:, xs_, :], [("xb", xs_)], Gf(0, j), SHf(0, j), lambda k, hs=hs: hT[:, hs, k, :], ("hT", hs), 0)
                        for pc, (c0, c1) in enumerate(((0, 512), (512, 800))):
                            pb = 1 + pc
                            for k in range(8):
                                S.pe(lambda e, k=k, pb=pb, c0=c0, c1=c1, hs=hs: e.matmul(
                                    banks[pb][:, 0:c1 - c0], lhsT=hT[:, hs, k, :], rhs=wdn[:, k, c0:c1],
                                    start=(k == 0), stop=(k == 7)), r=[("hT", hs), "wdn"], w=[PB(pb)])
                        need_q = (b >= 2) or with_ctx
                        rs, rn = row_rstd(banks[2][:, 0:256], [PB(2)], 256)
                        S.act(lambda e, rs=rs: e.activation(out=ckn[:, :], in_=banks[2][:, 0:256], func=AF.Copy, scale=rs),
                              r=[PB(2), rn], w=["ckn"])
                        if b >= 2:
                            blk = b - 2
                            rope_tm(banks[2][:, 256:288].rearrange("p (h i t) -> p h i t", h=1, t=2), 1, 16,
                                    cosB[:, blk, :], sinB[:, blk, :],
                                    krs[:, 64:96].rearrange("p (h i t) -> p h i t", h=1, t=2), rtmp1,
                                    [PB(2), "cosB", "sinB"], "krs", "rtmp1")
                        else:
                            S.dve(lambda e: e.tensor_copy(out=krs[:, 64:96], in_=banks[2][:, 256:288]), r=[PB(2)], w=["krs"])
                        for k in range(2):
                            S.pe(lambda e, k=k: e.transpose(out=banks[3][:, k * 128:(k + 1) * 128], in_=ckn[:, k * 128:(k + 1) * 128],
                                                           identity=identF[:, :]), r=["ckn", "identF"], w=[PB(3)])
                        S.pe(lambda e: e.transpose(out=banks[3][0:96, 256:384], in_=krs[:, :], identity=identF[:, :]),
                             r=["krs", "identF"], w=[PB(3)])
                        for k in range(2):
                            S.dve(lambda e, k=k, b=b: e.tensor_scalar(out=ckvT[:, k, b * 128:(b + 1) * 128], in0=banks[3][:, k * 128:(k + 1) * 128],
                                                                     scalar1=kvg[:, jb * 2 + k:jb * 2 + k + 1], scalar2=None, op0=ALU.mult),
                                  r=[PB(3), "params"], w=[("ckvT", b)])
                        for kk in range(2):
                            S.act(lambda e, kk=kk, b=b: e.activation(out=Kb[64:96, kk, b * 128:(b + 1) * 128], in_=banks[3][64:96, 256:384],
                                                                    func=AF.Copy), r=[PB(3)], w=[("KR", b)])
                        if not need_q:
                            return
                        cs_ = b % 2
                        rs2, rn2 = row_rstd(banks[1][:, :], [PB(1)], 512)
                        S.act(lambda e, rs2=rs2: e.activation(out=cqn[:, :], in_=banks[1][:, :], func=AF.Copy, scale=rs2),
                              r=[PB(1), rn2], w=["cqn"])
                        for k in range(4):
                            S.pe(lambda e, k=k: e.transpose(out=banks[4][:, k * 128:(k + 1) * 128], in_=cqn[:, k * 128:(k + 1) * 128],
                                                           identity=identF[:, :]), r=["cqn", "identF"], w=[PB(4)])
                        for k in range(4):
                            S.dve(lambda e, k=k, cs_=cs_: e.tensor_scalar(out=cqT[:, cs_, k, :], in0=banks[4][:, k * 128:(k + 1) * 128],
                                                                scalar1=qng[:, jb * 4 + k:jb * 4 + k + 1], scalar2=None, op0=ALU.mult),
                                  r=[PB(4), "params"], w=[("cqT", cs_)])

                    def back(b):
                        cs_ = b % 2
                        for pc in range(4):
                            pb = 5 + pc % 2
                            for k in range(4):
                                S.pe(lambda e, k=k, pb=pb, pc=pc, cs_=cs_: e.matmul(banks[pb][:, 0:384], lhsT=cqT[:, cs_, k, :],
                                                                         rhs=wuq[:, k, pc * 384:(pc + 1) * 384],
                                                                         start=(k == 0), stop=(k == 3)), r=[("cqT", cs_), "wuq"], w=[PB(pb)])
                            pv = banks[pb][:, 0:384].rearrange("p (h d) -> p h d", h=4)
                            if b >= 2:
                                S.act(lambda e, pv=pv, pc=pc: e.activation(out=qrb[:, pc * 4:(pc + 1) * 4, 0:64], in_=pv[:, :, 0:64], func=AF.Copy),
                                      r=[PB(pb)], w=[("qrb", pc, 0)])
                                S.act(lambda e, pv=pv, pc=pc: e.activation(out=qf[:, pc * 4:(pc + 1) * 4, :], in_=pv[:, :, 64:96], func=AF.Copy),
                                      r=[PB(pb)], w=[("qf", pc)])
                            else:
                                S.act(lambda e, pv=pv, pc=pc: e.activation(out=qrb[:, pc * 4:(pc + 1) * 4, :], in_=pv, func=AF.Copy),
                                      r=[PB(pb)], w=[("qrb", pc, 0), ("qrb", pc, 1)])
                        if b >= 2:
                            rope_tm(qf[:, :, :].rearrange("p h (i t) -> p h i t", t=2), 16, 16,
                                    cosB[:, b - 2, :], sinB[:, b - 2, :],
                                    qrb[:, :, 64:96].rearrange("p h (i t) -> p h i t", t=2), rtmp,
                                    [("qf", 0), ("qf", 1), ("qf", 2), ("qf", 3), "cosB", "sinB"], ("qrb", "rope"), "rtmp")
                        if b < 2:
                            grp, bi = 0, b
                        else:
                            grp, bi = 1 + (b - 2) // 4, (b - 2) % 4
                        qa = grp % 2
                        for hh in range(2):
                            pb = 7
                            pst = banks[pb][:, :].bitcast(BF16)
                            for h8 in range(8):
                                h = hh * 8 + h8
                                S.pe(lambda e, h=h, h8=h8, pst=pst: e.transpose(out=pst[0:96, h8 * 128:(h8 + 1) * 128], in_=qrb[:, h, :],
                                                                               identity=identB[:, :]),
                                     r=[("qrb", h // 4, 0), ("qrb", h // 4, 1), ("qrb", "rope"), "identB"], w=[PB(pb)])
                            S.act(lambda e, hh=hh, pst=pst, qa=qa, bi=bi: e.activation(
                                out=QTa[0:96, qa, hh * 8:(hh + 1) * 8, bi * 128:(bi + 1) * 128],
                                in_=pst[0:96, :].rearrange("p (h t) -> p h t", h=8), func=AF.Copy), r=[PB(pb)], w=[("QTa", qa)])
                        if b == 1:
                            qstore(0, 2, 0)
                        elif b >= 2 and bi == 3:
                            qstore(grp, 4, 256 + (grp - 1) * 512)

                    xload(0)
                    recF, recB = {}, {}
                    for b in range(NB):
                        S.rec()
                        front(b)
                        recF[b] = S.end_rec()
                        if (b >= 2) or with_ctx:
                            S.rec()
                            back(b)
                            recB[b] = S.end_rec()
                    S.play([recF[0]])
                    for b in range(NB):
                        S.play([recB.get(b), recF.get(b + 1)])
                    S.barrier()
                with ExitStack() as p2:
                    Oall = sb("Oall", [128, NB, D], BF16, p2)
                    with ExitStack() as p2a:
                        Qb = sb("Qb", [128, 2, U], BF16, p2a)
                        Vb = sb("Vb", [128, 2, NB, 65], BF16, p2a)
                        PTr = sb("PTr", [128, 6, 512], BF16, p2a)
                        rec = sb("rec", [128, 2, 4], F32, p2a)
                        S.pool(lambda e: e.memset(Vb[:, :, :, 64:65], 1.0), w=["Vbones"])
                        ptc = [0]

                        def build_kv(h):
                            hb = h % 2
                            S.dma("sp", "qld%d" % hb, lambda e, h=h, hb=hb: e.dma_start(out=Qb[0:96, hb, :], in_=qT_d[h, :, :]),
                                  r=[("qT", g_) for g_ in range(9)], w=[("Qb", hb)])
                            for kt in range(9):
                                u0 = kt * 512
                                n = min(512, U - u0)
                                pb = 1
                                for k in range(2):
                                    S.pe(lambda e, k=k, u0=u0, n=n, pb=pb, h=h: e.matmul(
                                        banks[pb][0:64, 0:n], lhsT=wuk[:, k, h * 64:(h + 1) * 64], rhs=ckvT[:, k, u0:u0 + n],
                                        start=(k == 0), stop=(k == 1)), r=["wuk"] + [("ckvT", bb) for bb in range(u0 // 128, (u0 + n) // 128)],
                                        w=[PB(pb)])
                                S.dve(lambda e, u0=u0, n=n, pb=pb, hb=hb: e.tensor_copy(out=Kb[0:64, hb, u0:u0 + n], in_=banks[pb][0:64, 0:n]),
                                      r=[PB(pb)], w=[("Kn", hb)])
                            for vg in range(5):
                                kb0 = vg * 7
                                nkb = min(7, NB - kb0)
                                pb = 3
                                for kk in range(nkb):
                                    kb = kb0 + kk
                                    for k in range(2):
                                        S.pe(lambda e, k=k, kb=kb, kk=kk, pb=pb, h=h: e.matmul(
                                            banks[pb][:, kk * 64:(kk + 1) * 64], lhsT=ckvT[:, k, kb * 128:(kb + 1) * 128],
                                            rhs=wuv[:, k, h * 64:(h + 1) * 64], start=(k == 0), stop=(k == 1)),
                                            r=["wuv", ("ckvT", kb)], w=[PB(pb)])
                                S.dve(lambda e, kb0=kb0, nkb=nkb, pb=pb, hb=hb: e.tensor_copy(
                                    out=Vb[:, hb, kb0:kb0 + nkb, 0:64], in_=banks[pb][:, 0:nkb * 64].rearrange("p (k d) -> p k d", d=64)),
                                    r=[PB(pb)], w=[("Vb", hb)])

                        qtiles = ([(0, 2, [0, 1])] if with_ctx else []) + [(2 + 4 * t, 4, list(range(NB))) for t in range(8)]

                        def attention(h):
                            hb = h % 2
                            items = []
                            for qi, (qb0, nqb, kbs) in enumerate(qtiles):
                                for ki, kb in enumerate(kbs):
                                    items.append((qi, qb0, nqb, ki, kb, len(kbs)))
                            slots = {}

                            def stA(i):
                                qi, qb0, nqb, ki, kb, nk = items[i]
                                nq = nqb * 128
                                sbk = (7, 0, 2, 4)[ptc[0] % 4]
                                pslot = ptc[0] % 6
                                ptc[0] += 1
                                slots[i] = pslot
                                S.pe(lambda e: e.matmul(
                                    banks[sbk][:, 0:nq], lhsT=Kb[0:96, hb, kb * 128:(kb + 1) * 128],
                                    rhs=Qb[0:96, hb, qb0 * 128:qb0 * 128 + nq], start=True, stop=True),
                                    r=[("Kn", hb), ("KR", kb), ("Qb", hb)], w=[PB(sbk)])
                                S.act(lambda e: e.activation(out=PTr[:, pslot, 0:nq], in_=banks[sbk][:, 0:nq], func=AF.Exp, scale=scale),
                                      r=[PB(sbk)], w=[("PTr", pslot)])

                            def stB(i):
                                qi, qb0, nqb, ki, kb, nk = items[i]
                                ob = 5 + (h * 9 + qi) % 2
                                pslot = slots[i]
                                if ki == 0:
                                    S.pe(lambda e: e.matmul(banks[ob][:, 0:nqb * 65], lhsT=zerosB[:, 0:128], rhs=zerosB[:, 0:nqb * 65],
                                                            start=True, stop=True), r=["zerosB"], w=[PB(ob)])
                                for qq in range(nqb):
                                    S.pe(lambda e, qq=qq: e.matmul(
                                        banks[ob][:, qq * 65:(qq + 1) * 65], lhsT=PTr[:, pslot, qq * 128:(qq + 1) * 128],
                                        rhs=Vb[:, hb, kb, :], start=False, stop=True, skip_group_check=True),
                                        r=[("PTr", pslot), ("Vb", hb), "Vbones"], w=[PB(ob)])
                                if ki == nk - 1:
                                    rs_ = (h * 9 + qi) % 2
                                    ov = banks[ob][:, 0:nqb * 65].rearrange("p (q d) -> p q d", d=65)
                                    S.dve(lambda e: e.reciprocal(out=rec[:, rs_, 0:nqb], in_=ov[:, :, 64]), r=[PB(ob)], w=[("rec", rs_)])
                                    S.dve(lambda e: e.tensor_tensor(
                                        out=Oall[:, qb0:qb0 + nqb, h * 64:(h + 1) * 64], in0=ov[:, :, 0:64],
                                        in1=rec[:, rs_, 0:nqb].unsqueeze(2).broadcast_to([128, nqb, 64]), op=ALU.mult),
                                        r=[PB(ob), ("rec", rs_)], w=["Oall"])

                            n_it = len(items)
                            for i in range(min(3, n_it)):
                                stA(i)
                            for i in range(n_it):
                                if i + 3 < n_it:
                                    stA(i + 3)
                                stB(i)

                        build_kv(0)
                        for h in range(16):
                            S.rec()
                            attention(h)
                            ra = S.end_rec()
                            rb = None
                            if h + 1 < 16:
                                S.rec()
                                build_kv(h + 1)
                                rb = S.end_rec()
                            S.play([ra, rb])
                        S.barrier()
                    with ExitStack() as p3:
                        xbr = sb("xbr", [128, 2, D], F32, p3)
                        OTt = sb("OTt", [128, 2, D], BF16, p3)
                        xnew = sb("xnew", [128, 2, D], F32, p3)
                        tl = {"OT": [OTt[:, 0, :], OTt[:, 1, :]], "xnew": [xnew[:, 0, :], xnew[:, 1, :]], "pbT": 4, "pbo": (1, 2)}
                        def xload3(b):
                            S.dma("sp", "xld%d" % (b % 2), lambda e, b=b: e.dma_start(out=xbr[:, b % 2, :], in_=Xsrc(l, b)),
                                  r=[("X", b)], w=[("xb", b % 2)])

                        xload3(b0)
                        proj_T(Oall[:, b0, :], [], tl, b0 % 2)
                        for b in range(b0, NB):
                            j = 1 if b < 2 else 0
                            xs_ = b % 2
                            if b + 1 < NB:
                                xload3(b + 1)
                                proj_T(Oall[:, b + 1, :], [], tl, (b + 1) % 2)
                            proj_MM(wo_sb, xbr[:, xs_, :], [("xb", xs_)], 0 + j, b, tl, b % 2)
                        S.barrier()

        def phase_F(l):
            with_ctx = l < DEPTH - 1
            last = l == DEPTH - 1
            S.raw("act", lambda e: e.preload_act_table(AF.Silu))
            with ExitStack() as pf:
                wout = sb("wout", [128, NCF, D], BF16, pf)
                wr = sb("wr", [128, 4, 8, 256], BF16, pf)
                h2T = sb("h2T", [128, 2, 8, 512], BF16, pf)
                halo = sb("halo", [128, 3, 8, 2], BF16, pf)
                gT = sb("gT", [128, NCF, 512], BF16, pf)
                xbr4 = sb("xbr4", [128, 4, D], F32, pf)
                xn4 = sb("xn4", [128, 4, D], F32, pf)
                xb3 = sb("xb3", [128, 2, D], F32, pf)
                acc = sb("acc", [128, 2, 512], F32, pf)
                sl = sb("sl", [128, 2, 512], F32, pf)
                xnew = sb("xnew", [128, 2, D], F32, pf)
                yo = sb("yo", [128, 2, D], F32, pf)
                wov = f_wout[l].rearrange("(c p) n -> p c n", p=128)
                for i in range(2):
                    S.dma("pool", "wF%d" % i, lambda e, i=i: e.dma_start(out=wout[:, i * 11:(i + 1) * 11, :], in_=wov[:, i * 11:(i + 1) * 11, :]),
                          w=["wout"])
                wiv = f_win[l].rearrange("(k p) n -> p k n", p=128)
                tiles = ([(0, 2)] if with_ctx else []) + [(2 + 4 * t, 4) for t in range(8)]
                wc = [0]
                xc = [0]

                def s1a(ti):
                    b0_, nb_ = tiles[ti]
                    lat_first = (b0_ == 2)
                    lat_last = (b0_ + nb_ == NB)
                    if b0_ < 2 or lat_first:
                        S.pool(lambda e: e.memset(halo[:, ti % 3, :, 0:1], 0.0), w=[("halo", ti % 3, 0)])
                    if b0_ < 2 or lat_last:
                        S.pool(lambda e: e.memset(halo[:, ti % 3, :, 1:2], 0.0), w=[("halo", ti % 3, 1)])
                    st = []
                    for bi in range(nb_):
                        b = b0_ + bi
                        S.dma("sp", "xld%d" % bi, lambda e, b=b, bi=bi: e.dma_start(out=xbr4[:, bi, :], in_=Xs[b * 128:(b + 1) * 128, :]),
                              r=[("X", b)], w=[("xb", bi)])
                    for bi in range(nb_):
                        st.append(row_rstd(xbr4[:, bi, :], [("xb", bi)], D))
                    for bi in range(nb_):
                        rs, rn = st[bi]
                        S.act(lambda e, bi=bi, rs=rs: e.activation(out=xn4[:, bi, :], in_=xbr4[:, bi, :], func=AF.Copy, scale=rs),
                              r=[("xb", bi), rn], w=[("xn4", bi)])

                def s1b(ti, bis):
                    b0_, nb_ = tiles[ti]
                    hs = ti % 2
                    lat_first = (b0_ == 2)
                    lat_last = (b0_ + nb_ == NB)
                    for bi in bis:
                        b = b0_ + bi
                        j = 1 if b < 2 else 0
                        G, SH = Gf(1, j), SHf(1, j)
                        for hlf in range(2):
                            pbk = (0, 6)[hlf]
                            for kk in range(4):
                                k = hlf * 4 + kk
                                S.pe(lambda e, k=k, kk=kk, pbk=pbk, bi=bi: e.transpose(out=banks[pbk][:, kk * 128:(kk + 1) * 128],
                                                                                   in_=xn4[:, bi, k * 128:(k + 1) * 128], identity=identF[:, :]),
                                     r=[("xn4", bi), "identF"], w=[PB(pbk)])
                        for hlf in range(2):
                            pbk = (0, 6)[hlf]
                            for kk in range(4):
                                k = hlf * 4 + kk
                                dst = h2T[:, hs, k, bi * 128:(bi + 1) * 128]
                                if hlf == 1:
                                    S.act(lambda e, k=k, kk=kk, pbk=pbk, dst=dst, G=G, SH=SH: e.activation(
                                        out=dst, in_=banks[pbk][:, kk * 128:(kk + 1) * 128], func=AF.Identity, scale=G(k), bias=SH(k)),
                                        r=[PB(pbk), "mods"], w=[("h2T", hs, bi)])
                                else:
                                    S.dve(lambda e, k=k, kk=kk, pbk=pbk, dst=dst, G=G, SH=SH: e.tensor_scalar(
                                        out=dst, in0=banks[pbk][:, kk * 128:(kk + 1) * 128], scalar1=G(k), scalar2=SH(k),
                                        op0=ALU.mult, op1=ALU.add), r=[PB(pbk), "mods"], w=[("h2T", hs, bi)])
                        if b >= 2 and bi == 0 and not lat_first:
                            S.pool(lambda e: e.tensor_copy(out=halo[:, (ti - 1) % 3, :, 1:2], in_=h2T[:, hs, :, 0:1]),
                                   r=[("h2T", hs, 0)], w=[("halo", (ti - 1) % 3, 1)])
                        if b >= 2 and bi == nb_ - 1 and not lat_last:
                            S.pool(lambda e: e.tensor_copy(out=halo[:, (ti + 1) % 3, :, 0:1],
                                                           in_=h2T[:, hs, :, nb_ * 128 - 1:nb_ * 128]),
                                   r=[("h2T", hs, nb_ - 1)], w=[("halo", (ti + 1) % 3, 0)])

                def s2(ti):
                    b0_, nb_ = tiles[ti]
                    n = nb_ * 128
                    hs = ti % 2
                    hres = [("h2T", hs, bi) for bi in range(nb_)]
                    for c in range(NCF):
                        ws = wc[0] % 4
                        wc[0] += 1
                        S.dma("pool", "wi%d" % ws, lambda e, c=c, ws=ws: e.dma_start(out=wr[:, ws, :, 0:128], in_=wiv[:, :, c * 128:(c + 1) * 128]),
                              w=[("wr", ws, 0)])
                        S.dma("pool", "wj%d" % ws, lambda e, c=c, ws=ws: e.dma_start(out=wr[:, ws, :, 128:256],
                                                                                   in_=wiv[:, :, DFF + c * 128:DFF + (c + 1) * 128]),
                              w=[("wr", ws, 1)])
                        pa = 1 + 2 * (c % 2)
                        pv = 2 + 2 * (c % 2)
                        hcol = (c % 2) * 2
                        for k in range(8):
                            S.pe(lambda e, k=k, ws=ws, pa=pa: e.matmul(banks[pa][:, 0:n], lhsT=wr[:, ws, k, 0:128], rhs=h2T[:, hs, k, 0:n],
                                                                     start=(k == 0), stop=(k == 7)), r=[("wr", ws, 0)] + hres, w=[PB(pa)])
                        for k in range(8):
                            S.pe(lambda e, k=k, ws=ws, hcol=hcol: e.matmul(banks[5][:, hcol:hcol + 2], lhsT=wr[:, ws, k, 0:128],
                                                                         rhs=halo[:, ti % 3, k, :], start=(k == 0), stop=(k == 7)),
                                 r=[("wr", ws, 0), ("halo", ti % 3, 0), ("halo", ti % 3, 1)], w=[PB(5)])
                        for k in range(8):
                            S.pe(lambda e, k=k, ws=ws, pv=pv: e.matmul(banks[pv][:, 0:n], lhsT=wr[:, ws, k, 128:256], rhs=h2T[:, hs, k, 0:n],
                                                                     start=(k == 0), stop=(k == 7)), r=[("wr", ws, 1)] + hres, w=[PB(pv)])
                        a_ = c % 2
                        w0 = cw[:, (l * 3 + 0) * NCF + c:(l * 3 + 0) * NCF + c + 1]
                        w1 = cw[:, (l * 3 + 1) * NCF + c:(l * 3 + 1) * NCF + c + 1]
                        w2 = cw[:, (l * 3 + 2) * NCF + c:(l * 3 + 2) * NCF + c + 1]
                        bb = cb[:, l * NCF + c:l * NCF + c + 1]
                        ar = ("acc", a_)
                        S.act(lambda e, pa=pa, a_=a_, w1=w1, bb=bb: e.activation(out=acc[:, a_, 0:n], in_=banks[pa][:, 0:n], func=AF.Identity,
                                                                                scale=w1, bias=bb), r=[PB(pa), "params"], w=[ar])
                        S.dve(lambda e, pa=pa, a_=a_, w0=w0: e.scalar_tensor_tensor(out=acc[:, a_, 1:n], in0=banks[pa][:, 0:n - 1], scalar=w0,
                                                                                   in1=acc[:, a_, 1:n], op0=ALU.mult, op1=ALU.add),
                              r=[PB(pa), ar, "params"], w=[ar])
                        S.dve(lambda e, pa=pa, a_=a_, w2=w2: e.scalar_tensor_tensor(out=acc[:, a_, 0:n - 1], in0=banks[pa][:, 1:n], scalar=w2,
                                                                                   in1=acc[:, a_, 0:n - 1], op0=ALU.mult, op1=ALU.add),
                              r=[PB(pa), ar], w=[ar])
                        S.dve(lambda e, a_=a_, w0=w0, hcol=hcol: e.scalar_tensor_tensor(out=acc[:, a_, 0:1], in0=banks[5][:, hcol:hcol + 1], scalar=w0,
                                                                                       in1=acc[:, a_, 0:1], op0=ALU.mult, op1=ALU.add),
                              r=[PB(5), ar], w=[ar])
                        S.dve(lambda e, a_=a_, w2=w2, hcol=hcol: e.scalar_tensor_tensor(out=acc[:, a_, n - 1:n], in0=banks[5][:, hcol + 1:hcol + 2],
                                                                                       scalar=w2, in1=acc[:, a_, n - 1:n], op0=ALU.mult, op1=ALU.add),
                              r=[PB(5), ar], w=[ar])
                        S.act(lambda e, a_=a_: e.activation(out=sl[:, a_, 0:n], in_=acc[:, a_, 0:n], func=AF.Silu), r=[ar], w=[("sl", a_)])
                        S.dve(lambda e, a_=a_, pv=pv, c=c: e.tensor_tensor(out=gT[:, c, 0:n], in0=banks[pv][:, 0:n], in1=sl[:, a_, 0:n], op=ALU.mult),
                              r=[PB(pv), ("sl", a_)], w=[("gT", c)])

                def s3(ti):
                    b0_, nb_ = tiles[ti]
                    for bi in range(nb_):
                        b = b0_ + bi
                        j = 1 if b < 2 else 0
                        xs_ = b % 2
                        xo = xc[0] % 2
                        xc[0] += 1
                        S.dma("sp", "xl3%d" % xs_, lambda e, b=b, xs_=xs_: e.dma_start(out=xb3[:, xs_, :], in_=Xs[b * 128:(b + 1) * 128, :]),
                              r=[("X", b)], w=[("xb3", xs_)])
                        for hlf in range(2):
                            pb = 6 + hlf
                            for c in range(NCF):
                                S.pe(lambda e, c=c, hlf=hlf, pb=pb, bi=bi: e.matmul(banks[pb][:, :], lhsT=gT[:, c, bi * 128:(bi + 1) * 128],
                                                                                   rhs=wout[:, c, hlf * 512:(hlf + 1) * 512],
                                                                                   start=(c == 0), stop=(c == NCF - 1)),
                                     r=[("gT", c), "wout"], w=[PB(pb)])
                            S.dve(lambda e, hlf=hlf, pb=pb, xo=xo, j=j: e.tensor_tensor(out=xnew[:, xo, hlf * 512:(hlf + 1) * 512], in0=banks[pb][:, :],
                                                                                      in1=gateB[:, 2 + j, hlf * 512:(hlf + 1) * 512], op=ALU.mult),
                                  r=[PB(pb), "gates"], w=[("xnew", xo, hlf)])
                            S.pool(lambda e, hlf=hlf, xo=xo, xs_=xs_: e.tensor_tensor(out=xnew[:, xo, hlf * 512:(hlf + 1) * 512],
                                                                                    in0=xnew[:, xo, hlf * 512:(hlf + 1) * 512],
                                                                                    in1=xb3[:, xs_, hlf * 512:(hlf + 1) * 512], op=ALU.add),
                                   r=[("xnew", xo, hlf), ("xb3", xs_)], w=[("xnew", xo, hlf)])
                        xres = [("xnew", xo, 0), ("xnew", xo, 1)]
                        if not last:
                            S.dma("sp", "xst", lambda e, b=b, xo=xo: e.dma_start(out=Xs[b * 128:(b + 1) * 128, :], in_=xnew[:, xo, :]),
                                  r=xres, w=[("X", b)])
                        else:
                            i = new_stat()
                            rn = ("st", i)
                            S.dve(lambda e, xo=xo, i=i: e.scalar_tensor_tensor(out=yo[:, xo, :], in0=xnew[:, xo, :], scalar=1.0, in1=xnew[:, xo, :],
                                                                              op0=ALU.mult, op1=ALU.mult, accum_out=ssr[:, i:i + 1]),
                                  r=xres, w=[rn, ("yo", xo)])
                            S.dve(lambda e, i=i: e.tensor_scalar(out=msr[:, i:i + 1], in0=ssr[:, i:i + 1], scalar1=1.0 / D, scalar2=EPS,
                                                                op0=ALU.mult, op1=ALU.add), r=[rn], w=[rn])
                            S.pool(lambda e, i=i: e.tensor_tensor(out=rstdr[:, i:i + 1], in0=msr[:, i:i + 1], in1=neghalf[:, :], op=ALU.pow),
                                   r=[rn, "neghalf"], w=[rn])
                            S.dve(lambda e, xo=xo, i=i: e.scalar_tensor_tensor(out=yo[:, xo, :], in0=xnew[:, xo, :], scalar=rstdr[:, i:i + 1],
                                                                              in1=finalgB[:, :], op0=ALU.mult, op1=ALU.mult),
                                  r=xres + [rn, "finalgB", ("yo", xo)], w=[("yo", xo)])
                            S.dma("sp", "ost", lambda e, b=b, xo=xo: e.dma_start(out=out[(b - 2) * 128:(b - 1) * 128, :], in_=yo[:, xo, :]),
                                  r=[("yo", xo)], w=[("out", b)])

                nt = len(tiles)
                s1a(0)
                s1b(0, range(tiles[0][1]))
                if nt > 1:
                    s1a(1)
                    s1b(1, [0])
                for ti in range(nt):
                    s2(ti)
                    if ti + 1 < nt:
                        s1b(ti + 1, range(1, tiles[ti + 1][1]))
                    if ti + 2 < nt:
                        s1a(ti + 2)
                    s3(ti)
                    if ti + 2 < nt:
                        s1b(ti + 2, [0])
                S.barrier()

        phases = []
        for l in range(DEPTH):
            phases.append(("M", l))
            phases.append(("A" if l % 2 == 0 else "B", l))
            phases.append(("F", l))
        nrun = 0
        for kind, l in phases:
            if kind == "M":
                if n_phases is not None and nrun >= n_phases:
                    break
                modulation(l)
                continue
            if n_phases is not None and nrun >= n_phases:
                break
            if kind == "A":
                phase_A(l)
            elif kind == "B":
                phase_B(l)
            else:
                phase_F(l)
            nrun += 1
        S.finish()
    return nc


def _consts():
    def ang(rot_dim):
        row = np.repeat(np.arange(T // GRID_W), GRID_W).astype(np.float32)
        col = np.tile(np.arange(GRID_W), T // GRID_W).astype(np.float32)
        nf = rot_dim // 4
        inv = (np.float32(10000.0) ** (-np.arange(nf, dtype=np.float32) / np.float32(nf))).astype(np.float32)
        return np.concatenate([row[:, None] * inv, col[:, None] * inv], axis=-1).astype(np.float32)
    aA, aB = ang(64), ang(32)
    k = np.arange(128)[:, None]
    q = np.arange(128)[None, :]
    return {
        "ident": np.eye(128, dtype=np.float32),
        "maskp": (k >= q).astype(np.float32),
        "maskn": (k <= q).astype(np.float32),
        "cosA": np.cos(aA).astype(np.float32), "sinA": np.sin(aA).astype(np.float32),
        "cosB": np.cos(aB).astype(np.float32), "sinB": np.sin(aB).astype(np.float32),
    }


_WNAMES = ["mod_w", "mod_b", "norm1_g", "norm2_g", "a_wqkv", "a_wo", "a_sink", "b_wdown", "b_qnorm_g", "b_wuq",
           "b_kvnorm_g", "b_wuk", "b_wuv", "b_wo", "f_win", "f_conv_w", "f_conv_b", "f_wout"]


def make_in_maps(inputs, cores):
    consts = _consts()
    shared = {n: np.ascontiguousarray(np.asarray(inputs[n], dtype=np.float32)) for n in _WNAMES}
    shared["final_g"] = np.ascontiguousarray(np.asarray(inputs["final_g"], dtype=np.float32).reshape(1, D))
    shared.update(consts)
    x = np.asarray(inputs["x"], dtype=np.float32)
    c = np.asarray(inputs["c"], dtype=np.float32)
    ctx = np.asarray(inputs["ctx"], dtype=np.float32)
    c_ctx = np.asarray(inputs["c_ctx"], dtype=np.float32)
    maps = []
    for b in cores:
        m = dict(shared)
        m["x"] = np.ascontiguousarray(x[b])
        m["ctx"] = np.ascontiguousarray(ctx[b])
        m["cc"] = np.ascontiguousarray(np.concatenate([c[b].reshape(8, 128), c_ctx.reshape(8, 128)], axis=0))
        maps.append(m)
    return maps


def kernel(**inputs):
    nc = build()
    maps = make_in_maps(inputs, list(range(8)))
    res = run_bass_kernel_spmd(nc, maps, core_ids=list(range(8)))
    return np.stack([np.asarray(r["out"], dtype=np.float32) for r in res.results], axis=0)
```

```python
import numpy as np
from contextlib import ExitStack
import concourse.bass as bass
import concourse.mybir as mybir
from concourse.bass_utils import run_bass_kernel_spmd

F32 = mybir.dt.float32
BF16 = mybir.dt.bfloat16
AF = mybir.ActivationFunctionType
ALU = mybir.AluOpType

D = 1024
T = 4096
L = 256
U = T + L
NB = U // 128
DFF = 2816
NCF = DFF // 128
DEPTH = 4
EPS = 1e-6
GRID_W = 64
ENGS = ("pe", "dve", "act", "pool", "sp")
_DBG = {}
ENGATTR = {"pe": "tensor", "dve": "vector", "act": "scalar", "pool": "gpsimd", "sp": "sync"}


class Sched:
    def __init__(self, nc, block, stack):
        self.nc, self.block, self.stack = nc, block, stack
        self.sem, self.cnt = {}, {}
        self.seen = {e: {} for e in ENGS}
        self.res = {}
        for e in ENGS:
            self.sem[e] = stack.enter_context(nc.semaphore("s_" + e))
            self.cnt[e] = 0

    def chan(self, name):
        if name not in self.sem:
            self.sem[name] = self.stack.enter_context(self.nc.semaphore("c_" + name))
            self.cnt[name] = 0
        return name

    def _deps(self, eng, reads, writes):
        deps = {}

        def add(p, c, raw):
            if p == eng and eng == "pe":
                return
            if deps.get(p, 0) < c:
                deps[p] = c

        for r in reads:
            st = self.res.get(r)
            if st and st["w"]:
                add(st["w"][0], st["w"][1], True)
        for w in writes:
            st = self.res.get(w)
            if st:
                if st["w"]:
                    add(st["w"][0], st["w"][1], False)
                for p, c in st["r"].items():
                    add(p, c, False)
        return deps

    def _commit(self, prod, val, reads, writes):
        for r in reads:
            st = self.res.setdefault(r, {"w": None, "r": {}})
            if st["r"].get(prod, 0) < val:
                st["r"][prod] = val
        for w in writes:
            self.res[w] = {"w": (prod, val), "r": {}}

    def _waits(self, eng, deps):
        waits = []
        for p, c in deps.items():
            if self.seen[eng].get(p, 0) < c:
                waits.append((p, c))
                self.seen[eng][p] = c
        return waits

    @staticmethod
    def _excl(r, w):
        ps = [x for x in r if isinstance(x, tuple) and x and x[0] == "ps"]
        if not ps:
            return list(r), list(w)
        return [x for x in r if x not in ps], list(w) + [x for x in ps if x not in w]

    def rec(self):
        self._rec = []

    def end_rec(self):
        lst, self._rec = self._rec, None
        return lst

    def play(self, lists):
        lists = [l for l in lists if l]
        idx = [0] * len(lists)
        total = sum(len(l) for l in lists)
        for _ in range(total):
            best, bi = None, -1
            for i, l in enumerate(lists):
                if idx[i] < len(l):
                    frac = idx[i] / len(l)
                    if best is None or frac < best:
                        best, bi = frac, i
            kind, a = lists[bi][idx[bi]]
            idx[bi] += 1
            if kind == "op":
                self.op(*a)
            else:
                self.dma(*a)

    def op(self, eng, fn, r=(), w=()):
        if getattr(self, "_rec", None) is not None:
            self._rec.append(("op", (eng, fn, tuple(r), tuple(w))))
            return
        r, w = self._excl(r, w)
        deps = self._deps(eng, r, w)
        waits = self._waits(eng, deps)
        self.cnt[eng] += 1
        val = self.cnt[eng]
        sem = self.sem

        def body(e):
            for p, c in waits:
                e.wait_ge(sem[p], c)
            fn(e).then_inc(sem[eng], 1)

        getattr(self.block, ENGATTR[eng])(body)
        self._commit(eng, val, r, w)

    def pe(self, fn, r=(), w=()):
        self.op("pe", fn, r, w)

    def dve(self, fn, r=(), w=()):
        self.op("dve", fn, r, w)

    def act(self, fn, r=(), w=()):
        self.op("act", fn, r, w)

    def pool(self, fn, r=(), w=()):
        self.op("pool", fn, r, w)

    def dma(self, q, chan, fn, r=(), w=()):
        if getattr(self, "_rec", None) is not None:
            self._rec.append(("dma", (q, chan, fn, tuple(r), tuple(w))))
            return
        self.chan(chan)
        r, w = self._excl(r, w)
        deps = self._deps(q, r, w)
        if self.cnt[chan] > 0 and deps.get(chan, 0) < self.cnt[chan]:
            deps[chan] = self.cnt[chan]
        waits = self._waits(q, deps)
        self.cnt[chan] += 16
        val = self.cnt[chan]
        sem = self.sem

        def body(e):
            for p, c in waits:
                e.wait_ge(sem[p], c)
            fn(e).then_inc(sem[chan], 16)

        getattr(self.block, ENGATTR[q])(body)
        self._commit(chan, val, r, w)

    def raw(self, eng, fn):
        getattr(self.block, ENGATTR[eng])(lambda e: fn(e))

    def barrier(self):
        snap = dict(self.cnt)
        sem = self.sem
        for eng in ENGS:
            waits = []
            for p, c in snap.items():
                if c > 0 and self.seen[eng].get(p, 0) < c:
                    if p == eng and eng == "pe":
                        continue
                    waits.append((p, c))
                    self.seen[eng][p] = c
            if waits:
                def body(e, waits=waits):
                    for p, c in waits:
                        e.wait_ge(sem[p], c)
                getattr(self.block, ENGATTR[eng])(body)
        self.res = {}

    def finish(self):
        snap = dict(self.cnt)
        sem = self.sem

        def body(e):
            for p, c in snap.items():
                if c > 0:
                    e.wait_ge(sem[p], c)
        self.block.sync(body)


def build(n_phases=None, debug=False):
    nc = bass.Bass("TRN2", target_bir_lowering=False)

    def din(name, shape, dt=F32):
        return nc.dram_tensor(name, list(shape), dt, kind="ExternalInput").ap()

    x_in = din("x", [T, D])
    ctx_in = din("ctx", [L, D])
    cc_in = din("cc", [16, 128])
    mod_w = din("mod_w", [DEPTH, D, 6 * D])
    mod_b = din("mod_b", [DEPTH, 6 * D])
    norm1_g = din("norm1_g", [DEPTH, D])
    norm2_g = din("norm2_g", [DEPTH, D])
    a_wqkv = din("a_wqkv", [2, D, 1536])
    a_wo = din("a_wo", [2, D, D])
    a_sink = din("a_sink", [2, 16])
    b_wdown = din("b_wdown", [2, D, 800])
    b_qnorm_g = din("b_qnorm_g", [2, 512])
    b_wuq = din("b_wuq", [2, 512, 1536])
    b_kvnorm_g = din("b_kvnorm_g", [2, 256])
    b_wuk = din("b_wuk", [2, 256, 1024])
    b_wuv = din("b_wuv", [2, 256, 1024])
    b_wo = din("b_wo", [2, D, D])
    f_win = din("f_win", [DEPTH, D, 2 * DFF])
    f_conv_w = din("f_conv_w", [DEPTH, 3, DFF])
    f_conv_b = din("f_conv_b", [DEPTH, DFF])
    f_wout = din("f_wout", [DEPTH, DFF, D])
    final_g = din("final_g", [1, D])
    ident_in = din("ident", [128, 128])
    maskp_in = din("maskp", [128, 128])
    maskn_in = din("maskn", [128, 128])
    cosA_in = din("cosA", [T, 32])
    sinA_in = din("sinA", [T, 32])
    cosB_in = din("cosB", [T, 16])
    sinB_in = din("sinB", [T, 16])

    out = nc.dram_tensor("out", [T, D], F32, kind="ExternalOutput").ap()
    if debug:
        Xs = nc.dram_tensor("xs", [U, D], F32, kind="ExternalOutput").ap()
    else:
        Xs = nc.dram_tensor("xs", [U, D], F32).ap()
    qT_d = nc.dram_tensor("qT", [16, 96, U], BF16).ap()

    stack = ExitStack()
    with stack:
        uniq = [0]

        def sb(name, shape, dt=F32, st=stack):
            uniq[0] += 1
            return st.enter_context(nc.sbuf_tensor("%s_%d" % (name, uniq[0]), list(shape), dt))

        banks = [stack.enter_context(nc.psum_tensor("bank%d" % i, [128, 512], F32)) for i in range(8)]

        def PB(i):
            return ("ps", i)

        identF = sb("identF", [128, 128])
        identB = sb("identB", [128, 128], BF16)
        zerosB = sb("zerosB", [128, 512], BF16)
        maskP = sb("maskP", [128, 128], BF16)
        maskN = sb("maskN", [128, 128], BF16)
        n1g = sb("n1g", [128, 32])
        n2g = sb("n2g", [128, 32])
        modbF = sb("modbF", [128, DEPTH * 48])
        cw = sb("cw", [128, DEPTH * 3 * NCF])
        cb = sb("cb", [128, DEPTH * NCF])
        qng = sb("qng", [128, 8])
        kvg = sb("kvg", [128, 4])
        cF = sb("cF", [128, 16])
        finalgB = sb("finalgB", [128, D])
        sinkB = sb("sinkB", [128, 32])
        esink = sb("esink", [128, 16])
        neghalf = sb("neghalf", [128, 1])
        gateB = sb("gateB", [128, 4, D])
        modF = sb("modF", [128, 48, 2])
        Gs = sb("Gs", [128, 2, 8, 2])
        scT = sb("scT", [128, 8, 2], BF16)
        screp = sb("screp", [128, 8, 2, 128], BF16)
        stage = sb("stage", [128, 128])
        ssr = sb("ssr", [128, 8])
        msr = sb("msr", [128, 8])
        rstdr = sb("rstdr", [128, 8])
        junk = sb("junk", [128, D], BF16)
        xn = sb("xn", [128, 2, D])

        block = stack.enter_context(nc.Block())
        S = Sched(nc, block, stack)
        ctr = {"st": 0, "xn": 0}

        S.dma("sp", "k0", lambda e: e.dma_start(out=identF[:, :], in_=ident_in[:, :]), w=["identF"])
        S.dma("pool", "k1", lambda e: e.dma_start(out=identB[:, :], in_=ident_in[:, :]), w=["identB"])
        S.dma("pool", "k2", lambda e: e.dma_start(out=maskP[:, :], in_=maskp_in[:, :]), w=["maskP"])
        S.dma("pool", "k3", lambda e: e.dma_start(out=maskN[:, :], in_=maskn_in[:, :]), w=["maskN"])
        S.dve(lambda e: e.memset(zerosB[:, :], 0.0), w=["zerosB"])
        S.dve(lambda e: e.memset(neghalf[:, :], -0.5), w=["neghalf"])
        S.dma("sp", "k4", lambda e: e.dma_start(out=finalgB[:, :], in_=final_g[0:1, :].broadcast_to([128, D])), w=["finalgB"])
        S.dma("sp", "k5", lambda e: e.dma_start(out=sinkB[:, :], in_=a_sink.rearrange("a h -> (a h)").unsqueeze(0).broadcast_to([128, 32])), w=["sinkB"])

        def featmajor(dst, src2d, n):
            S.dma("sp", "k5", lambda e: e.dma_start(out=stage[0:n, :], in_=src2d), w=["stage"])
            S.pe(lambda e: e.transpose(out=banks[0][:, 0:n], in_=stage[0:n, :], identity=identF[0:n, 0:n]),
                 r=["stage", "identF"], w=[PB(0)])
            S.dve(lambda e: e.tensor_copy(out=dst, in_=banks[0][:, 0:n]), r=[PB(0)], w=["params"])

        featmajor(n1g[:, :], norm1_g.rearrange("l (k p) -> (l k) p", p=128), 32)
        featmajor(n2g[:, :], norm2_g.rearrange("l (k p) -> (l k) p", p=128), 32)
        mb2 = mod_b.rearrange("l (k p) -> (l k) p", p=128)
        featmajor(modbF[:, 0:128], mb2[0:128, :], 128)
        featmajor(modbF[:, 128:192], mb2[128:192, :], 64)
        cw2 = f_conv_w.rearrange("l j (k p) -> (l j k) p", p=128)
        featmajor(cw[:, 0:128], cw2[0:128, :], 128)
        featmajor(cw[:, 128:256], cw2[128:256, :], 128)
        featmajor(cw[:, 256:264], cw2[256:264, :], 8)
        featmajor(cb[:, :], f_conv_b.rearrange("l (k p) -> (l k) p", p=128), 88)
        featmajor(qng[:, :], b_qnorm_g.rearrange("l (k p) -> (l k) p", p=128), 8)
        featmajor(kvg[:, :], b_kvnorm_g.rearrange("l (k p) -> (l k) p", p=128), 4)
        featmajor(cF[:, :], cc_in[:, :], 16)
        S.raw("act", lambda e: e.preload_act_table(AF.Silu))
        S.act(lambda e: e.activation(out=scT[:, :, :].rearrange("p k j -> p j k"),
                                     in_=cF[:, :].rearrange("p (j k) -> p j k", j=2), func=AF.Silu),
              r=["params"], w=["scT"])
        for j in range(2):
            S.dve(lambda e, j=j: e.tensor_copy(out=screp[:, :, j, :], in_=scT[:, :, j:j + 1].broadcast_to([128, 8, 128])),
                  r=["scT"], w=["screp"])

        def rstd_of(ss_ap, n_feat):
            i = ctr["st"] % 8
            return i

        def new_stat():
            i = ctr["st"] % 8
            ctr["st"] += 1
            return i

        def row_rstd(src, src_res, n_feat):
            i = new_stat()
            rn = ("st", i)
            S.act(lambda e: e.activation(out=junk[:, 0:n_feat], in_=src, func=AF.Square, accum_out=ssr[:, i:i + 1]),
                  r=src_res, w=[rn, "junk"])
            S.dve(lambda e: e.tensor_scalar(out=msr[:, i:i + 1], in0=ssr[:, i:i + 1], scalar1=1.0 / n_feat, scalar2=EPS,
                                            op0=ALU.mult, op1=ALU.add), r=[rn], w=[rn])
            S.pool(lambda e: e.tensor_tensor(out=rstdr[:, i:i + 1], in0=msr[:, i:i + 1], in1=neghalf[:, :], op=ALU.pow),
                   r=[rn, "neghalf"], w=[rn])
            return rstdr[:, i:i + 1], rn

        def normT(xb, xb_res, G, SH, dst_fn, dst_res, psb, psb2=None, act_half=False):
            rs, rn = row_rstd(xb, xb_res, D)
            s = ctr["xn"] % 2
            ctr["xn"] += 1
            S.act(lambda e: e.activation(out=xn[:, s, :], in_=xb, func=AF.Copy, scale=rs), r=list(xb_res) + [rn], w=[("xn", s)])
            def tr(hlf, pbk):
                for kk in range(4):
                    k = hlf * 4 + kk
                    S.pe(lambda e, k=k, kk=kk: e.transpose(out=banks[pbk][:, kk * 128:(kk + 1) * 128],
                                                           in_=xn[:, s, k * 128:(k + 1) * 128], identity=identF[:, :]),
                         r=[("xn", s), "identF"], w=[PB(pbk)])

            def ev(hlf, pbk):
                for kk in range(4):
                    k = hlf * 4 + kk
                    if act_half and hlf == 1:
                        S.act(lambda e, k=k, kk=kk: e.activation(out=dst_fn(k), in_=banks[pbk][:, kk * 128:(kk + 1) * 128],
                                                                 func=AF.Identity, scale=G(k), bias=SH(k)),
                              r=[PB(pbk), "mods"], w=[dst_res])
                    else:
                        S.dve(lambda e, k=k, kk=kk: e.tensor_scalar(out=dst_fn(k), in0=banks[pbk][:, kk * 128:(kk + 1) * 128],
                                                                    scalar1=G(k), scalar2=SH(k), op0=ALU.mult, op1=ALU.add),
                              r=[PB(pbk), "mods"], w=[dst_res])

            if psb2 is None:
                tr(0, psb)
                ev(0, psb)
                tr(1, psb)
                ev(1, psb)
            else:
                tr(0, psb)
                tr(1, psb2)
                ev(0, psb)
                ev(1, psb2)

        def Xsrc(l, b):
            if l == 0:
                return ctx_in[b * 128:(b + 1) * 128, :] if b < 2 else x_in[(b - 2) * 128:(b - 1) * 128, :]
            return Xs[b * 128:(b + 1) * 128, :]

        def rope_tm(ps_ap, H, npair, cs, sn, out_ap, tmp, res_in, res_out, tmp_res):
            csb = cs.unsqueeze(1).broadcast_to([128, H, npair])
            snb = sn.unsqueeze(1).broadcast_to([128, H, npair])
            ev, od = ps_ap[:, :, :, 0], ps_ap[:, :, :, 1]
            t0 = tmp[:, 0, 0:H * npair].rearrange("p (h i) -> p h i", h=H)
            t1 = tmp[:, 1, 0:H * npair].rearrange("p (h i) -> p h i", h=H)
            S.dve(lambda e: e.tensor_tensor(out=t0, in0=ev, in1=csb, op=ALU.mult), r=res_in, w=[tmp_res + "0"])
            S.dve(lambda e: e.tensor_tensor(out=t1, in0=od, in1=snb, op=ALU.mult), r=res_in, w=[tmp_res + "1"])
            S.dve(lambda e: e.tensor_tensor(out=out_ap[:, :, :, 0], in0=t0, in1=t1, op=ALU.subtract),
                  r=[tmp_res + "0", tmp_res + "1"], w=[res_out])
            S.dve(lambda e: e.tensor_tensor(out=t0, in0=ev, in1=snb, op=ALU.mult), r=res_in, w=[tmp_res + "0"])
            S.dve(lambda e: e.tensor_tensor(out=t1, in0=od, in1=csb, op=ALU.mult), r=res_in, w=[tmp_res + "1"])
            S.dve(lambda e: e.tensor_tensor(out=out_ap[:, :, :, 1], in0=t0, in1=t1, op=ALU.add),
                  r=[tmp_res + "0", tmp_res + "1"], w=[res_out])

        def modulation(l):
            with ExitStack() as ms:
                mw = [sb("mw%d" % i, [128, 8, D], BF16, ms) for i in range(2)]
                modbB = sb("modbB", [128, 2, D], F32, ms)
                mwv = mod_w[l].rearrange("(k p) n -> p k n", p=128)
                psF = banks[7]
                for m in range(6):
                    buf = mw[m % 2]
                    S.dma("pool", "mw%d" % (m % 2),
                          lambda e, m=m, buf=buf: e.dma_start(out=buf[:, :, :], in_=mwv[:, :, m * D:(m + 1) * D]),
                          w=[("mw", m % 2)])
                    for js in range(8):
                        col = (m * 8 + js) * 2
                        for k in range(8):
                            S.pe(lambda e, js=js, k=k, col=col, buf=buf: e.matmul(
                                psF[:, col:col + 2], lhsT=buf[:, k, js * 128:(js + 1) * 128], rhs=scT[:, k, :],
                                start=(k == 0), stop=(k == 7)), r=[("mw", m % 2), "scT"], w=[PB(7)])
                    if m in (2, 5):
                        gi = 0 if m == 2 else 2
                        mi = 0 if m == 2 else 1
                        S.dma("sp", "k6", lambda e, m=m, mi=mi: e.dma_start(
                            out=modbB[:, mi, :], in_=mod_b[l:l + 1, m * D:(m + 1) * D].broadcast_to([128, D])), w=[("modbB", mi)])
                        for j in range(2):
                            for hlf in range(2):
                                pb = 5 + hlf
                                for k in range(8):
                                    S.pe(lambda e, j=j, hlf=hlf, k=k, pb=pb, buf=buf: e.matmul(
                                        banks[pb][:, :], lhsT=screp[:, k, j, :], rhs=buf[:, k, hlf * 512:(hlf + 1) * 512],
                                        start=(k == 0), stop=(k == 7)), r=[("mw", m % 2), "screp"], w=[PB(pb)])
                                S.dve(lambda e, j=j, hlf=hlf, pb=pb, gi=gi, mi=mi: e.tensor_tensor(
                                    out=gateB[:, gi + j, hlf * 512:(hlf + 1) * 512], in0=banks[pb][:, :],
                                    in1=modbB[:, mi, hlf * 512:(hlf + 1) * 512], op=ALU.add),
                                    r=[PB(pb), ("modbB", mi)], w=["gates"])
                S.dve(lambda e: e.tensor_tensor(out=modF[:, :, :], in0=psF[:, 0:96].rearrange("p (f j) -> p f j", j=2),
                                                in1=modbF[:, l * 48:(l + 1) * 48].unsqueeze(2).broadcast_to([128, 48, 2]),
                                                op=ALU.add), r=[PB(7), "params"], w=["mods"])
                for ni, (gt, sc0) in enumerate(((n1g, 8), (n2g, 32))):
                    S.dve(lambda e, ni=ni, sc0=sc0: e.tensor_scalar(out=Gs[:, ni, :, :], in0=modF[:, sc0:sc0 + 8, :], scalar1=1.0,
                                                                   scalar2=None, op0=ALU.add), r=["mods"], w=["mods"])
                    S.dve(lambda e, ni=ni, gt=gt: e.tensor_tensor(
                        out=Gs[:, ni, :, :], in0=Gs[:, ni, :, :],
                        in1=gt[:, l * 8:(l + 1) * 8].unsqueeze(2).broadcast_to([128, 8, 2]), op=ALU.mult),
                        r=["mods", "params"], w=["mods"])
                S.barrier()

        def Gf(ni, j):
            return lambda k: Gs[:, ni, k, j:j + 1]

        def SHf(ni, j):
            base = 0 if ni == 0 else 24
            return lambda k: modF[:, base + k, j:j + 1]

        def proj_T(O_ap, O_res, tl, sl=0):
            OTt = tl["OT"][sl]
            pbT = tl["pbT"]
            psT = banks[pbT][:, :].bitcast(BF16)
            for k in range(8):
                S.pe(lambda e, k=k: e.transpose(out=psT[:, k * 128:(k + 1) * 128], in_=O_ap[:, k * 128:(k + 1) * 128],
                                               identity=identB[:, :]), r=list(O_res) + ["identB"], w=[PB(pbT)])
            S.act(lambda e: e.activation(out=OTt[:, :], in_=psT[:, :], func=AF.Copy), r=[PB(pbT)], w=[("OT", sl)])

        def proj_MM(wo_sb, xb_ap, xb_res, gate_idx, b, tl, sl=0):
            OTt, xnew = tl["OT"][sl], tl["xnew"][sl]
            for hlf in range(2):
                pb = tl["pbo"][hlf]
                for k in range(8):
                    S.pe(lambda e, k=k, hlf=hlf, pb=pb: e.matmul(banks[pb][:, :], lhsT=OTt[:, k * 128:(k + 1) * 128],
                                                               rhs=wo_sb[:, k, hlf * 512:(hlf + 1) * 512],
                                                               start=(k == 0), stop=(k == 7)), r=[("OT", sl), "wo"], w=[PB(pb)])
                S.dve(lambda e, hlf=hlf, pb=pb: e.tensor_tensor(out=xnew[:, hlf * 512:(hlf + 1) * 512], in0=banks[pb][:, :],
                                                              in1=gateB[:, gate_idx, hlf * 512:(hlf + 1) * 512], op=ALU.mult),
                      r=[PB(pb), "gates"], w=[("xnew", sl, hlf)])
                S.pool(lambda e, hlf=hlf: e.tensor_tensor(out=xnew[:, hlf * 512:(hlf + 1) * 512], in0=xnew[:, hlf * 512:(hlf + 1) * 512],
                                                         in1=xb_ap[:, hlf * 512:(hlf + 1) * 512], op=ALU.add),
                       r=[("xnew", sl, hlf)] + list(xb_res), w=[("xnew", sl, hlf)])
            S.dma("sp", "xst%d" % sl, lambda e: e.dma_start(out=Xs[b * 128:(b + 1) * 128, :], in_=xnew[:, :]),
                  r=[("xnew", sl, 0), ("xnew", sl, 1)], w=[("X", b)])

        def proj_residual(O_ap, O_res, wo_sb, xb_ap, xb_res, gate_idx, b, tl):
            proj_T(O_ap, O_res, tl, 0)
            proj_MM(wo_sb, xb_ap, xb_res, gate_idx, b, tl, 0)

        def phase_A(l):
            ja = l // 2
            with_ctx = l < DEPTH - 1
            S.raw("act", lambda e: e.preload_act_table(AF.Exp))
            with ExitStack() as ps_:
                wqkv = sb("wqkv", [128, 8, 1536], BF16, ps_)
                wo_sb = sb("wo", [128, 8, D], BF16, ps_)
                KT = sb("KT", [128, 4, U], BF16, ps_)
                Vaug = sb("Vaug", [128, NB, 4, 65], BF16, ps_)
                xbr = sb("xbr", [128, 4, D], F32, ps_)
                cosA = sb("cosA", [128, 32, 32], F32, ps_)
                sinA = sb("sinA", [128, 32, 32], F32, ps_)
                S.dma("sp", "k6", lambda e: e.dma_start(out=cosA[:, :, :], in_=cosA_in.rearrange("(b p) i -> p b i", p=128)), w=["cosA"])
                S.dma("sp", "k7", lambda e: e.dma_start(out=sinA[:, :, :], in_=sinA_in.rearrange("(b p) i -> p b i", p=128)), w=["sinA"])
                hT = sb("hT", [128, 2, 8, 128], BF16, ps_)
                qr = sb("qr", [128, 16, 32, 2], BF16, ps_)
                krd = sb("krd", [128, 4, 2, 64], BF16, ps_)
                rtmp = sb("rtmp", [128, 2, 256], F32, ps_)
                QT = sb("QT", [128, 3, 16, 128], BF16, ps_)
                PT = sb("PT", [128, 2, 5, 4, 128], BF16, ps_)
                Osb = sb("Osb", [128, D], BF16, ps_)
                den = sb("den", [128, 2, 4], F32, ps_)
                OTt = sb("OTt", [128, D], BF16, ps_)
                xnew = sb("xnew", [128, D], F32, ps_)
                tl = {"OT": [OTt], "xnew": [xnew], "pbT": 6, "pbo": (7, 6)}

                wv = a_wqkv[ja].rearrange("(k p) n -> p k n", p=128)
                for i in range(3):
                    S.dma("pool", "wA%d" % i, lambda e, i=i: e.dma_start(out=wqkv[:, :, i * 512:(i + 1) * 512], in_=wv[:, :, i * 512:(i + 1) * 512]),
                          w=["wqkv"])
                wov = a_wo[ja].rearrange("(k p) n -> p k n", p=128)
                for i in range(2):
                    S.dma("pool", "wA%d" % i, lambda e, i=i: e.dma_start(out=wo_sb[:, :, i * 512:(i + 1) * 512], in_=wov[:, :, i * 512:(i + 1) * 512]),
                          w=["wo"])
                S.pool(lambda e: e.memset(Vaug[:, :, :, 64:65], 1.0), w=["Vones"])
                for qs0 in range(3):
                    S.pool(lambda e, qs0=qs0: e.memset(QT[:, qs0, :, :], 0.0), w=[("QT", qs0)])
                S.act(lambda e: e.activation(out=esink[:, :], in_=sinkB[:, ja * 16:(ja + 1) * 16], func=AF.Exp), r=["sinkB"], w=["esink"])

                def xloadA(b):
                    S.dma("sp", "xld%d" % (b % 2), lambda e: e.dma_start(out=xbr[:, b % 4, :], in_=Xsrc(l, b)),
                          r=[("X", b)], w=[("xb", b % 4)])

                xloadA(0)

                def stage1(b):
                    j = 1 if b < 2 else 0
                    xs_ = b % 4
                    if b + 1 < NB:
                        xloadA(b + 1)
                    hs = b % 2
                    normT(xbr[:, xs_, :], [("xb", xs_)], Gf(0, j), SHf(0, j), lambda k: hT[:, hs, k, :], ("hT", hs), 0)
                    need_q = (b >= 2) or with_ctx

                    def piece(pc, pb):
                        for k in range(8):
                            S.pe(lambda e, k=k: e.matmul(banks[pb][:, :], lhsT=hT[:, hs, k, :], rhs=wqkv[:, k, pc * 512:(pc + 1) * 512],
                                                         start=(k == 0), stop=(k == 7)), r=[("hT", hs), "wqkv"], w=[PB(pb)])

                    piece(2, 1)
                    if b >= 2:
                        blk = b - 2
                        rope_tm(banks[1][:, 0:256].rearrange("p (h i t) -> p h i t", h=4, t=2), 4, 32,
                                cosA[:, blk, :], sinA[:, blk, :],
                                krd[:, :, 0, :].rearrange("p h (i t) -> p h i t", t=2), rtmp,
                                [PB(1), "cosA", "sinA"], "krd0", "rtmp")
                    else:
                        S.act(lambda e: e.activation(out=krd[:, :, 0, :], in_=banks[1][:, 0:256].rearrange("p (h d) -> p h d", h=4),
                                                     func=AF.Copy), r=[PB(1)], w=["krd0"])
                    S.act(lambda e: e.activation(out=Vaug[:, b, :, 0:64], in_=banks[1][:, 256:512].rearrange("p (h d) -> p h d", h=4),
                                                 func=AF.Copy), r=[PB(1)], w=[("V", b)])
                    S.pool(lambda e: e.tensor_copy(out=krd[:, :, 1, :], in_=krd[:, :, 0, :]), r=["krd0"], w=["krd1"])
                    psk = banks[0][:, :].bitcast(BF16)
                    kflat = krd[:, :, :, :].rearrange("p h c d -> p (h c d)")
                    for g in range(4):
                        S.pe(lambda e, g=g: e.transpose(out=psk[:, g * 128:(g + 1) * 128], in_=kflat[:, g * 128:(g + 1) * 128],
                                                       identity=identB[:, :]), r=["krd0", "krd1", "identB"], w=[PB(0)])
                    S.dve(lambda e: e.tensor_copy(out=KT[:, :, b * 128:(b + 1) * 128],
                                                  in_=psk[:, 0:512].rearrange("p (g t) -> p g t", g=4)), r=[PB(0)], w=[("K", b)])
                    if not need_q:
                        return
                    for pc, pb in ((0, 2), (1, 1)):
                        piece(pc, pb)
                        if b >= 2:
                            rope_tm(banks[pb][:, :].rearrange("p (h i t) -> p h i t", h=8, t=2), 8, 32,
                                    cosA[:, b - 2, :], sinA[:, b - 2, :], qr[:, pc * 8:(pc + 1) * 8, :, :], rtmp,
                                    [PB(pb), "cosA", "sinA"], ("qr", pc), "rtmp")
                        else:
                            S.act(lambda e, pc=pc, pb=pb: e.activation(out=qr[:, pc * 8:(pc + 1) * 8, :, :].rearrange("p h i t -> p (h i t)"),
                                                                      in_=banks[pb][:, :], func=AF.Copy), r=[PB(pb)], w=[("qr", pc)])
                    psq = banks[3][:, :].bitcast(BF16)
                    qs = b % 3
                    qflat = qr[:, :, :, :].rearrange("p h i t -> p (h i t)")
                    for k in range(8):
                        S.pe(lambda e, k=k: e.transpose(out=psq[:, k * 128:(k + 1) * 128], in_=qflat[:, k * 128:(k + 1) * 128],
                                                       identity=identB[:, :]), r=[("qr", k // 4), "identB"], w=[PB(3)])
                    S.act(lambda e: e.activation(out=QT[0:64, qs, 0:16:2, :], in_=psq[0:64, :].rearrange("p (k t) -> p k t", k=8), func=AF.Copy),
                          r=[PB(3)], w=[("QT", qs, 0)])
                    S.dve(lambda e: e.tensor_copy(out=QT[64:128, qs, 1:16:2, :], in_=psq[64:128, :].rearrange("p (k t) -> p k t", k=8)),
                          r=[PB(3)], w=[("QT", qs, 1)])

                def stage2(b):
                    j = 1 if b < 2 else 0
                    qs = b % 3
                    if b < 2:
                        keys = [(0, None), (1, None)]
                    else:
                        keys = [(0, None), (1, None)]
                        if b - 1 >= 2:
                            keys.append((b - 1, "P"))
                        keys.append((b, None))
                        if b + 1 < NB:
                            keys.append((b + 1, "N"))
                    nk = len(keys)
                    def qk_exp(g):
                        ps_ = (b * 4 + g) % 2
                        pt_res = ("PT", ps_)
                        for ji, (kb, mk) in enumerate(keys):
                            pb = 4 + ji % 2
                            S.pe(lambda e, kb=kb, pb=pb: e.matmul(
                                banks[pb][:, :].rearrange("p (c t) -> p c t", c=4),
                                lhsT=KT[:, g, kb * 128:(kb + 1) * 128], rhs=QT[:, qs, 4 * g:4 * g + 4, :], start=True, stop=True),
                                r=[("K", kb), ("QT", qs), ("QT", qs, 0), ("QT", qs, 1)], w=[PB(pb)])
                            S.act(lambda e, pb=pb, ji=ji: e.activation(
                                out=PT[:, ps_, ji, :, :], in_=banks[pb][:, :].rearrange("p (c t) -> p c t", c=4), func=AF.Exp, scale=0.125),
                                r=[PB(pb)], w=[(pt_res, ji)])
                            if mk is not None:
                                mt = maskP if mk == "P" else maskN
                                S.pool(lambda e, ji=ji, mt=mt: e.tensor_tensor(
                                    out=PT[:, ps_, ji, :, :], in0=PT[:, ps_, ji, :, :],
                                    in1=mt[:, :].unsqueeze(1).broadcast_to([128, 4, 128]), op=ALU.mult),
                                    r=[(pt_res, ji), "maskP", "maskN"], w=[(pt_res, ji)])

                    def pv_norm(g):
                        ps_ = (b * 4 + g) % 2
                        pt_res = ("PT", ps_)
                        ob = 6
                        for s in range(4):
                            for ji, (kb, mk) in enumerate(keys):
                                S.pe(lambda e, s=s, ji=ji, kb=kb: e.matmul(
                                    banks[ob][:, s * 65:(s + 1) * 65], lhsT=PT[:, ps_, ji, s, :], rhs=Vaug[:, kb, g, :],
                                    start=(ji == 0), stop=(ji == nk - 1)),
                                    r=[(pt_res, ji), ("V", kb), "Vones"], w=[PB(ob)])
                        ov = banks[ob][:, 0:260].rearrange("p (s d) -> p s d", s=4)
                        S.dve(lambda e: e.tensor_tensor(out=den[:, ps_, :], in0=ov[:, :, 64], in1=esink[:, 4 * g:4 * g + 4], op=ALU.add),
                              r=[PB(ob), "esink"], w=[("den", ps_)])
                        S.dve(lambda e: e.reciprocal(out=den[:, ps_, :], in_=den[:, ps_, :]), r=[("den", ps_)], w=[("den", ps_)])
                        Ov = Osb[:, :].rearrange("p (h d) -> p h d", d=64)
                        S.dve(lambda e: e.tensor_tensor(
                            out=Ov[:, 4 * g:4 * g + 4, :], in0=ov[:, :, 0:64],
                            in1=den[:, ps_, :].unsqueeze(2).broadcast_to([128, 4, 64]), op=ALU.mult),
                            r=[PB(ob), ("den", ps_)], w=["Osb"])

                    qk_exp(0)
                    for g in range(4):
                        if g + 1 < 4:
                            qk_exp(g + 1)
                        pv_norm(g)
                    proj_residual(Osb, ["Osb"], wo_sb, xbr[:, b % 4, :], [("xb", b % 4)], 0 + j, b, tl)

                stage1(0)
                stage1(1)
                if with_ctx:
                    stage2(0)
                    stage2(1)
                stage1(2)
                stage1(3)
                for b in range(2, NB):
                    S.rec()
                    stage2(b)
                    r2 = S.end_rec()
                    r1 = None
                    if b + 2 < NB:
                        S.rec()
                        stage1(b + 2)
                        r1 = S.end_rec()
                    S.play([r2, r1])
                S.barrier()

        def phase_B(l):
            jb = l // 2
            with_ctx = l < DEPTH - 1
            b0 = 0 if with_ctx else 2
            scale = 96.0 ** -0.5
            S.raw("act", lambda e: e.preload_act_table(AF.Exp))
            with ExitStack() as pB:
                ckvT = sb("ckvT", [128, 2, U], BF16, pB)
                Kb = sb("Kb", [128, 2, U], BF16, pB)
                wuk = sb("wuk", [128, 2, D], BF16, pB)
                wuv = sb("wuv", [128, 2, D], BF16, pB)
                wo_sb = sb("wo", [128, 8, D], BF16, pB)
                with ExitStack() as p1:
                    wdn = sb("wdn", [128, 8, 800], BF16, p1)
                    wuq = sb("wuq", [128, 4, 1536], BF16, p1)
                    xbr = sb("xbr", [128, 2, D], F32, p1)
                    hT = sb("hT", [128, 2, 8, 128], BF16, p1)
                    cqn = sb("cqn", [128, 512], F32, p1)
                    ckn = sb("ckn", [128, 256], F32, p1)
                    cqT = sb("cqT", [128, 2, 4, 128], BF16, p1)
                    rtmp1 = sb("rtmp1", [128, 2, 16], F32, p1)
                    krs = sb("krs", [128, 96], F32, p1)
                    rtmp = sb("rtmp", [128, 2, 256], F32, p1)
                    qf = sb("qf", [128, 16, 32], F32, p1)
                    qrb = sb("qrb", [128, 16, 96], BF16, p1)
                    QTa = sb("QTa", [128, 2, 16, 512], BF16, p1)
                    cosB = sb("cosB", [128, 32, 16], F32, p1)
                    sinB = sb("sinB", [128, 32, 16], F32, p1)
                    S.dma("sp", "k6", lambda e: e.dma_start(out=cosB[:, :, :], in_=cosB_in.rearrange("(b p) i -> p b i", p=128)), w=["cosB"])
                    S.dma("sp", "k7", lambda e: e.dma_start(out=sinB[:, :, :], in_=sinB_in.rearrange("(b p) i -> p b i", p=128)), w=["sinB"])
                    wdv = b_wdown[jb].rearrange("(k p) n -> p k n", p=128)
                    S.dma("pool", "wB0", lambda e: e.dma_start(out=wdn[:, :, 0:400], in_=wdv[:, :, 0:400]), w=["wdn"])
                    S.dma("pool", "wB1", lambda e: e.dma_start(out=wdn[:, :, 400:800], in_=wdv[:, :, 400:800]), w=["wdn"])
                    wqv = b_wuq[jb].rearrange("(k p) n -> p k n", p=128)
                    for i in range(3):
                        S.dma("pool", "wB%d" % (i % 2), lambda e, i=i: e.dma_start(out=wuq[:, :, i * 512:(i + 1) * 512], in_=wqv[:, :, i * 512:(i + 1) * 512]),
                              w=["wuq"])
                    S.dve(lambda e: e.memset(krs[:, :], 0.0), w=["krs"])
                    S.dma("pool", "wB2", lambda e: e.dma_start(out=wuk[:, :, :], in_=b_wuk[jb].rearrange("(k p) n -> p k n", p=128)), w=["wuk"])
                    S.dma("pool", "wB3", lambda e: e.dma_start(out=wuv[:, :, :], in_=b_wuv[jb].rearrange("(k p) n -> p k n", p=128)), w=["wuv"])
                    wov = b_wo[jb].rearrange("(k p) n -> p k n", p=128)
                    for i in range(2):
                        S.dma("pool", "wB%d" % (4 + i), lambda e, i=i: e.dma_start(out=wo_sb[:, :, i * 512:(i + 1) * 512], in_=wov[:, :, i * 512:(i + 1) * 512]),
                              w=[("wo", i)])

                    def qstore(grp, nblk, u0):
                        qa = grp % 2
                        for hh in range(2):
                            S.dma("sp", "qst%d" % hh, lambda e, hh=hh: e.dma_start(
                                out=qT_d[hh * 8:(hh + 1) * 8, :, u0:u0 + nblk * 128].rearrange("h r u -> r h u"),
                                in_=QTa[0:96, qa, hh * 8:(hh + 1) * 8, 0:nblk * 128]), r=[("QTa", qa)], w=[("qT", grp)])

                    def xload(b):
                        S.dma("sp", "xld%d" % (b % 2), lambda e, b=b: e.dma_start(out=xbr[:, b % 2, :], in_=Xsrc(l, b)),
                              r=[("X", b)], w=[("xb", b % 2)])

                    def front(b):
                        j = 1 if b < 2 else 0
                        xs_ = b % 2
                        if b + 1 < NB:
                            xload(b + 1)
                        hs = b % 2
                        normT(xbr[:, xs_, :], [("xb", xs_)], Gf(0, j), SHf(0, j), lambda k, hs=hs: hT[:, hs, k, :], ("hT", hs), 0)
                        for pc, (c0, c1) in enumerate(((0, 512), (512, 800))):
                            pb = 1 + pc
                            for k in range(8):
                                S.pe(lambda e, k=k, pb=pb, c0=c0, c1=c1, hs=hs: e.matmul(
                                    banks[pb][:, 0:c1 - c0], lhsT=hT[:, hs, k, :], rhs=wdn[:, k, c0:c1],
                                    start=(k == 0), stop=(k == 7)), r=[("hT", hs), "wdn"], w=[PB(pb)])
                        need_q = (b >= 2) or with_ctx
                        rs, rn = row_rstd(banks[2][:, 0:256], [PB(2)], 256)
                        S.act(lambda e, rs=rs: e.activation(out=ckn[:, :], in_=banks[2][:, 0:256], func=AF.Copy, scale=rs),
                              r=[PB(2), rn], w=["ckn"])
                        if b >= 2:
                            blk = b - 2
                            rope_tm(banks[2][:, 256:288].rearrange("p (h i t) -> p h i t", h=1, t=2), 1, 16,
                                    cosB[:, blk, :], sinB[:, blk, :],
                                    krs[:, 64:96].rearrange("p (h i t) -> p h i t", h=1, t=2), rtmp1,
                                    [PB(2), "cosB", "sinB"], "krs", "rtmp1")
                        else:
                            S.dve(lambda e: e.tensor_copy(out=krs[:, 64:96], in_=banks[2][:, 256:288]), r=[PB(2)], w=["krs"])
                        for k in range(2):
                            S.pe(lambda e, k=k: e.transpose(out=banks[3][:, k * 128:(k + 1) * 128], in_=ckn[:, k * 128:(k + 1) * 128],
                                                           identity=identF[:, :]), r=["ckn", "identF"], w=[PB(3)])
                        S.pe(lambda e: e.transpose(out=banks[3][0:96, 256:384], in_=krs[:, :], identity=identF[:, :]),
                             r=["krs", "identF"], w=[PB(3)])
                        for k in range(2):
                            S.dve(lambda e, k=k, b=b: e.tensor_scalar(out=ckvT[:, k, b * 128:(b + 1) * 128], in0=banks[3][:, k * 128:(k + 1) * 128],
                                                                     scalar1=kvg[:, jb * 2 + k:jb * 2 + k + 1], scalar2=None, op0=ALU.mult),
                                  r=[PB(3), "params"], w=[("ckvT", b)])
                        for kk in range(2):
                            S.act(lambda e, kk=kk, b=b: e.activation(out=Kb[64:96, kk, b * 128:(b + 1) * 128], in_=banks[3][64:96, 256:384],
                                                                    func=AF.Copy), r=[PB(3)], w=[("KR", b)])
                        if not need_q:
                            return
                        cs_ = b % 2
                        rs2, rn2 = row_rstd(banks[1][:, :], [PB(1)], 512)
                        S.act(lambda e, rs2=rs2: e.activation(out=cqn[:, :], in_=banks[1][:, :], func=AF.Copy, scale=rs2),
                              r=[PB(1), rn2], w=["cqn"])
                        for k in range(4):
                            S.pe(lambda e, k=k: e.transpose(out=banks[4][:, k * 128:(k + 1) * 128], in_=cqn[:, k * 128:(k + 1) * 128],
                                                           identity=identF[:, :]), r=["cqn", "identF"], w=[PB(4)])
                        for k in range(4):
                            S.dve(lambda e, k=k, cs_=cs_: e.tensor_scalar(out=cqT[:, cs_, k, :], in0=banks[4][:, k * 128:(k + 1) * 128],
                                                                scalar1=qng[:, jb * 4 + k:jb * 4 + k + 1], scalar2=None, op0=ALU.mult),
                                  r=[PB(4), "params"], w=[("cqT", cs_)])

                    def back(b):
                        cs_ = b % 2
                        for pc in range(4):
                            pb = 5 + pc % 2
                            for k in range(4):
                                S.pe(lambda e, k=k, pb=pb, pc=pc, cs_=cs_: e.matmul(banks[pb][:, 0:384], lhsT=cqT[:, cs_, k, :],
                                                                         rhs=wuq[:, k, pc * 384:(pc + 1) * 384],
                                                                         start=(k == 0), stop=(k == 3)), r=[("cqT", cs_), "wuq"], w=[PB(pb)])
                            pv = banks[pb][:, 0:384].rearrange("p (h d) -> p h d", h=4)
                            if b >= 2:
                                S.act(lambda e, pv=pv, pc=pc: e.activation(out=qrb[:, pc * 4:(pc + 1) * 4, 0:64], in_=pv[:, :, 0:64], func=AF.Copy),
                                      r=[PB(pb)], w=[("qrb", pc, 0)])
                                S.act(lambda e, pv=pv, pc=pc: e.activation(out=qf[:, pc * 4:(pc + 1) * 4, :], in_=pv[:, :, 64:96], func=AF.Copy),
                                      r=[PB(pb)], w=[("qf", pc)])
                            else:
                                S.act(lambda e, pv=pv, pc=pc: e.activation(out=qrb[:, pc * 4:(pc + 1) * 4, :], in_=pv, func=AF.Copy),
                                      r=[PB(pb)], w=[("qrb", pc, 0), ("qrb", pc, 1)])
                        if b >= 2:
                            rope_tm(qf[:, :, :].rearrange("p h (i t) -> p h i t", t=2), 16, 16,
                                    cosB[:, b - 2, :], sinB[:, b - 2, :],
                                    qrb[:, :, 64:96].rearrange("p h (i t) -> p h i t", t=2), rtmp,
                                    [("qf", 0), ("qf", 1), ("qf", 2), ("qf", 3), "cosB", "sinB"], ("qrb", "rope"), "rtmp")
                        if b < 2:
                            grp, bi = 0, b
                        else:
                            grp, bi = 1 + (b - 2) // 4, (b - 2) % 4
                        qa = grp % 2
                        for hh in range(2):
                            pb = 7
                            pst = banks[pb][:, :].bitcast(BF16)
                            for h8 in range(8):
                                h = hh * 8 + h8
                                S.pe(lambda e, h=h, h8=h8, pst=pst: e.transpose(out=pst[0:96, h8 * 128:(h8 + 1) * 128], in_=qrb[:, h, :],
                                                                               identity=identB[:, :]),
                                     r=[("qrb", h // 4, 0), ("qrb", h // 4, 1), ("qrb", "rope"), "identB"], w=[PB(pb)])
                            S.act(lambda e, hh=hh, pst=pst, qa=qa, bi=bi: e.activation(
                                out=QTa[0:96, qa, hh * 8:(hh + 1) * 8, bi * 128:(bi + 1) * 128],
                                in_=pst[0:96, :].rearrange("p (h t) -> p h t", h=8), func=AF.Copy), r=[PB(pb)], w=[("QTa", qa)])
                        if b == 1:
                            qstore(0, 2, 0)
                        elif b >= 2 and bi == 3:
                            qstore(grp, 4, 256 + (grp - 1) * 512)

                    xload(0)
                    recF, recB = {}, {}
                    for b in range(NB):
                        S.rec()
                        front(b)
                        recF[b] = S.end_rec()
                        if (b >= 2) or with_ctx:
                            S.rec()
                            back(b)
                            recB[b] = S.end_rec()
                    S.play([recF[0]])
                    for b in range(NB):
                        S.play([recB.get(b), recF.get(b + 1)])
                    S.barrier()
                with ExitStack() as p2:
                    Oall = sb("Oall", [128, NB, D], BF16, p2)
                    with ExitStack() as p2a:
                        Qb = sb("Qb", [128, 2, U], BF16, p2a)
                        Vb = sb("Vb", [128, 2, NB, 65], BF16, p2a)
                        PTr = sb("PTr", [128, 6, 512], BF16, p2a)
                        rec = sb("rec", [128, 2, 4], F32, p2a)
                        S.pool(lambda e: e.memset(Vb[:, :, :, 64:65], 1.0), w=["Vbones"])
                        ptc = [0]

                        def build_kv(h):
                            hb = h % 2
                            S.dma("sp", "qld%d" % hb, lambda e, h=h, hb=hb: e.dma_start(out=Qb[0:96, hb, :], in_=qT_d[h, :, :]),
                                  r=[("qT", g_) for g_ in range(9)], w=[("Qb", hb)])
                            for kt in range(9):
                                u0 = kt * 512
                                n = min(512, U - u0)
                                pb = 1
                                for k in range(2):
                                    S.pe(lambda e, k=k, u0=u0, n=n, pb=pb, h=h: e.matmul(
                                        banks[pb][0:64, 0:n], lhsT=wuk[:, k, h * 64:(h + 1) * 64], rhs=ckvT[:, k, u0:u0 + n],
                                        start=(k == 0), stop=(k == 1)), r=["wuk"] + [("ckvT", bb) for bb in range(u0 // 128, (u0 + n) // 128)],
                                        w=[PB(pb)])
                                S.dve(lambda e, u0=u0, n=n, pb=pb, hb=hb: e.tensor_copy(out=Kb[0:64, hb, u0:u0 + n], in_=banks[pb][0:64, 0:n]),
                                      r=[PB(pb)], w=[("Kn", hb)])
                            for vg in range(5):
                                kb0 = vg * 7
                                nkb = min(7, NB - kb0)
                                pb = 3
                                for kk in range(nkb):
                                    kb = kb0 + kk
                                    for k in range(2):
                                        S.pe(lambda e, k=k, kb=kb, kk=kk, pb=pb, h=h: e.matmul(
                                            banks[pb][:, kk * 64:(kk + 1) * 64], lhsT=ckvT[:, k, kb * 128:(kb + 1) * 128],
                                            rhs=wuv[:, k, h * 64:(h + 1) * 64], start=(k == 0), stop=(k == 1)),
                                            r=["wuv", ("ckvT", kb)], w=[PB(pb)])
                                S.dve(lambda e, kb0=kb0, nkb=nkb, pb=pb, hb=hb: e.tensor_copy(
                                    out=Vb[:, hb, kb0:kb0 + nkb, 0:64], in_=banks[pb][:, 0:nkb * 64].rearrange("p (k d) -> p k d", d=64)),
                                    r=[PB(pb)], w=[("Vb", hb)])

                        qtiles = ([(0, 2, [0, 1])] if with_ctx else []) + [(2 + 4 * t, 4, list(range(NB))) for t in range(8)]

                        def attention(h):
                            hb = h % 2
                            items = []
                            for qi, (qb0, nqb, kbs) in enumerate(qtiles):
                                for ki, kb in enumerate(kbs):
                                    items.append((qi, qb0, nqb, ki, kb, len(kbs)))
                            slots = {}

                            def stA(i):
                                qi, qb0, nqb, ki, kb, nk = items[i]
                                nq = nqb * 128
                                sbk = (7, 0, 2, 4)[ptc[0] % 4]
                                pslot = ptc[0] % 6
                                ptc[0] += 1
                                slots[i] = pslot
                                S.pe(lambda e: e.matmul(
                                    banks[sbk][:, 0:nq], lhsT=Kb[0:96, hb, kb * 128:(kb + 1) * 128],
                                    rhs=Qb[0:96, hb, qb0 * 128:qb0 * 128 + nq], start=True, stop=True),
                                    r=[("Kn", hb), ("KR", kb), ("Qb", hb)], w=[PB(sbk)])
                                S.act(lambda e: e.activation(out=PTr[:, pslot, 0:nq], in_=banks[sbk][:, 0:nq], func=AF.Exp, scale=scale),
                                      r=[PB(sbk)], w=[("PTr", pslot)])

                            def stB(i):
                                qi, qb0, nqb, ki, kb, nk = items[i]
                                ob = 5 + (h * 9 + qi) % 2
                                pslot = slots[i]
                                if ki == 0:
                                    S.pe(lambda e: e.matmul(banks[ob][:, 0:nqb * 65], lhsT=zerosB[:, 0:128], rhs=zerosB[:, 0:nqb * 65],
                                                            start=True, stop=True), r=["zerosB"], w=[PB(ob)])
                                for qq in range(nqb):
                                    S.pe(lambda e, qq=qq: e.matmul(
                                        banks[ob][:, qq * 65:(qq + 1) * 65], lhsT=PTr[:, pslot, qq * 128:(qq + 1) * 128],
                                        rhs=Vb[:, hb, kb, :], start=False, stop=True, skip_group_check=True),
                                        r=[("PTr", pslot), ("Vb", hb), "Vbones"], w=[PB(ob)])
                                if ki == nk - 1:
                                    rs_ = (h * 9 + qi) % 2
                                    ov = banks[ob][:, 0:nqb * 65].rearrange("p (q d) -> p q d", d=65)
                                    S.dve(lambda e: e.reciprocal(out=rec[:, rs_, 0:nqb], in_=ov[:, :, 64]), r=[PB(ob)], w=[("rec", rs_)])
                                    S.dve(lambda e: e.tensor_tensor(
                                        out=Oall[:, qb0:qb0 + nqb, h * 64:(h + 1) * 64], in0=ov[:, :, 0:64],
                                        in1=rec[:, rs_, 0:nqb].unsqueeze(2).broadcast_to([128, nqb, 64]), op=ALU.mult),
                                        r=[PB(ob), ("rec", rs_)], w=["Oall"])

                            n_it = len(items)
                            for i in range(min(3, n_it)):
                                stA(i)
                            for i in range(n_it):
                                if i + 3 < n_it:
                                    stA(i + 3)
                                stB(i)

                        build_kv(0)
                        for h in range(16):
                            S.rec()
                            attention(h)
                            ra = S.end_rec()
                            rb = None
                            if h + 1 < 16:
                                S.rec()
                                build_kv(h + 1)
                                rb = S.end_rec()
                            S.play([ra, rb])
                        S.barrier()
                    with ExitStack() as p3:
                        xbr = sb("xbr", [128, 2, D], F32, p3)
                        OTt = sb("OTt", [128, 2, D], BF16, p3)
                        xnew = sb("xnew", [128, 2, D], F32, p3)
                        tl = {"OT": [OTt[:, 0, :], OTt[:, 1, :]], "xnew": [xnew[:, 0, :], xnew[:, 1, :]], "pbT": 4, "pbo": (1, 2)}
                        def xload3(b):
                            S.dma("sp", "xld%d" % (b % 2), lambda e, b=b: e.dma_start(out=xbr[:, b % 2, :], in_=Xsrc(l, b)),
                                  r=[("X", b)], w=[("xb", b % 2)])

                        xload3(b0)
                        proj_T(Oall[:, b0, :], [], tl, b0 % 2)
                        for b in range(b0, NB):
                            j = 1 if b < 2 else 0
                            xs_ = b % 2
                            if b + 1 < NB:
                                xload3(b + 1)
                                proj_T(Oall[:, b + 1, :], [], tl, (b + 1) % 2)
                            proj_MM(wo_sb, xbr[:, xs_, :], [("xb", xs_)], 0 + j, b, tl, b % 2)
                        S.barrier()

        def phase_F(l):
            with_ctx = l < DEPTH - 1
            last = l == DEPTH - 1
            S.raw("act", lambda e: e.preload_act_table(AF.Silu))
            with ExitStack() as pf:
                wout = sb("wout", [128, NCF, D], BF16, pf)
                wr = sb("wr", [128, 4, 8, 256], BF16, pf)
                h2T = sb("h2T", [128, 2, 8, 512], BF16, pf)
                halo = sb("halo", [128, 3, 8, 2], BF16, pf)
                gT = sb("gT", [128, NCF, 512], BF16, pf)
                xbr4 = sb("xbr4", [128, 4, D], F32, pf)
                xn4 = sb("xn4", [128, 4, D], F32, pf)
                xb3 = sb("xb3", [128, 2, D], F32, pf)
                acc = sb("acc", [128, 2, 512], F32, pf)
                sl = sb("sl", [128, 2, 512], F32, pf)
                xnew = sb("xnew", [128, 2, D], F32, pf)
                yo = sb("yo", [128, 2, D], F32, pf)
                wov = f_wout[l].rearrange("(c p) n -> p c n", p=128)
                for i in range(2):
                    S.dma("pool", "wF%d" % i, lambda e, i=i: e.dma_start(out=wout[:, i * 11:(i + 1) * 11, :], in_=wov[:, i * 11:(i + 1) * 11, :]),
                          w=["wout"])
                wiv = f_win[l].rearrange("(k p) n -> p k n", p=128)
                tiles = ([(0, 2)] if with_ctx else []) + [(2 + 4 * t, 4) for t in range(8)]
                wc = [0]
                xc = [0]

                def s1a(ti):
                    b0_, nb_ = tiles[ti]
                    lat_first = (b0_ == 2)
                    lat_last = (b0_ + nb_ == NB)
                    if b0_ < 2 or lat_first:
                        S.pool(lambda e: e.memset(halo[:, ti % 3, :, 0:1], 0.0), w=[("halo", ti % 3, 0)])
                    if b0_ < 2 or lat_last:
                        S.pool(lambda e: e.memset(halo[:, ti % 3, :, 1:2], 0.0), w=[("halo", ti % 3, 1)])
                    st = []
                    for bi in range(nb_):
                        b = b0_ + bi
                        S.dma("sp", "xld%d" % bi, lambda e, b=b, bi=bi: e.dma_start(out=xbr4[:, bi, :], in_=Xs[b * 128:(b + 1) * 128, :]),
                              r=[("X", b)], w=[("xb", bi)])
                    for bi in range(nb_):
                        st.append(row_rstd(xbr4[:, bi, :], [("xb", bi)], D))
                    for bi in range(nb_):
                        rs, rn = st[bi]
                        S.act(lambda e, bi=bi, rs=rs: e.activation(out=xn4[:, bi, :], in_=xbr4[:, bi, :], func=AF.Copy, scale=rs),
                              r=[("xb", bi), rn], w=[("xn4", bi)])

                def s1b(ti, bis):
                    b0_, nb_ = tiles[ti]
                    hs = ti % 2
                    lat_first = (b0_ == 2)
                    lat_last = (b0_ + nb_ == NB)
                    for bi in bis:
                        b = b0_ + bi
                        j = 1 if b < 2 else 0
                        G, SH = Gf(1, j), SHf(1, j)
                        for hlf in range(2):
                            pbk = (0, 6)[hlf]
                            for kk in range(4):
                                k = hlf * 4 + kk
                                S.pe(lambda e, k=k, kk=kk, pbk=pbk, bi=bi: e.transpose(out=banks[pbk][:, kk * 128:(kk + 1) * 128],
                                                                                   in_=xn4[:, bi, k * 128:(k + 1) * 128], identity=identF[:, :]),
                                     r=[("xn4", bi), "identF"], w=[PB(pbk)])
                        for hlf in range(2):
                            pbk = (0, 6)[hlf]
                            for kk in range(4):
                                k = hlf * 4 + kk
                                dst = h2T[:, hs, k, bi * 128:(bi + 1) * 128]
                                if hlf == 1:
                                    S.act(lambda e, k=k, kk=kk, pbk=pbk, dst=dst, G=G, SH=SH: e.activation(
                                        out=dst, in_=banks[pbk][:, kk * 128:(kk + 1) * 128], func=AF.Identity, scale=G(k), bias=SH(k)),
                                        r=[PB(pbk), "mods"], w=[("h2T", hs, bi)])
                                else:
                                    S.dve(lambda e, k=k, kk=kk, pbk=pbk, dst=dst, G=G, SH=SH: e.tensor_scalar(
                                        out=dst, in0=banks[pbk][:, kk * 128:(kk + 1) * 128], scalar1=G(k), scalar2=SH(k),
                                        op0=ALU.mult, op1=ALU.add), r=[PB(pbk), "mods"], w=[("h2T", hs, bi)])
                        if b >= 2 and bi == 0 and not lat_first:
                            S.pool(lambda e: e.tensor_copy(out=halo[:, (ti - 1) % 3, :, 1:2], in_=h2T[:, hs, :, 0:1]),
                                   r=[("h2T", hs, 0)], w=[("halo", (ti - 1) % 3, 1)])
                        if b >= 2 and bi == nb_ - 1 and not lat_last:
                            S.pool(lambda e: e.tensor_copy(out=halo[:, (ti + 1) % 3, :, 0:1],
                                                           in_=h2T[:, hs, :, nb_ * 128 - 1:nb_ * 128]),
                                   r=[("h2T", hs, nb_ - 1)], w=[("halo", (ti + 1) % 3, 0)])

                pref = {}

                def wload(ti, c):
                    if (ti, c) in pref:
                        return pref[(ti, c)]
                    ws = wc[0] % 4
                    wc[0] += 1
                    pref[(ti, c)] = ws
                    S.dma("pool", "wi%d" % ws, lambda e: e.dma_start(out=wr[:, ws, :, 0:128], in_=wiv[:, :, c * 128:(c + 1) * 128]),
                          w=[("wr", ws, 0)])
                    S.dma("pool", "wj%d" % ws, lambda e: e.dma_start(out=wr[:, ws, :, 128:256],
                                                                     in_=wiv[:, :, DFF + c * 128:DFF + (c + 1) * 128]),
                          w=[("wr", ws, 1)])
                    return ws

                def s2(ti):
                    b0_, nb_ = tiles[ti]
                    n = nb_ * 128
                    hs = ti % 2
                    hres = [("h2T", hs, bi) for bi in range(nb_)]
                    for c in range(NCF):
                        ws = wload(ti, c)
                        pa = 1 + 2 * (c % 2)
                        pv = 2 + 2 * (c % 2)
                        hcol = (c % 2) * 2
                        for k in range(8):
                            S.pe(lambda e, k=k, ws=ws, pa=pa: e.matmul(banks[pa][:, 0:n], lhsT=wr[:, ws, k, 0:128], rhs=h2T[:, hs, k, 0:n],
                                                                     start=(k == 0), stop=(k == 7)), r=[("wr", ws, 0)] + hres, w=[PB(pa)])
                        for k in range(8):
                            S.pe(lambda e, k=k, ws=ws, hcol=hcol: e.matmul(banks[5][:, hcol:hcol + 2], lhsT=wr[:, ws, k, 0:128],
                                                                         rhs=halo[:, ti % 3, k, :], start=(k == 0), stop=(k == 7)),
                                 r=[("wr", ws, 0), ("halo", ti % 3, 0), ("halo", ti % 3, 1)], w=[PB(5)])
                        for k in range(8):
                            S.pe(lambda e, k=k, ws=ws, pv=pv: e.matmul(banks[pv][:, 0:n], lhsT=wr[:, ws, k, 128:256], rhs=h2T[:, hs, k, 0:n],
                                                                     start=(k == 0), stop=(k == 7)), r=[("wr", ws, 1)] + hres, w=[PB(pv)])
                        a_ = c % 2
                        w0 = cw[:, (l * 3 + 0) * NCF + c:(l * 3 + 0) * NCF + c + 1]
                        w1 = cw[:, (l * 3 + 1) * NCF + c:(l * 3 + 1) * NCF + c + 1]
                        w2 = cw[:, (l * 3 + 2) * NCF + c:(l * 3 + 2) * NCF + c + 1]
                        bb = cb[:, l * NCF + c:l * NCF + c + 1]
                        ar = ("acc", a_)
                        S.act(lambda e, pa=pa, a_=a_, w1=w1, bb=bb: e.activation(out=acc[:, a_, 0:n], in_=banks[pa][:, 0:n], func=AF.Identity,
                                                                                scale=w1, bias=bb), r=[PB(pa), "params"], w=[ar])
                        S.dve(lambda e, pa=pa, a_=a_, w0=w0: e.scalar_tensor_tensor(out=acc[:, a_, 1:n], in0=banks[pa][:, 0:n - 1], scalar=w0,
                                                                                   in1=acc[:, a_, 1:n], op0=ALU.mult, op1=ALU.add),
                              r=[PB(pa), ar, "params"], w=[ar])
                        S.dve(lambda e, pa=pa, a_=a_, w2=w2: e.scalar_tensor_tensor(out=acc[:, a_, 0:n - 1], in0=banks[pa][:, 1:n], scalar=w2,
                                                                                   in1=acc[:, a_, 0:n - 1], op0=ALU.mult, op1=ALU.add),
                              r=[PB(pa), ar], w=[ar])
                        S.dve(lambda e, a_=a_, w0=w0, hcol=hcol: e.scalar_tensor_tensor(out=acc[:, a_, 0:1], in0=banks[5][:, hcol:hcol + 1], scalar=w0,
                                                                                       in1=acc[:, a_, 0:1], op0=ALU.mult, op1=ALU.add),
                              r=[PB(5), ar], w=[ar])
                        S.dve(lambda e, a_=a_, w2=w2, hcol=hcol: e.scalar_tensor_tensor(out=acc[:, a_, n - 1:n], in0=banks[5][:, hcol + 1:hcol + 2],
                                                                                       scalar=w2, in1=acc[:, a_, n - 1:n], op0=ALU.mult, op1=ALU.add),
                              r=[PB(5), ar], w=[ar])
                        S.act(lambda e, a_=a_: e.activation(out=sl[:, a_, 0:n], in_=acc[:, a_, 0:n], func=AF.Silu), r=[ar], w=[("sl", a_)])
                        S.dve(lambda e, a_=a_, pv=pv, c=c: e.tensor_tensor(out=gT[:, c, 0:n], in0=banks[pv][:, 0:n], in1=sl[:, a_, 0:n], op=ALU.mult),
                              r=[PB(pv), ("sl", a_)], w=[("gT", c)])

                def s3(ti):
                    b0_, nb_ = tiles[ti]
                    for bi in range(nb_):
                        b = b0_ + bi
                        j = 1 if b < 2 else 0
                        xs_ = b % 2
                        xo = xc[0] % 2
                        xc[0] += 1
                        S.dma("sp", "xl3%d" % xs_, lambda e, b=b, xs_=xs_: e.dma_start(out=xb3[:, xs_, :], in_=Xs[b * 128:(b + 1) * 128, :]),
                              r=[("X", b)], w=[("xb3", xs_)])
                        for hlf in range(2):
                            pb = 6 + hlf
                            for c in range(NCF):
                                S.pe(lambda e, c=c, hlf=hlf, pb=pb, bi=bi: e.matmul(banks[pb][:, :], lhsT=gT[:, c, bi * 128:(bi + 1) * 128],
                                                                                   rhs=wout[:, c, hlf * 512:(hlf + 1) * 512],
                                                                                   start=(c == 0), stop=(c == NCF - 1)),
                                     r=[("gT", c), "wout"], w=[PB(pb)])
                            S.dve(lambda e, hlf=hlf, pb=pb, xo=xo, j=j: e.tensor_tensor(out=xnew[:, xo, hlf * 512:(hlf + 1) * 512], in0=banks[pb][:, :],
                                                                                      in1=gateB[:, 2 + j, hlf * 512:(hlf + 1) * 512], op=ALU.mult),
                                  r=[PB(pb), "gates"], w=[("xnew", xo, hlf)])
                            S.pool(lambda e, hlf=hlf, xo=xo, xs_=xs_: e.tensor_tensor(out=xnew[:, xo, hlf * 512:(hlf + 1) * 512],
                                                                                    in0=xnew[:, xo, hlf * 512:(hlf + 1) * 512],
                                                                                    in1=xb3[:, xs_, hlf * 512:(hlf + 1) * 512], op=ALU.add),
                                   r=[("xnew", xo, hlf), ("xb3", xs_)], w=[("xnew", xo, hlf)])
                        xres = [("xnew", xo, 0), ("xnew", xo, 1)]
                        if not last:
                            S.dma("sp", "xst", lambda e, b=b, xo=xo: e.dma_start(out=Xs[b * 128:(b + 1) * 128, :], in_=xnew[:, xo, :]),
                                  r=xres, w=[("X", b)])
                        else:
                            i = new_stat()
                            rn = ("st", i)
                            S.dve(lambda e, xo=xo, i=i: e.scalar_tensor_tensor(out=yo[:, xo, :], in0=xnew[:, xo, :], scalar=1.0, in1=xnew[:, xo, :],
                                                                              op0=ALU.mult, op1=ALU.mult, accum_out=ssr[:, i:i + 1]),
                                  r=xres, w=[rn, ("yo", xo)])
                            S.dve(lambda e, i=i: e.tensor_scalar(out=msr[:, i:i + 1], in0=ssr[:, i:i + 1], scalar1=1.0 / D, scalar2=EPS,
                                                                op0=ALU.mult, op1=ALU.add), r=[rn], w=[rn])
                            S.pool(lambda e, i=i: e.tensor_tensor(out=rstdr[:, i:i + 1], in0=msr[:, i:i + 1], in1=neghalf[:, :], op=ALU.pow),
                                   r=[rn, "neghalf"], w=[rn])
                            S.dve(lambda e, xo=xo, i=i: e.scalar_tensor_tensor(out=yo[:, xo, :], in0=xnew[:, xo, :], scalar=rstdr[:, i:i + 1],
                                                                              in1=finalgB[:, :], op0=ALU.mult, op1=ALU.mult),
                                  r=xres + [rn, "finalgB", ("yo", xo)], w=[("yo", xo)])
                            S.dma("sp", "ost", lambda e, b=b, xo=xo: e.dma_start(out=out[(b - 2) * 128:(b - 1) * 128, :], in_=yo[:, xo, :]),
                                  r=[("yo", xo)], w=[("out", b)])

                nt = len(tiles)
                s1a(0)
                s1b(0, range(tiles[0][1]))
                if nt > 1:
                    s1a(1)
                    s1b(1, [0])
                for ti in range(nt):
                    s2(ti)
                    if ti + 1 < nt:
                        s1b(ti + 1, range(1, tiles[ti + 1][1]))
                    if ti + 2 < nt:
                        s1a(ti + 2)
                    if ti + 1 < nt:
                        for c in range(4):
                            wload(ti + 1, c)
                    s3(ti)
                    if ti + 2 < nt:
                        s1b(ti + 2, [0])
                S.barrier()

        phases = []
        for l in range(DEPTH):
            phases.append(("M", l))
            phases.append(("A" if l % 2 == 0 else "B", l))
            phases.append(("F", l))
        nrun = 0
        for kind, l in phases:
            if kind == "M":
                if n_phases is not None and nrun >= n_phases:
                    break
                modulation(l)
                continue
            if n_phases is not None and nrun >= n_phases:
                break
            if kind == "A":
                phase_A(l)
            elif kind == "B":
                phase_B(l)
            else:
                phase_F(l)
            nrun += 1
        S.finish()
    return nc


def _consts():
    def ang(rot_dim):
        row = np.repeat(np.arange(T // GRID_W), GRID_W).astype(np.float32)
        col = np.tile(np.arange(GRID_W), T // GRID_W).astype(np.float32)
        nf = rot_dim // 4
        inv = (np.float32(10000.0) ** (-np.arange(nf, dtype=np.float32) / np.float32(nf))).astype(np.float32)
        return np.concatenate([row[:, None] * inv, col[:, None] * inv], axis=-1).astype(np.float32)
    aA, aB = ang(64), ang(32)
    k = np.arange(128)[:, None]
    q = np.arange(128)[None, :]
    return {
        "ident": np.eye(128, dtype=np.float32),
        "maskp": (k >= q).astype(np.float32),
        "maskn": (k <= q).astype(np.float32),
        "cosA": np.cos(aA).astype(np.float32), "sinA": np.sin(aA).astype(np.float32),
        "cosB": np.cos(aB).astype(np.float32), "sinB": np.sin(aB).astype(np.float32),
    }


_WNAMES = ["mod_w", "mod_b", "norm1_g", "norm2_g", "a_wqkv", "a_wo", "a_sink", "b_wdown", "b_qnorm_g", "b_wuq",
           "b_kvnorm_g", "b_wuk", "b_wuv", "b_wo", "f_win", "f_conv_w", "f_conv_b", "f_wout"]


def make_in_maps(inputs, cores):
    consts = _consts()
    shared = {n: np.ascontiguousarray(np.asarray(inputs[n], dtype=np.float32)) for n in _WNAMES}
    shared["final_g"] = np.ascontiguousarray(np.asarray(inputs["final_g"], dtype=np.float32).reshape(1, D))
    shared.update(consts)
    x = np.asarray(inputs["x"], dtype=np.float32)
    c = np.asarray(inputs["c"], dtype=np.float32)
    ctx = np.asarray(inputs["ctx"], dtype=np.float32)
    c_ctx = np.asarray(inputs["c_ctx"], dtype=np.float32)
    maps = []
    for b in cores:
        m = dict(shared)
        m["x"] = np.ascontiguousarray(x[b])
        m["ctx"] = np.ascontiguousarray(ctx[b])
        m["cc"] = np.ascontiguousarray(np.concatenate([c[b].reshape(8, 128), c_ctx.reshape(8, 128)], axis=0))
        maps.append(m)
    return maps


def kernel(**inputs):
    nc = build()
    maps = make_in_maps(inputs, list(range(8)))
    res = run_bass_kernel_spmd(nc, maps, core_ids=list(range(8)))
    return np.stack([np.asarray(r["out"], dtype=np.float32) for r in res.results], axis=0)
```
